# Optimizing a Trainium2 kernel written in Bass

```python
import math
import jax, jax.numpy as jnp
from jax import lax
import numpy as np

D_MODEL = 4096
BATCH = 1
SEQ = 8192
DEPTH = 4

N_A_LAYERS = DEPTH // 2
N_B_LAYERS = DEPTH - N_A_LAYERS
POOL_WIDTH = D_MODEL
POOL_WINDOWS = (2, 4, 8, 16)
N_POOL_GROUPS = len(POOL_WINDOWS)
POOL_GROUP_DIM = POOL_WIDTH // N_POOL_GROUPS
SB_HEAD_DIM = 128
SB_N_HEADS = D_MODEL // SB_HEAD_DIM
SB_WIDTH = SB_N_HEADS * SB_HEAD_DIM
Q_BLOCK = 128
RMS_EPS = 1e-6

kernel_name = "yoco_pool_stickbreaking_hybrid"


def rms_norm(x, g):
    xf = x.astype(jnp.float32)
    y = xf * lax.rsqrt(jnp.mean(xf * xf, axis=-1, keepdims=True) + RMS_EPS)
    return (y * g.astype(jnp.float32)).astype(x.dtype)


def causal_multiscale_pool(u):
    b, s, _, c = u.shape
    uf = u.astype(jnp.float32)
    csum = jnp.cumsum(uf, axis=1)
    pos = jnp.arange(s)
    outs = []
    for g, w in enumerate(POOL_WINDOWS):
        cg = csum[:, :, g]
        shifted = jnp.pad(cg, ((0, 0), (w, 0), (0, 0)))[:, :s]
        count = jnp.minimum(pos + 1, w).astype(jnp.float32)[None, :, None]
        outs.append((cg - shifted) / count - uf[:, :, g])
    return jnp.stack(outs, axis=2).astype(u.dtype)


def pool_layer(x, pre_g, w_in, w_group, scale, w_out, post_g):
    b, s, _ = x.shape
    h = rms_norm(x, pre_g)
    ug = h @ w_in
    u, gate = jnp.split(ug, 2, axis=-1)
    u = u.reshape(b, s, N_POOL_GROUPS, POOL_GROUP_DIM)
    pooled = causal_multiscale_pool(u)
    mixed = jnp.einsum('bsgc,gcd->bsgd', pooled, w_group).reshape(b, s, POOL_WIDTH)
    y = (mixed * scale * jax.nn.silu(gate)) @ w_out
    return x + rms_norm(y, post_g)


def stick_breaking_attention(q, k, v):
    b, h, s, d = q.shape
    nb = s // Q_BLOCK
    q_blocks = q.reshape(b, h, nb, Q_BLOCK, d).transpose(2, 0, 1, 3, 4)
    kf = k.astype(jnp.float32)
    vf = v.astype(jnp.float32)
    key_pos = jnp.arange(s)
    inv_sqrt_d = 1.0 / math.sqrt(d)

    def one_block(args):
        qb, i = args
        q_pos = i * Q_BLOCK + jnp.arange(Q_BLOCK)
        z = jnp.einsum('bhqd,bhkd->bhqk', qb.astype(jnp.float32), kf) * inv_sqrt_d
        mask = key_pos[None, :] < q_pos[:, None]
        log_1mb = jnp.where(mask, jax.nn.log_sigmoid(-z), 0.0)
        after = lax.cumsum(log_1mb, axis=3, reverse=True) - log_1mb
        a = jnp.where(mask, jnp.exp(jax.nn.log_sigmoid(z) + after), 0.0)
        return jnp.einsum('bhqk,bhkd->bhqd', a, vf)

    out = lax.map(one_block, (q_blocks, jnp.arange(nb)))
    return out.transpose(1, 2, 0, 3, 4).reshape(b, h, s, d).astype(q.dtype)


def sb_layer(x, k, v, pre_g, w_in, w_out, post_g):
    b, s, _ = x.shape
    h = rms_norm(x, pre_g)
    qg = h @ w_in
    q, gate = jnp.split(qg, 2, axis=-1)
    q = q.reshape(b, s, SB_N_HEADS, SB_HEAD_DIM).transpose(0, 2, 1, 3)
    o = stick_breaking_attention(q, k, v)
    o = o.transpose(0, 2, 1, 3).reshape(b, s, SB_WIDTH)
    y = (o * jax.nn.silu(gate)) @ w_out
    return x + rms_norm(y, post_g)


def setup_inputs(seed: int = 0) -> dict:
    key = jax.random.key(seed)
    ks = jax.random.split(key, 16)
    f32 = jnp.float32

    def dense(k, shape, fan_in):
        return jax.random.normal(k, shape, f32) * (fan_in ** -0.5)

    def gain(k, shape):
        return 1.0 + 0.05 * jax.random.normal(k, shape, f32)

    return {
        "x": jax.random.normal(ks[0], (BATCH, SEQ, D_MODEL), f32),
        "a_pre_norm": gain(ks[1], (N_A_LAYERS, D_MODEL)),
        "a_w_in": dense(ks[2], (N_A_LAYERS, D_MODEL, 2 * POOL_WIDTH), D_MODEL),
        "a_w_group": dense(ks[3], (N_A_LAYERS, N_POOL_GROUPS, POOL_GROUP_DIM, POOL_GROUP_DIM), POOL_GROUP_DIM),
        "a_scale": gain(ks[4], (N_A_LAYERS, POOL_WIDTH)),
        "a_w_out": dense(ks[5], (N_A_LAYERS, POOL_WIDTH, D_MODEL), POOL_WIDTH),
        "a_post_norm": gain(ks[6], (N_A_LAYERS, D_MODEL)),
        "kv_norm": gain(ks[7], (D_MODEL,)),
        "w_kv": dense(ks[8], (D_MODEL, 2 * SB_WIDTH), D_MODEL),
        "b_pre_norm": gain(ks[9], (N_B_LAYERS, D_MODEL)),
        "b_w_in": dense(ks[10], (N_B_LAYERS, D_MODEL, 2 * SB_WIDTH), D_MODEL),
        "b_w_out": dense(ks[11], (N_B_LAYERS, SB_WIDTH, D_MODEL), SB_WIDTH),
        "b_post_norm": gain(ks[12], (N_B_LAYERS, D_MODEL)),
    }


def reference(x, a_pre_norm, a_w_in, a_w_group, a_scale, a_w_out, a_post_norm,
              kv_norm, w_kv, b_pre_norm, b_w_in, b_w_out, b_post_norm):
    b, s, _ = x.shape
    k = None
    v = None
    for layer in range(DEPTH):
        if layer < N_A_LAYERS:
            x = pool_layer(x, a_pre_norm[layer], a_w_in[layer], a_w_group[layer],
                           a_scale[layer], a_w_out[layer], a_post_norm[layer])
            if layer == N_A_LAYERS - 1:
                kv = rms_norm(x, kv_norm) @ w_kv
                k, v = jnp.split(kv, 2, axis=-1)
                k = k.reshape(b, s, SB_N_HEADS, SB_HEAD_DIM).transpose(0, 2, 1, 3)
                v = v.reshape(b, s, SB_N_HEADS, SB_HEAD_DIM).transpose(0, 2, 1, 3)
        else:
            j = layer - N_A_LAYERS
            x = sb_layer(x, k, v, b_pre_norm[j], b_w_in[j], b_w_out[j], b_post_norm[j])
    return x
```

```python
import contextlib
import numpy as np
import ml_dtypes
import concourse.bass as bass
import concourse.mybir as mybir
from concourse.bass_utils import run_bass_kernel_spmd

F32 = mybir.dt.float32
BF16 = mybir.dt.bfloat16
AF = mybir.ActivationFunctionType
ALU = mybir.AluOpType

D = 4096
KC = D // 128
SEQ = 8192
NCORE = 8
NH = 32
EPS = 1e-6
HALO = 32
SW = 128 + HALO
NSTR = 8
PA_COLS = 4 * SW
PA_CT = PA_COLS // 2
PB_COLS = 512
WINS = (2, 4, 8, 16)
ENGS = ("pe", "act", "dve", "pool", "sp")


class Slot:
    __slots__ = ("name", "w", "r", "sem", "ndma")

    def __init__(self, name):
        self.name = name
        self.w = None
        self.r = []
        self.sem = None
        self.ndma = 0


class Sched:
    def __init__(self):
        self.ops = {e: [] for e in ENGS}
        self.seen_c = {e: {} for e in ENGS}
        self.seen_d = {e: {} for e in ENGS}
        self.dma_slots = []

    def _need(self, eng, tok, is_dma):
        if tok is None:
            return False
        if tok[0] == "c":
            _, se, idx = tok
            if se == eng and not is_dma and not self.sync_same:
                return False
            if self.seen_c[eng].get(se, -1) >= idx:
                return False
            self.seen_c[eng][se] = idx
            return True
        _, sl, cnt = tok
        if self.seen_d[eng].get(sl, 0) >= cnt:
            return False
        self.seen_d[eng][sl] = cnt
        return True

    def op(self, eng, fn, reads=(), writes=(), dma=None, sync_same=False):
        is_dma = dma is not None
        self.sync_same = sync_same
        waits = []
        for s in reads:
            if self._need(eng, s.w, is_dma):
                waits.append(s.w)
        for s in writes:
            if self._need(eng, s.w, is_dma):
                waits.append(s.w)
            for t in s.r:
                if self._need(eng, t, is_dma):
                    waits.append(t)
        idx = len(self.ops[eng])
        if is_dma:
            if dma.sem is None:
                dma.sem = True
                self.dma_slots.append(dma)
            dma.ndma += 1
            tok = ("d", dma, dma.ndma)
        else:
            tok = ("c", eng, idx)
        self.ops[eng].append({"fn": fn, "waits": waits, "dma": dma, "inc": False})
        for s in reads:
            s.r.append(tok)
        for s in writes:
            s.w = tok
            s.r = []
        return tok

    def emit(self, nc, final_slots):
        for e in ENGS:
            for o in self.ops[e]:
                for t in o["waits"]:
                    if t[0] == "c":
                        self.ops[t[1]][t[2]]["inc"] = True
        cum = {}
        for e in ENGS:
            c = 0
            arr = []
            for o in self.ops[e]:
                if o["inc"]:
                    c += 1
                arr.append(c)
            cum[e] = arr
        with contextlib.ExitStack() as st:
            esem = {e: st.enter_context(nc.semaphore("e_" + e)) for e in ENGS}
            for i, sl in enumerate(self.dma_slots):
                sl.sem = st.enter_context(nc.semaphore("d%d" % i))
            block = st.enter_context(nc.Block())

            def run(e, h):
                for o in self.ops[e]:
                    for t in o["waits"]:
                        if t[0] == "c":
                            h.wait_ge(esem[t[1]], cum[t[1]][t[2]])
                        else:
                            h.wait_ge(t[1].sem, 16 * t[2])
                    ins = o["fn"](h)
                    if o["dma"] is not None:
                        ins.then_inc(o["dma"].sem, 16)
                    elif o["inc"]:
                        ins.then_inc(esem[e], 1)
                if e == "sp":
                    for sl in final_slots:
                        if sl.ndma:
                            h.wait_ge(sl.sem, 16 * sl.ndma)

            @block.tensor
            def _(h):
                run("pe", h)

            @block.scalar
            def _(h):
                run("act", h)

            @block.vector
            def _(h):
                run("dve", h)

            @block.gpsimd
            def _(h):
                run("pool", h)

            @block.sync
            def _(h):
                run("sp", h)


class Arena:
    def __init__(self, nc, nbytes):
        self.t = nc.alloc_sbuf_tensor("arena", [128, nbytes // 4], F32)
        self.off = 0
        self.cap = nbytes

    def mark(self):
        return self.off

    def reset(self, m):
        self.off = m

    def alloc(self, n, dt, name=None):
        nb = n * (2 if dt == BF16 else 4)
        nb = (nb + 63) // 64 * 64
        assert self.off + nb <= self.cap, ("SBUF overflow", name, self.off, nb)
        a = self.t[:, self.off // 4:(self.off + nb) // 4]
        self.off += nb
        if dt == BF16:
            a = a.bitcast(BF16)
        return a[:, 0:n]


class Ring:
    def __init__(self, bufs, name):
        self.bufs = bufs
        self.slots = [Slot("%s%d" % (name, i)) for i in range(len(bufs))]
        self.i = 0

    def next(self):
        k = self.i % len(self.bufs)
        self.i += 1
        return self.bufs[k], self.slots[k]


def load_consts(S, ar, dram, names):
    out = {}
    sl = Slot("consts")
    for key, n, dt in names:
        buf = ar.alloc(n, dt, key)
        out[key] = buf
        S.op("sp", lambda h, b=buf, k=key: h.dma_start(out=b, in_=dram[k]), writes=[sl], dma=sl)
    return out, sl


class Ctx:
    pass


def rms_stats(S, C, xsrc_fn, ncols, cts, want_h_gcol, hT, hslots, src_slot):
    for kc in range(KC):
        xb, xs = C.xring.next()
        S.op("sp", lambda h, b=xb, k=kc: h.dma_start(out=b[:, 0:ncols], in_=xsrc_fn(k)),
             reads=[src_slot], writes=[xs], dma=xs)
        qb, qs = C.sqring.next()
        S.op("act", lambda h, b=xb, q=qb: h.activation(out=q[:, 0:ncols], in_=b[:, 0:ncols], func=AF.Square),
             reads=[xs], writes=[qs])
        for i, (c0, c1) in enumerate(cts):
            S.op("pe", lambda h, q=qb, i=i, c0=c0, c1=c1, k=kc: h.matmul(
                C.psS[i][:, 0:c1 - c0], lhsT=C.ones32, rhs=q[:, c0:c1], start=(k == 0), stop=(k == KC - 1)),
                reads=[qs, C.const_slot], writes=[C.psS_slot[i]])
    for i, (c0, c1) in enumerate(cts):
        S.op("act", lambda h, i=i, c0=c0, c1=c1: h.activation(
            out=C.rstd[:, c0:c1], in_=C.psS[i][:, 0:c1 - c0], func=AF.Sqrt, bias=C.eps_ap, scale=1.0 / D),
            reads=[C.psS_slot[i], C.const_slot], writes=[C.rstd_slot])
    S.op("dve", lambda h: h.reciprocal(out=C.rstd[:, 0:ncols], in_=C.rstd[:, 0:ncols]),
         reads=[C.rstd_slot], writes=[C.rstd_slot])
    for kc in range(KC):
        xb, xs = C.xring.next()
        S.op("sp", lambda h, b=xb, k=kc: h.dma_start(out=b[:, 0:ncols], in_=xsrc_fn(k)),
             reads=[src_slot], writes=[xs], dma=xs)
        S.op("dve", lambda h, b=xb, k=kc: h.scalar_tensor_tensor(
            out=hT[k][:, 0:ncols], in0=b[:, 0:ncols], scalar=C.vecs[:, want_h_gcol + k:want_h_gcol + k + 1],
            in1=C.rstd[:, 0:ncols], op0=ALU.mult, op1=ALU.mult),
            reads=[xs, C.rstd_slot, C.const_slot], writes=[hslots[kc]])


def proj_chunk(S, C, w_ap, nk, rhs_list, rhs_slots, cts, wring, ps_bufs, ps_slots):
    wb, ws = wring.next()
    wv = wb.rearrange("p (a b) -> p a b", b=128)
    S.op("pool", lambda h: h.dma_start(out=wv[:, 0:nk, :], in_=w_ap.rearrange("(a p) n -> p a n", p=128)),
         writes=[ws], dma=ws)
    for k in range(nk):
        for i, (c0, c1) in enumerate(cts):
            S.op("pe", lambda h, k=k, i=i, c0=c0, c1=c1: h.matmul(
                ps_bufs[i][:, 0:c1 - c0], lhsT=wv[:, k, :], rhs=rhs_list[k][:, c0:c1],
                start=(k == 0), stop=(k == nk - 1)),
                reads=[ws, rhs_slots[k]], writes=[ps_slots[i]])


def wout_and_residual(S, C, w_out_ap, aT, aslots, ncols, cts, gcol, xsrc_fn, src_slot, xdst_fn, dst_slot, yscr_fn, yslot):
    pend = None
    for m in range(KC):
        pb, pslots = C.psring.next()
        proj_chunk(S, C, w_out_ap[:, m * 128:(m + 1) * 128], KC, aT, aslots, cts, C.wring, pb, pslots)
        yb, ys = C.yring.next()
        qb, qs = C.sqring.next()
        for i, (c0, c1) in enumerate(cts):
            S.op("act", lambda h, i=i, c0=c0, c1=c1, pb=pb, yb=yb: h.activation(
                out=yb[:, c0:c1], in_=pb[i][:, 0:c1 - c0], func=AF.Copy), reads=[pslots[i]], writes=[ys])
            S.op("act", lambda h, i=i, c0=c0, c1=c1, pb=pb, qb=qb: h.activation(
                out=qb[:, c0:c1], in_=pb[i][:, 0:c1 - c0], func=AF.Square), reads=[pslots[i]], writes=[qs])
        S.op("sp", lambda h, m=m, yb=yb: h.dma_start(out=yscr_fn(m), in_=yb[:, 0:ncols]),
             reads=[ys], writes=[yslot], dma=yslot)
        if pend is not None:
            pend()

        def mk(m=m, qb=qb, qs=qs):
            def f():
                for i, (c0, c1) in enumerate(cts):
                    S.op("pe", lambda h, i=i, c0=c0, c1=c1: h.matmul(
                        C.psS[i][:, 0:c1 - c0], lhsT=C.ones32, rhs=qb[:, c0:c1], start=(m == 0), stop=(m == KC - 1)),
                        reads=[qs, C.const_slot], writes=[C.psS_slot[i]])
            return f
        pend = mk()
    pend()
    for i, (c0, c1) in enumerate(cts):
        S.op("act", lambda h, i=i, c0=c0, c1=c1: h.activation(
            out=C.rstd[:, c0:c1], in_=C.psS[i][:, 0:c1 - c0], func=AF.Sqrt, bias=C.eps_ap, scale=1.0 / D),
            reads=[C.psS_slot[i], C.const_slot], writes=[C.rstd_slot])
    S.op("dve", lambda h: h.reciprocal(out=C.rstd[:, 0:ncols], in_=C.rstd[:, 0:ncols]),
         reads=[C.rstd_slot], writes=[C.rstd_slot])
    for m in range(KC):
        yb, ys = C.yring.next()
        S.op("sp", lambda h, m=m, yb=yb: h.dma_start(out=yb[:, 0:ncols], in_=yscr_fn(m)),
             reads=[yslot], writes=[ys], dma=ys)
        xb, xs = C.xring.next()
        S.op("sp", lambda h, m=m, xb=xb: h.dma_start(out=xb[:, 0:ncols], in_=xsrc_fn(m)),
             reads=[src_slot], writes=[xs], dma=xs)
        S.op("dve", lambda h, m=m, yb=yb: h.scalar_tensor_tensor(
            out=yb[:, 0:ncols], in0=yb[:, 0:ncols], scalar=C.vecs[:, gcol + m:gcol + m + 1],
            in1=C.rstd[:, 0:ncols], op0=ALU.mult, op1=ALU.mult),
            reads=[ys, C.rstd_slot, C.const_slot], writes=[ys])
        S.op("pool", lambda h, yb=yb, xb=xb: h.tensor_tensor(
            out=xb[:, 0:ncols], in0=yb[:, 0:ncols], in1=xb[:, 0:ncols], op=ALU.add),
            reads=[ys, xs], writes=[xs])
        S.op("sp", lambda h, m=m, xb=xb: h.dma_start(out=xdst_fn(m), in_=xb[:, 0:ncols]),
             reads=[xs], writes=[dst_slot], dma=dst_slot)


def alloc_psum(nc, C):
    C.banks = [nc.alloc_psum_tensor("bank%d" % i, [128, 512], F32) for i in range(8)]
    C.bank_slots = [Slot("bank%d" % i) for i in range(8)]


def build_phase_a():
    nc = bass.Bass("TRN2", target_bir_lowering=False)
    NT = NSTR * SW
    dr = {}
    dr["xT"] = nc.dram_tensor("xT", [D, NT], F32, kind="ExternalInput").ap()
    dr["a_w_in"] = nc.dram_tensor("a_w_in", [2, D, 2 * D], F32, kind="ExternalInput").ap()
    dr["a_w_group"] = nc.dram_tensor("a_w_group", [2, 4, 1024, 1024], F32, kind="ExternalInput").ap()
    dr["a_w_out"] = nc.dram_tensor("a_w_out", [2, D, D], F32, kind="ExternalInput").ap()
    dr["w_kv"] = nc.dram_tensor("w_kv", [D, 2 * D], F32, kind="ExternalInput").ap()
    dr["vecs"] = nc.dram_tensor("vecs", [128, 7 * KC], F32, kind="ExternalInput").ap()
    dr["icnt"] = nc.dram_tensor("icnt", [128, 64], F32, kind="ExternalInput").ap()
    dr["ones32"] = nc.dram_tensor("ones32", [128, 128], F32, kind="ExternalInput").ap()
    dr["epsv"] = nc.dram_tensor("epsv", [128, 1], F32, kind="ExternalInput").ap()
    x1 = nc.dram_tensor("x1s", [D, NT], F32).ap()
    x2 = nc.dram_tensor("x2T", [D, NT], F32, kind="ExternalOutput").ap()
    yscr = nc.dram_tensor("yscr", [D, PA_COLS], F32).ap()
    kto = nc.dram_tensor("KT", [NH, 128, NSTR * 128], BF16, kind="ExternalOutput").ap()
    vo = nc.dram_tensor("V", [NSTR * 128, D], BF16, kind="ExternalOutput").ap()

    S = Sched()
    C = Ctx()
    ar = Arena(nc, 206 * 1024)
    alloc_psum(nc, C)
    consts, C.const_slot = load_consts(S, ar, dr, [("vecs", 7 * KC, F32), ("icnt", 64, F32),
                                                    ("ones32", 128, F32), ("epsv", 1, F32)])
    C.vecs = consts["vecs"]
    C.ones32 = consts["ones32"]
    C.eps_ap = consts["epsv"]
    icnt = consts["icnt"]
    ncols = PA_COLS
    cts = [(0, PA_CT), (PA_CT, PA_COLS)]
    C.rstd = ar.alloc(ncols, F32, "rstd")
    C.rstd_slot = Slot("rstd")
    C.psS = [C.banks[6], C.banks[7]]
    C.psS_slot = [C.bank_slots[6], C.bank_slots[7]]

    class PsRing:
        def __init__(self):
            self.i = 0

        def next(self):
            b = self.i % 3
            self.i += 1
            return [C.banks[2 * b], C.banks[2 * b + 1]], [C.bank_slots[2 * b], C.bank_slots[2 * b + 1]]
    C.psring = PsRing()
    hT = [ar.alloc(ncols, BF16, "hT") for _ in range(KC)]
    hslots = [Slot("h%d" % k) for k in range(KC)]
    base = ar.mark()
    aT = [ar.alloc(ncols, BF16, "aT") for _ in range(KC)]
    aslots = [Slot("a%d" % k) for k in range(KC)]
    C.wring = Ring([ar.alloc(KC * 128, BF16, "w") for _ in range(3)], "w")
    wgring = Ring([ar.alloc(8 * 128, BF16, "wg") for _ in range(2)], "wg")
    C.xring = Ring([ar.alloc(ncols, F32, "xc") for _ in range(3)], "xc")
    C.yring = Ring([ar.alloc(ncols, F32, "yc") for _ in range(3)], "yc")
    C.sqring = Ring([ar.alloc(ncols, F32, "sq") for _ in range(2)], "sq")
    uring = Ring([ar.alloc(ncols, F32, "u") for _ in range(2)], "u")
    sA = ar.alloc(ncols, F32, "sA")
    sB = ar.alloc(ncols, F32, "sB")
    sAs, sBs = Slot("sA"), Slot("sB")
    pooled = [ar.alloc(ncols, BF16, "pooled") for _ in range(8)]
    pslots_ = [Slot("pl%d" % k) for k in range(8)]
    sgring = Ring([ar.alloc(ncols, BF16, "sg") for _ in range(2)], "sg")
    tmp16 = ar.alloc(16, F32, "tmp16")
    tmp16s = Slot("tmp16")
    for b, s in ((sA, sAs), (sB, sBs)):
        S.op("pool", lambda h, b=b: h.memset(b, 0.0), writes=[s])
    for b, s in zip(uring.bufs, uring.slots):
        S.op("pool", lambda h, b=b: h.memset(b, 0.0), writes=[s])
    for b, s in zip(pooled, pslots_):
        S.op("pool", lambda h, b=b: h.memset(b, 0.0), writes=[s])

    xin_slot, x1_slot, x2_slot, y_slot = Slot("xin"), Slot("x1"), Slot("x2"), Slot("yscr")
    kt_slot, v_slot = Slot("kt"), Slot("v")

    for p in range(2):
        cb = p * PA_COLS

        def src_of(t, cb=cb):
            return lambda k: t[k * 128:(k + 1) * 128, cb:cb + ncols]
        for l in range(2):
            xsrc, sslot = (dr["xT"], xin_slot) if l == 0 else (x1, x1_slot)
            xdst, dslot = (x1, x1_slot) if l == 0 else (x2, x2_slot)
            rms_stats(S, C, src_of(xsrc), ncols, cts, l * KC, hT, hslots, sslot)
            w_in = dr["a_w_in"][l]
            for og in range(4):
                win = WINS[og]
                for j in range(8):
                    m = og * 8 + j
                    pb, psl = C.psring.next()
                    proj_chunk(S, C, w_in[:, m * 128:(m + 1) * 128], KC, hT, hslots, cts, C.wring, pb, psl)
                    ub, us = uring.next()
                    for i, (c0, c1) in enumerate(cts):
                        S.op("act", lambda h, i=i, c0=c0, c1=c1, pb=pb, ub=ub: h.activation(
                            out=ub[:, c0:c1], in_=pb[i][:, 0:c1 - c0], func=AF.Copy), reads=[psl[i]], writes=[us])
                    cur, curs = ub, us
                    sh = 1
                    tgl = 0
                    while sh < win:
                        nb, nbs = (sA, sAs) if tgl == 0 else (sB, sBs)
                        S.op("pool", lambda h, cur=cur, nb=nb, sh=sh: h.tensor_tensor(
                            out=nb[:, sh:ncols], in0=cur[:, sh:ncols], in1=cur[:, 0:ncols - sh], op=ALU.add),
                            reads=[curs], writes=[nbs])
                        cur, curs = nb, nbs
                        sh *= 2
                        tgl ^= 1
                    S.op("dve", lambda h, cur=cur, ub=ub, j=j, win=win: h.scalar_tensor_tensor(
                        out=pooled[j][:, 16:ncols], in0=cur[:, 16:ncols], scalar=1.0 / win, in1=ub[:, 16:ncols],
                        op0=ALU.mult, op1=ALU.subtract), reads=[curs, us], writes=[pslots_[j]])
                    if p == 0:
                        S.op("dve", lambda h, cur=cur, og=og: h.tensor_tensor(
                            out=tmp16, in0=cur[:, HALO:HALO + 16], in1=icnt[:, og * 16:(og + 1) * 16], op=ALU.mult),
                            reads=[curs, C.const_slot], writes=[tmp16s], sync_same=True)
                        S.op("dve", lambda h, ub=ub, j=j: h.tensor_tensor(
                            out=pooled[j][:, HALO:HALO + 16], in0=tmp16, in1=ub[:, HALO:HALO + 16], op=ALU.subtract),
                            reads=[tmp16s, us], writes=[pslots_[j]], sync_same=True)
                for jo in range(8):
                    m = og * 8 + jo
                    pb, psl = C.psring.next()
                    proj_chunk(S, C, w_in[:, D + m * 128:D + (m + 1) * 128], KC, hT, hslots, cts, C.wring, pb, psl)
                    gb, gs = sgring.next()
                    for i, (c0, c1) in enumerate(cts):
                        S.op("act", lambda h, i=i, c0=c0, c1=c1, pb=pb, gb=gb: h.activation(
                            out=gb[:, c0:c1], in_=pb[i][:, 0:c1 - c0], func=AF.Silu), reads=[psl[i]], writes=[gs])
                    pb2, psl2 = C.psring.next()
                    proj_chunk(S, C, dr["a_w_group"][l, og][:, jo * 128:(jo + 1) * 128], 8, pooled, pslots_, cts,
                               wgring, pb2, psl2)
                    for i, (c0, c1) in enumerate(cts):
                        S.op("dve", lambda h, i=i, c0=c0, c1=c1, pb2=pb2, gb=gb, m=m, l=l: h.scalar_tensor_tensor(
                            out=aT[m][:, c0:c1], in0=pb2[i][:, 0:c1 - c0],
                            scalar=C.vecs[:, (2 + l) * KC + m:(2 + l) * KC + m + 1], in1=gb[:, c0:c1],
                            op0=ALU.mult, op1=ALU.mult), reads=[psl2[i], gs, C.const_slot], writes=[aslots[m]])
            wout_and_residual(S, C, dr["a_w_out"][l], aT, aslots, ncols, cts, (4 + l) * KC,
                              src_of(xsrc), sslot, src_of(xdst), dslot,
                              lambda m: yscr[m * 128:(m + 1) * 128, :], y_slot)
        rms_stats(S, C, src_of(x2), ncols, cts, 6 * KC, hT, hslots, x2_slot)
        ktring = Ring([aT[0][:, 0:512], aT[1][:, 0:512]], "ktb")
        for a_ in (0, 1):
            ktring.slots[a_] = aslots[a_]
        for hd in range(NH):
            pb, psl = C.psring.next()
            proj_chunk(S, C, dr["w_kv"][:, hd * 128:(hd + 1) * 128], KC, hT, hslots, cts, C.wring, pb, psl)
            kb_, ks_ = ktring.next()
            for s4 in range(4):
                i = s4 // 2
                o = (s4 % 2) * SW + HALO
                S.op("act", lambda h, i=i, o=o, s4=s4, pb=pb, kb_=kb_: h.activation(
                    out=kb_[:, s4 * 128:(s4 + 1) * 128], in_=pb[i][:, o:o + 128], func=AF.Copy),
                    reads=[psl[i]], writes=[ks_])
            S.op("sp", lambda h, hd=hd, kb_=kb_, p=p: h.dma_start(
                out=kto[hd, :, p * 512:(p + 1) * 512], in_=kb_), reads=[ks_], writes=[kt_slot], dma=kt_slot)
        m0 = ar.mark()
        ar.reset(base + 2 * ncols * 2)
        wv_bufs = [ar.alloc(16 * 512, BF16, "wv") for _ in range(2)]
        vb_bufs = [ar.alloc(512, BF16, "vb") for _ in range(2)]
        assert ar.mark() <= base + KC * ncols * 2
        ar.reset(m0)
        wvring = Ring(wv_bufs, "wv")
        vbring = Ring(vb_bufs, "vb")
        for s_ in wvring.slots + vbring.slots:
            s_.w = None
        region_slots = aslots[2:]
        for ft in range(8):
            for half in range(2):
                wb, ws = wvring.next()
                wv = wb.rearrange("p (a b) -> p a b", b=512)
                src = dr["w_kv"][half * 2048:(half + 1) * 2048, D + ft * 512:D + (ft + 1) * 512]
                S.op("pool", lambda h, wv=wv, src=src: h.dma_start(
                    out=wv, in_=src.rearrange("(a p) n -> p a n", p=128)),
                    reads=[], writes=[ws] + (region_slots if (ft == 0) else []), dma=ws)
                for s4 in range(4):
                    o = s4 * SW + HALO
                    for kk in range(16):
                        k = half * 16 + kk
                        S.op("pe", lambda h, s4=s4, o=o, k=k, kk=kk, wv=wv: h.matmul(
                            C.banks[s4][:, :], lhsT=hT[k][:, o:o + 128], rhs=wv[:, kk, :],
                            start=(k == 0), stop=(k == KC - 1)),
                            reads=[ws, hslots[k]], writes=[C.bank_slots[s4]])
            for s4 in range(4):
                vb, vs = vbring.next()
                S.op("act" if s4 % 2 == 0 else "dve",
                     (lambda h, s4=s4, vb=vb: h.activation(out=vb, in_=C.banks[s4][:, :], func=AF.Copy))
                     if s4 % 2 == 0 else
                     (lambda h, s4=s4, vb=vb: h.tensor_copy(out=vb, in_=C.banks[s4][:, :])),
                     reads=[C.bank_slots[s4]], writes=[vs] + (region_slots if (ft == 0 and s4 < 2) else []))
                r0 = p * 512 + s4 * 128
                S.op("sp", lambda h, vb=vb, r0=r0, ft=ft: h.dma_start(
                    out=vo[r0:r0 + 128, ft * 512:(ft + 1) * 512], in_=vb), reads=[vs], writes=[v_slot], dma=v_slot)
        for s_ in region_slots:
            for r_ in wvring.slots + vbring.slots:
                if r_.w is not None:
                    s_.r.append(r_.w)
                s_.r.extend(r_.r)
    S.emit(nc, [x2_slot, kt_slot, v_slot])
    return nc


def build_phase_b():
    nc = bass.Bass("TRN2", target_bir_lowering=False)
    NT = NSTR * 128
    dr = {}
    dr["x2T"] = nc.dram_tensor("x2T", [D, NT], F32, kind="ExternalInput").ap()
    dr["KT"] = nc.dram_tensor("KT", [NH, 128, SEQ], BF16, kind="ExternalInput").ap()
    dr["Vh"] = nc.dram_tensor("Vh", [NH, 128, SEQ], BF16, kind="ExternalInput").ap()
    dr["b_w_in"] = nc.dram_tensor("b_w_in", [2, D, 2 * D], F32, kind="ExternalInput").ap()
    dr["b_w_out"] = nc.dram_tensor("b_w_out", [2, D, D], F32, kind="ExternalInput").ap()
    dr["vecs"] = nc.dram_tensor("vecs", [128, 4 * KC], F32, kind="ExternalInput").ap()
    dr["masks"] = nc.dram_tensor("masks", [128, 8 * 128], BF16, kind="ExternalInput").ap()
    dr["negU"] = nc.dram_tensor("negU", [128, 128], BF16, kind="ExternalInput").ap()
    dr["negO"] = nc.dram_tensor("negO", [128, 128], BF16, kind="ExternalInput").ap()
    dr["zer"] = nc.dram_tensor("zer", [128, 128], BF16, kind="ExternalInput").ap()
    dr["ones32"] = nc.dram_tensor("ones32", [128, 128], F32, kind="ExternalInput").ap()
    dr["epsv"] = nc.dram_tensor("epsv", [128, 1], F32, kind="ExternalInput").ap()
    x3 = nc.dram_tensor("x3s", [D, NT], F32).ap()
    outT = nc.dram_tensor("outT", [D, NT], F32, kind="ExternalOutput").ap()
    yscr = nc.dram_tensor("yscr", [D, PB_COLS], F32).ap()

    S = Sched()
    C = Ctx()
    ar = Arena(nc, 206 * 1024)
    alloc_psum(nc, C)
    consts, C.const_slot = load_consts(S, ar, dr, [
        ("vecs", 4 * KC, F32), ("masks", 8 * 128, BF16), ("negU", 128, BF16), ("negO", 128, BF16),
        ("zer", 128, BF16), ("ones32", 128, F32), ("epsv", 1, F32)])
    C.vecs = consts["vecs"]
    C.ones32 = consts["ones32"]
    C.eps_ap = consts["epsv"]
    masks, negU, negO, zer = consts["masks"], consts["negU"], consts["negO"], consts["zer"]
    ncols = PB_COLS
    cts = [(0, ncols)]
    C.rstd = ar.alloc(ncols, F32, "rstd")
    C.rstd_slot = Slot("rstd")
    C.psS = [C.banks[7]]
    C.psS_slot = [C.bank_slots[7]]

    class PsRing:
        def __init__(self):
            self.i = 0

        def next(self):
            b = 5 + (self.i % 2)
            self.i += 1
            return [C.banks[b]], [C.bank_slots[b]]
    C.psring = PsRing()
    hT = [ar.alloc(ncols, BF16, "hT") for _ in range(KC)]
    hslots = [Slot("h%d" % k) for k in range(KC)]
    aT = [ar.alloc(ncols, BF16, "aT") for _ in range(KC)]
    aslots = [Slot("a%d" % k) for k in range(KC)]
    C.wring = Ring([ar.alloc(KC * 128, BF16, "w") for _ in range(3)], "w")
    C.xring = Ring([ar.alloc(ncols, F32, "xc") for _ in range(3)], "xc")
    C.yring = Ring([ar.alloc(ncols, F32, "yc") for _ in range(3)], "yc")
    C.sqring = Ring([ar.alloc(ncols, F32, "sq") for _ in range(2)], "sq")
    qring = Ring([ar.alloc(ncols, BF16, "qT") for _ in range(2)], "qT")
    sgring = Ring([ar.alloc(ncols, BF16, "sg") for _ in range(2)], "sg")
    NQ = 5
    kq_bufs = [ar.alloc(2048, BF16, "kq") for _ in range(NQ)]
    vq_bufs = [ar.alloc(2048, BF16, "vq") for _ in range(NQ)]
    kq_slots = [Slot("kq%d" % i) for i in range(NQ)]
    vq_slots = [Slot("vq%d" % i) for i in range(NQ)]
    e_buf = ar.alloc(ncols, F32, "e")
    e_slot = Slot("e")
    spring = Ring([ar.alloc(ncols, BF16, "sp") for _ in range(2)], "sp")
    pring = Ring([ar.alloc(ncols, BF16, "P") for _ in range(3)], "P")
    ssum = [ar.alloc(ncols, BF16, "ssum") for _ in range(2)]
    ssum_slots = [Slot("ssum0"), Slot("ssum1")]

    x2_slot, x3_slot, out_slot, y_slot = Slot("x2"), Slot("x3"), Slot("out"), Slot("yscr")
    qcount = [0]

    for G in range(2):
        g0 = G * ncols

        def src_of(t, g0=g0):
            return lambda k: t[k * 128:(k + 1) * 128, g0:g0 + ncols]
        for l in range(2):
            xsrc, sslot = (dr["x2T"], x2_slot) if l == 0 else (x3, x3_slot)
            xdst, dslot = (x3, x3_slot) if l == 0 else (outT, out_slot)
            rms_stats(S, C, src_of(xsrc), ncols, cts, l * KC, hT, hslots, sslot)
            w_in = dr["b_w_in"][l]

            def head_proj_ops(hd):
                thunks = []
                qb, qs = qring.next()
                gb, gs = sgring.next()
                for which in range(2):
                    col = hd * 128 + which * D
                    pb, psl = C.psring.next()
                    wb, ws = C.wring.next()
                    wv = wb.rearrange("p (a b) -> p a b", b=128)
                    w_ap = w_in[:, col:col + 128]

                    def t_load(wv=wv, w_ap=w_ap, ws=ws):
                        S.op("pool", lambda h: h.dma_start(out=wv, in_=w_ap.rearrange("(a p) n -> p a n", p=128)),
                             writes=[ws], dma=ws)
                    thunks.append(t_load)
                    for k in range(KC):
                        def t_mm(k=k, wv=wv, ws=ws, pb=pb, psl=psl):
                            S.op("pe", lambda h: h.matmul(pb[0][:, :], lhsT=wv[:, k, :], rhs=hT[k],
                                                          start=(k == 0), stop=(k == KC - 1)),
                                 reads=[ws, hslots[k]], writes=[psl[0]])
                        thunks.append(t_mm)
                    if which == 0:
                        def t_ev(pb=pb, psl=psl, qb=qb, qs=qs):
                            S.op("dve", lambda h: h.tensor_scalar(out=qb, in0=pb[0][:, :], scalar1=float(128 ** -0.5),
                                                                  scalar2=None, op0=ALU.mult),
                                 reads=[psl[0]], writes=[qs])
                    else:
                        def t_ev(pb=pb, psl=psl, gb=gb, gs=gs):
                            S.op("act", lambda h: h.activation(out=gb, in_=pb[0][:, :], func=AF.Silu),
                                 reads=[psl[0]], writes=[gs])
                    thunks.append(t_ev)
                return thunks, qb, qs, gb, gs

            if G == 0:
                kb_list = list(range(31, -1, -1))
            else:
                kb_list = list(range(63, -1, -1))
            nsteps = len(kb_list)
            heads = []
            th0, qb0, qs0, gb0, gs0 = head_proj_ops(0)
            for t_ in th0:
                t_()
            heads.append((qb0, qs0, gb0, gs0))
            steps = []
            for hd in range(NH):
                for si, kb in enumerate(kb_list):
                    steps.append({"hd": hd, "si": si, "kb": kb})
            pending = []
            state = {}
            qlist = [(hd_, kb_ // 16) for hd_ in range(NH) for kb_ in kb_list if kb_ % 16 == 15]
            qslot_of = {}
            T = len(steps)

            def stage_qk(st):
                hd, si, kb = st["hd"], st["si"], st["kb"]
                if si == 0:
                    if hd + 1 < NH:
                        th, qb, qs, gb, gs = head_proj_ops(hd + 1)
                        pending.extend(th)
                        heads.append((qb, qs, gb, gs))
                if kb % 16 == 15:
                    qidx = state.get("qidx", -1) + 1
                    state["qidx"] = qidx
                    while state.get("emitted", 0) < min(qidx + 3, len(qlist)):
                        e_i = state.get("emitted", 0)
                        hd_, qq_ = qlist[e_i]
                        qi_ = qcount[0] % NQ
                        qcount[0] += 1
                        qslot_of[e_i] = qi_
                        S.op("sp", lambda h, hd_=hd_, qq_=qq_, qi_=qi_: h.dma_start(
                            out=kq_bufs[qi_], in_=dr["KT"][hd_, :, qq_ * 2048:(qq_ + 1) * 2048]),
                            writes=[kq_slots[qi_]], dma=kq_slots[qi_])
                        S.op("sp", lambda h, hd_=hd_, qq_=qq_, qi_=qi_: h.dma_start(
                            out=vq_bufs[qi_], in_=dr["Vh"][hd_, :, qq_ * 2048:(qq_ + 1) * 2048]),
                            writes=[vq_slots[qi_]], dma=vq_slots[qi_])
                        state["emitted"] = e_i + 1
                    state["qi"] = qslot_of[qidx]
                st["qi"] = state["qi"]
                qi = st["qi"]
                if G == 0 or kb >= 32:
                    r = (kb % 32) // 8
                    st["a0"] = 128 * r
                    st["mask"] = kb % 8
                else:
                    st["a0"] = 0
                    st["mask"] = None
                a0 = st["a0"]
                zb = st["gi"] % 3
                st["zb"] = zb
                qb, qs, gb, gs = heads[hd]
                ko = (kb % 16) * 128
                S.op("pe", lambda h: h.matmul(C.banks[zb][:, a0:ncols], lhsT=kq_bufs[qi][:, ko:ko + 128],
                                              rhs=qb[:, a0:ncols], start=True, stop=False, skip_group_check=True),
                     reads=[kq_slots[qi], qs], writes=[C.bank_slots[zb]])
                S.op("act", lambda h: h.activation(out=e_buf[:, a0:ncols], in_=C.banks[zb][:, a0:ncols], func=AF.Exp),
                     reads=[C.bank_slots[zb]], writes=[e_slot])

            def stage_qk_b(st):
                a0 = st["a0"]
                sb, ss = spring.next()
                st["sb"], st["ss"] = sb, ss
                S.op("act", lambda h: h.activation(out=sb[:, a0:ncols], in_=e_buf[:, a0:ncols], func=AF.Ln,
                                                   bias=1.0, scale=1.0), reads=[e_slot], writes=[ss])
                if st["mask"] is not None:
                    mk = st["mask"]
                    S.op("pool", lambda h: h.tensor_tensor(out=sb[:, a0:a0 + 128], in0=sb[:, a0:a0 + 128],
                                                           in1=masks[:, mk * 128:(mk + 1) * 128], op=ALU.mult),
                         reads=[ss, C.const_slot], writes=[ss])

            def stage_s(st):
                si = st["si"]
                a0, zb, sb, ss = st["a0"], st["zb"], st["sb"], st["ss"]
                first = (si == 0)
                S.op("pe", lambda h: h.matmul(C.banks[zb][:, a0:ncols], lhsT=negU, rhs=sb[:, a0:ncols],
                                              start=False, stop=first, skip_group_check=True),
                     reads=[ss, C.const_slot], writes=[C.bank_slots[zb]])
                cur = si % 2
                nxt = (si + 1) % 2
                if not first:
                    S.op("pe", lambda h: h.matmul(C.banks[zb][:, a0:ncols], lhsT=negO, rhs=ssum[cur][:, a0:ncols],
                                                  start=False, stop=True, skip_group_check=True),
                         reads=[ssum_slots[cur], C.const_slot], writes=[C.bank_slots[zb]])
                    if si + 1 < nsteps:
                        S.op("dve", lambda h: h.tensor_tensor(out=ssum[nxt][:, a0:ncols], in0=ssum[cur][:, a0:ncols],
                                                              in1=sb[:, a0:ncols], op=ALU.add),
                             reads=[ssum_slots[cur], ss], writes=[ssum_slots[nxt]], sync_same=True)
                else:
                    if si + 1 < nsteps:
                        S.op("dve", lambda h: h.tensor_copy(out=ssum[nxt][:, a0:ncols], in_=sb[:, a0:ncols]),
                             reads=[ss], writes=[ssum_slots[nxt]], sync_same=True)
                        if a0 > 0:
                            S.op("dve", lambda h: h.memset(ssum[nxt][:, 0:a0], 0.0), writes=[ssum_slots[nxt]],
                                 sync_same=True)
                            S.op("dve", lambda h: h.memset(ssum[cur][:, 0:a0], 0.0), writes=[ssum_slots[cur]],
                                 sync_same=True)
                pb_, ps_ = pring.next()
                st["pb"], st["ps"] = pb_, ps_
                S.op("act", lambda h: h.activation(out=pb_[:, a0:ncols], in_=C.banks[zb][:, a0:ncols], func=AF.Exp),
                     reads=[C.bank_slots[zb]], writes=[ps_])
                if st["mask"] is not None:
                    mk = st["mask"]
                    S.op("pool", lambda h: h.tensor_tensor(out=pb_[:, a0:a0 + 128], in0=pb_[:, a0:a0 + 128],
                                                           in1=masks[:, mk * 128:(mk + 1) * 128], op=ALU.mult),
                         reads=[ps_, C.const_slot], writes=[ps_])

            def stage_av(st):
                hd, si, kb = st["hd"], st["si"], st["kb"]
                a0, qi = st["a0"], st["qi"]
                ob = 3 + (hd % 2)
                qb, qs, gb, gs = heads[hd]
                if si == 0:
                    S.op("pe", lambda h: h.matmul(C.banks[ob][:, :], lhsT=zer, rhs=qb, start=True, stop=False,
                                                  skip_group_check=True),
                         reads=[C.const_slot, qs], writes=[C.bank_slots[ob]])
                vo_ = (kb % 16) * 128
                pb_, ps_ = st["pb"], st["ps"]
                S.op("pe", lambda h: h.matmul(C.banks[ob][:, a0:ncols], lhsT=vq_bufs[qi][:, vo_:vo_ + 128],
                                              rhs=pb_[:, a0:ncols], start=False, stop=(si == nsteps - 1),
                                              skip_group_check=True),
                     reads=[vq_slots[qi], ps_], writes=[C.bank_slots[ob]])
                if si == nsteps - 1:
                    S.op("dve", lambda h: h.tensor_tensor(out=aT[hd], in0=C.banks[ob][:, :], in1=gb, op=ALU.mult),
                         reads=[C.bank_slots[ob], gs], writes=[aslots[hd]])

            for gi, st in enumerate(steps):
                st["gi"] = gi
            npend_per = 3 if G == 0 else 2
            for t in range(T + 2):
                if t < T:
                    stage_qk(steps[t])
                if 1 <= t <= T:
                    stage_s(steps[t - 1])
                if t < T:
                    stage_qk_b(steps[t])
                if t >= 2:
                    stage_av(steps[t - 2])
                for _ in range(npend_per):
                    if pending:
                        pending.pop(0)()
            while pending:
                pending.pop(0)()
            wout_and_residual(S, C, dr["b_w_out"][l], aT, aslots, ncols, cts, (2 + l) * KC,
                              src_of(xsrc), sslot, src_of(xdst), dslot,
                              lambda m: yscr[m * 128:(m + 1) * 128, :], y_slot)
    S.emit(nc, [out_slot])
    return nc


def _vec_cols(v):
    return np.ascontiguousarray(np.asarray(v, np.float32).reshape(KC, 128).T)


_CACHE = {}


def kernel(x, a_pre_norm, a_w_in, a_w_group, a_scale, a_w_out, a_post_norm,
           kv_norm, w_kv, b_pre_norm, b_w_in, b_w_out, b_post_norm):
    bf = ml_dtypes.bfloat16
    x = np.asarray(x, np.float32)[0]
    xT_full = np.ascontiguousarray(x.T)
    a_w_in = np.asarray(a_w_in, np.float32)
    a_w_group = np.asarray(a_w_group, np.float32)
    a_w_out = np.asarray(a_w_out, np.float32)
    w_kv = np.asarray(w_kv, np.float32)
    b_w_in = np.asarray(b_w_in, np.float32)
    b_w_out = np.asarray(b_w_out, np.float32)
    ones32 = np.ones((128, 128), np.float32)
    epsv = np.full((128, 1), EPS, np.float32)
    vecsA = np.concatenate([_vec_cols(a_pre_norm[0]), _vec_cols(a_pre_norm[1]), _vec_cols(a_scale[0]),
                            _vec_cols(a_scale[1]), _vec_cols(a_post_norm[0]), _vec_cols(a_post_norm[1]),
                            _vec_cols(kv_norm)], axis=1)
    in_maps = []
    for c in range(NCORE):
        xT = np.zeros((D, NSTR * SW), np.float32)
        for k in range(NSTR):
            b = 8 * k + c
            lo = 128 * b - HALO
            if lo >= 0:
                xT[:, k * SW:(k + 1) * SW] = xT_full[:, lo:lo + SW]
            else:
                xT[:, k * SW + HALO:(k + 1) * SW] = xT_full[:, 0:128]
        icnt = np.zeros((128, 64), np.float32)
        for g, w in enumerate(WINS):
            pos = 128 * c + np.arange(16)
            icnt[:, g * 16:(g + 1) * 16] = (1.0 / np.minimum(pos + 1, w)).astype(np.float32)[None, :]
        in_maps.append({"xT": xT, "a_w_in": a_w_in, "a_w_group": a_w_group, "a_w_out": a_w_out, "w_kv": w_kv,
                        "vecs": vecsA, "icnt": icnt, "ones32": ones32, "epsv": epsv})
    if "A" not in _CACHE:
        _CACHE["A"] = build_phase_a()
    resA = run_bass_kernel_spmd(_CACHE["A"], in_maps, core_ids=list(range(NCORE))).results
    KT = np.zeros((NH, 128, SEQ), bf)
    Vh = np.zeros((NH, 128, SEQ), bf)
    x2_list = []
    for c in range(NCORE):
        kt = np.asarray(resA[c]["KT"])
        v = np.asarray(resA[c]["V"])
        x2 = np.asarray(resA[c]["x2T"])
        own = np.concatenate([np.arange(k * SW + HALO, (k + 1) * SW) for k in range(NSTR)])
        x2_list.append(np.ascontiguousarray(x2[:, own]))
        for k in range(NSTR):
            b = 8 * k + c
            KT[:, :, 128 * b:128 * b + 128] = kt[:, :, 128 * k:128 * k + 128]
            vb = v[128 * k:128 * k + 128, :].reshape(128, NH, 128)
            Vh[:, :, 128 * b:128 * b + 128] = vb.transpose(1, 0, 2)
    vecsB = np.concatenate([_vec_cols(b_pre_norm[0]), _vec_cols(b_pre_norm[1]),
                            _vec_cols(b_post_norm[0]), _vec_cols(b_post_norm[1])], axis=1)
    jj = np.arange(128)
    tri = (jj[:, None] < jj[None, :]).astype(np.float32)
    negU = (-(jj[:, None] >= jj[None, :]).astype(np.float32)).astype(bf)
    negO = (-np.ones((128, 128), np.float32)).astype(bf)
    zer = np.zeros((128, 128), bf)
    in_maps = []
    for c in range(NCORE):
        m = np.zeros((128, 8, 128), np.float32)
        for r in range(8):
            if r < c:
                m[:, r, :] = 1.0
            elif r == c:
                m[:, r, :] = tri
        in_maps.append({"x2T": x2_list[c], "KT": KT, "Vh": Vh, "b_w_in": b_w_in, "b_w_out": b_w_out,
                        "vecs": vecsB, "masks": m.reshape(128, 1024).astype(bf), "negU": negU, "negO": negO,
                        "zer": zer, "ones32": ones32, "epsv": epsv})
    if "B" not in _CACHE:
        _CACHE["B"] = build_phase_b()
    resB = run_bass_kernel_spmd(_CACHE["B"], in_maps, core_ids=list(range(NCORE))).results
    out = np.zeros((SEQ, D), np.float32)
    for c in range(NCORE):
        oT = np.asarray(resB[c]["outT"])
        for k in range(NSTR):
            b = 8 * k + c
            out[128 * b:128 * b + 128, :] = oT[:, 128 * k:128 * k + 128].T
    return out[None]
```

```python
import contextlib
import numpy as np
import ml_dtypes
import concourse.bass as bass
import concourse.mybir as mybir
from concourse.bass_utils import run_bass_kernel_spmd

F32 = mybir.dt.float32
BF16 = mybir.dt.bfloat16
AF = mybir.ActivationFunctionType
ALU = mybir.AluOpType

D = 4096
KC = D // 128
SEQ = 8192
NCORE = 8
NH = 32
EPS = 1e-6
HALO = 32
SW = 128 + HALO
NSTR = 8
PA_COLS = 4 * SW
PA_CT = PA_COLS // 2
PB_COLS = 512
WINS = (2, 4, 8, 16)
ENGS = ("pe", "act", "dve", "pool", "sp")


class Slot:
    __slots__ = ("name", "w", "r", "sem", "ndma")

    def __init__(self, name):
        self.name = name
        self.w = None
        self.r = []
        self.sem = None
        self.ndma = 0


class Sched:
    def __init__(self):
        self.ops = {e: [] for e in ENGS}
        self.seen_c = {e: {} for e in ENGS}
        self.seen_d = {e: {} for e in ENGS}
        self.dma_slots = []

    def _need(self, eng, tok, is_dma):
        if tok is None:
            return False
        if tok[0] == "c":
            _, se, idx = tok
            if se == eng and not is_dma and not self.sync_same:
                return False
            if self.seen_c[eng].get(se, -1) >= idx:
                return False
            self.seen_c[eng][se] = idx
            return True
        _, sl, cnt = tok
        if self.seen_d[eng].get(sl, 0) >= cnt:
            return False
        self.seen_d[eng][sl] = cnt
        return True

    def op(self, eng, fn, reads=(), writes=(), dma=None, sync_same=False):
        is_dma = dma is not None
        self.sync_same = sync_same
        waits = []
        for s in reads:
            if self._need(eng, s.w, is_dma):
                waits.append(s.w)
        for s in writes:
            if self._need(eng, s.w, is_dma):
                waits.append(s.w)
            for t in s.r:
                if self._need(eng, t, is_dma):
                    waits.append(t)
        idx = len(self.ops[eng])
        if is_dma:
            if dma.sem is None:
                dma.sem = True
                self.dma_slots.append(dma)
            dma.ndma += 1
            tok = ("d", dma, dma.ndma)
        else:
            tok = ("c", eng, idx)
        self.ops[eng].append({"fn": fn, "waits": waits, "dma": dma, "inc": False})
        for s in reads:
            s.r.append(tok)
        for s in writes:
            s.w = tok
            s.r = []
        return tok

    def emit(self, nc, final_slots):
        for e in ENGS:
            for o in self.ops[e]:
                for t in o["waits"]:
                    if t[0] == "c":
                        self.ops[t[1]][t[2]]["inc"] = True
        cum = {}
        for e in ENGS:
            c = 0
            arr = []
            for o in self.ops[e]:
                if o["inc"]:
                    c += 1
                arr.append(c)
            cum[e] = arr
        with contextlib.ExitStack() as st:
            esem = {e: st.enter_context(nc.semaphore("e_" + e)) for e in ENGS}
            for i, sl in enumerate(self.dma_slots):
                sl.sem = st.enter_context(nc.semaphore("d%d" % i))
            block = st.enter_context(nc.Block())

            def run(e, h):
                for o in self.ops[e]:
                    for t in o["waits"]:
                        if t[0] == "c":
                            h.wait_ge(esem[t[1]], cum[t[1]][t[2]])
                        else:
                            h.wait_ge(t[1].sem, 16 * t[2])
                    ins = o["fn"](h)
                    if o["dma"] is not None:
                        ins.then_inc(o["dma"].sem, 16)
                    elif o["inc"]:
                        ins.then_inc(esem[e], 1)
                if e == "sp":
                    for sl in final_slots:
                        if sl.ndma:
                            h.wait_ge(sl.sem, 16 * sl.ndma)

            @block.tensor
            def _(h):
                run("pe", h)

            @block.scalar
            def _(h):
                run("act", h)

            @block.vector
            def _(h):
                run("dve", h)

            @block.gpsimd
            def _(h):
                run("pool", h)

            @block.sync
            def _(h):
                run("sp", h)


class Arena:
    def __init__(self, nc, nbytes):
        self.t = nc.alloc_sbuf_tensor("arena", [128, nbytes // 4], F32)
        self.off = 0
        self.cap = nbytes

    def mark(self):
        return self.off

    def reset(self, m):
        self.off = m

    def alloc(self, n, dt, name=None):
        nb = n * (2 if dt == BF16 else 4)
        nb = (nb + 63) // 64 * 64
        assert self.off + nb <= self.cap, ("SBUF overflow", name, self.off, nb)
        a = self.t[:, self.off // 4:(self.off + nb) // 4]
        self.off += nb
        if dt == BF16:
            a = a.bitcast(BF16)
        return a[:, 0:n]


class Ring:
    def __init__(self, bufs, name):
        self.bufs = bufs
        self.slots = [Slot("%s%d" % (name, i)) for i in range(len(bufs))]
        self.i = 0

    def next(self):
        k = self.i % len(self.bufs)
        self.i += 1
        return self.bufs[k], self.slots[k]


class WCache:
    def __init__(self, S, bufs, name):
        self.S = S
        self.ring = Ring(bufs, name)
        self.keys = {}

    def get(self, w2d, wname, col, nk):
        g = col // 512
        key = (wname, g)
        if key not in self.keys:
            wb, ws = self.ring.next()
            for k_ in [k_ for k_, v_ in self.keys.items() if v_[1] is ws]:
                del self.keys[k_]
            wv = wb.rearrange("p (a b) -> p a b", b=512)
            src = w2d[:, g * 512:(g + 1) * 512].rearrange("(a p) n -> p a n", p=128)
            self.S.op("pool", lambda h: h.dma_start(out=wv[:, 0:nk, :], in_=src), writes=[ws], dma=ws)
            self.keys[key] = (wv, ws)
        wv, ws = self.keys[key]
        o = col - g * 512
        return wv[:, :, o:o + 128], ws


def load_consts(S, ar, dram, names):
    out = {}
    sl = Slot("consts")
    for key, n, dt in names:
        buf = ar.alloc(n, dt, key)
        out[key] = buf
        S.op("sp", lambda h, b=buf, k=key: h.dma_start(out=b, in_=dram[k]), writes=[sl], dma=sl)
    return out, sl


class Ctx:
    pass


def rms_stats(S, C, xsrc_fn, ncols, cts, want_h_gcol, hT, hslots, src_slot):
    for kc in range(KC):
        xb, xs = C.xring.next()
        S.op("sp", lambda h, b=xb, k=kc: h.dma_start(out=b[:, 0:ncols], in_=xsrc_fn(k)),
             reads=[src_slot], writes=[xs], dma=xs)
        qb, qs = C.sqring.next()
        S.op("act", lambda h, b=xb, q=qb: h.activation(out=q[:, 0:ncols], in_=b[:, 0:ncols], func=AF.Square),
             reads=[xs], writes=[qs])
        for i, (c0, c1) in enumerate(cts):
            S.op("pe", lambda h, q=qb, i=i, c0=c0, c1=c1, k=kc: h.matmul(
                C.psS[i][:, 0:c1 - c0], lhsT=C.ones32, rhs=q[:, c0:c1], start=(k == 0), stop=(k == KC - 1)),
                reads=[qs, C.const_slot], writes=[C.psS_slot[i]])
    for i, (c0, c1) in enumerate(cts):
        S.op("act", lambda h, i=i, c0=c0, c1=c1: h.activation(
            out=C.rstd[:, c0:c1], in_=C.psS[i][:, 0:c1 - c0], func=AF.Sqrt, bias=C.eps_ap, scale=1.0 / D),
            reads=[C.psS_slot[i], C.const_slot], writes=[C.rstd_slot])
    S.op("dve", lambda h: h.reciprocal(out=C.rstd[:, 0:ncols], in_=C.rstd[:, 0:ncols]),
         reads=[C.rstd_slot], writes=[C.rstd_slot])
    for kc in range(KC):
        xb, xs = C.xring.next()
        S.op("sp", lambda h, b=xb, k=kc: h.dma_start(out=b[:, 0:ncols], in_=xsrc_fn(k)),
             reads=[src_slot], writes=[xs], dma=xs)
        S.op("dve", lambda h, b=xb, k=kc: h.scalar_tensor_tensor(
            out=hT[k][:, 0:ncols], in0=b[:, 0:ncols], scalar=C.vecs[:, want_h_gcol + k:want_h_gcol + k + 1],
            in1=C.rstd[:, 0:ncols], op0=ALU.mult, op1=ALU.mult),
            reads=[xs, C.rstd_slot, C.const_slot], writes=[hslots[kc]])


def proj_chunk(S, C, w_ap, nk, rhs_list, rhs_slots, cts, wring, ps_bufs, ps_slots):
    if isinstance(wring, WCache):
        w2d, wname, col = w_ap
        wv, ws = wring.get(w2d, wname, col, nk)
    else:
        wb, ws = wring.next()
        wv = wb.rearrange("p (a b) -> p a b", b=128)
        S.op("pool", lambda h: h.dma_start(out=wv[:, 0:nk, :], in_=w_ap.rearrange("(a p) n -> p a n", p=128)),
             writes=[ws], dma=ws)
    for k in range(nk):
        for i, (c0, c1) in enumerate(cts):
            S.op("pe", lambda h, k=k, i=i, c0=c0, c1=c1: h.matmul(
                ps_bufs[i][:, 0:c1 - c0], lhsT=wv[:, k, :], rhs=rhs_list[k][:, c0:c1],
                start=(k == 0), stop=(k == nk - 1)),
                reads=[ws, rhs_slots[k]], writes=[ps_slots[i]])


def wout_and_residual(S, C, w_out_ap, wname, aT, aslots, ncols, cts, gcol, xsrc_fn, src_slot, xdst_fn, dst_slot, yscr_fn, yslot):
    pend = None
    for m in range(KC):
        pb, pslots = C.psring.next()
        proj_chunk(S, C, (w_out_ap, wname, m * 128), KC, aT, aslots, cts, C.wring, pb, pslots)
        yb, ys = C.yring.next()
        qb, qs = C.sqring.next()
        for i, (c0, c1) in enumerate(cts):
            S.op("act", lambda h, i=i, c0=c0, c1=c1, pb=pb, yb=yb: h.activation(
                out=yb[:, c0:c1], in_=pb[i][:, 0:c1 - c0], func=AF.Copy), reads=[pslots[i]], writes=[ys])
            S.op("act", lambda h, i=i, c0=c0, c1=c1, pb=pb, qb=qb: h.activation(
                out=qb[:, c0:c1], in_=pb[i][:, 0:c1 - c0], func=AF.Square), reads=[pslots[i]], writes=[qs])
        S.op("act", lambda h, m=m, yb=yb: h.dma_start(out=yscr_fn(m), in_=yb[:, 0:ncols]),
             reads=[ys], writes=[yslot], dma=yslot)
        if pend is not None:
            pend()

        def mk(m=m, qb=qb, qs=qs):
            def f():
                for i, (c0, c1) in enumerate(cts):
                    S.op("pe", lambda h, i=i, c0=c0, c1=c1: h.matmul(
                        C.psS[i][:, 0:c1 - c0], lhsT=C.ones32, rhs=qb[:, c0:c1], start=(m == 0), stop=(m == KC - 1)),
                        reads=[qs, C.const_slot], writes=[C.psS_slot[i]])
            return f
        pend = mk()
    pend()
    for i, (c0, c1) in enumerate(cts):
        S.op("act", lambda h, i=i, c0=c0, c1=c1: h.activation(
            out=C.rstd[:, c0:c1], in_=C.psS[i][:, 0:c1 - c0], func=AF.Sqrt, bias=C.eps_ap, scale=1.0 / D),
            reads=[C.psS_slot[i], C.const_slot], writes=[C.rstd_slot])
    S.op("dve", lambda h: h.reciprocal(out=C.rstd[:, 0:ncols], in_=C.rstd[:, 0:ncols]),
         reads=[C.rstd_slot], writes=[C.rstd_slot])
    for m in range(KC):
        yb, ys = C.yring.next()
        S.op("sp", lambda h, m=m, yb=yb: h.dma_start(out=yb[:, 0:ncols], in_=yscr_fn(m)),
             reads=[yslot], writes=[ys], dma=ys)
        xb, xs = C.xring.next()
        S.op("sp", lambda h, m=m, xb=xb: h.dma_start(out=xb[:, 0:ncols], in_=xsrc_fn(m)),
             reads=[src_slot], writes=[xs], dma=xs)
        S.op("dve", lambda h, m=m, yb=yb: h.scalar_tensor_tensor(
            out=yb[:, 0:ncols], in0=yb[:, 0:ncols], scalar=C.vecs[:, gcol + m:gcol + m + 1],
            in1=C.rstd[:, 0:ncols], op0=ALU.mult, op1=ALU.mult),
            reads=[ys, C.rstd_slot, C.const_slot], writes=[ys])
        S.op("dve", lambda h, yb=yb, xb=xb: h.tensor_tensor(
            out=xb[:, 0:ncols], in0=yb[:, 0:ncols], in1=xb[:, 0:ncols], op=ALU.add),
            reads=[ys, xs], writes=[xs])
        S.op("act", lambda h, m=m, xb=xb: h.dma_start(out=xdst_fn(m), in_=xb[:, 0:ncols]),
             reads=[xs], writes=[dst_slot], dma=dst_slot)


def alloc_psum(nc, C):
    C.banks = [nc.alloc_psum_tensor("bank%d" % i, [128, 512], F32) for i in range(8)]
    C.bank_slots = [Slot("bank%d" % i) for i in range(8)]


def build_phase_a():
    nc = bass.Bass("TRN2", target_bir_lowering=False)
    NT = NSTR * SW
    dr = {}
    dr["xT"] = nc.dram_tensor("xT", [D, NT], F32, kind="ExternalInput").ap()
    dr["a_w_in"] = nc.dram_tensor("a_w_in", [2, D, 2 * D], F32, kind="ExternalInput").ap()
    dr["a_w_group"] = nc.dram_tensor("a_w_group", [2, 4, 1024, 1024], F32, kind="ExternalInput").ap()
    dr["a_w_out"] = nc.dram_tensor("a_w_out", [2, D, D], F32, kind="ExternalInput").ap()
    dr["w_kv"] = nc.dram_tensor("w_kv", [D, 2 * D], F32, kind="ExternalInput").ap()
    dr["vecs"] = nc.dram_tensor("vecs", [128, 7 * KC], F32, kind="ExternalInput").ap()
    dr["icnt"] = nc.dram_tensor("icnt", [128, 64], F32, kind="ExternalInput").ap()
    dr["ones32"] = nc.dram_tensor("ones32", [128, 128], F32, kind="ExternalInput").ap()
    dr["epsv"] = nc.dram_tensor("epsv", [128, 1], F32, kind="ExternalInput").ap()
    x1 = nc.dram_tensor("x1s", [D, NT], F32).ap()
    x2 = nc.dram_tensor("x2T", [D, NT], F32, kind="ExternalOutput").ap()
    yscr = nc.dram_tensor("yscr", [D, PA_COLS], F32).ap()
    kto = nc.dram_tensor("KT", [NH, 128, NSTR * 128], BF16, kind="ExternalOutput").ap()
    vo = nc.dram_tensor("V", [NSTR * 128, D], BF16, kind="ExternalOutput").ap()

    S = Sched()
    C = Ctx()
    ar = Arena(nc, 206 * 1024)
    alloc_psum(nc, C)
    consts, C.const_slot = load_consts(S, ar, dr, [("vecs", 7 * KC, F32), ("icnt", 64, F32),
                                                    ("ones32", 128, F32), ("epsv", 1, F32)])
    C.vecs = consts["vecs"]
    C.ones32 = consts["ones32"]
    C.eps_ap = consts["epsv"]
    icnt = consts["icnt"]
    ncols = PA_COLS
    cts = [(0, PA_CT), (PA_CT, PA_COLS)]
    C.rstd = ar.alloc(ncols, F32, "rstd")
    C.rstd_slot = Slot("rstd")
    C.psS = [C.banks[6], C.banks[7]]
    C.psS_slot = [C.bank_slots[6], C.bank_slots[7]]

    class PsRing:
        def __init__(self):
            self.i = 0

        def next(self):
            b = self.i % 3
            self.i += 1
            return [C.banks[2 * b], C.banks[2 * b + 1]], [C.bank_slots[2 * b], C.bank_slots[2 * b + 1]]
    C.psring = PsRing()
    hT = [ar.alloc(ncols, BF16, "hT") for _ in range(KC)]
    hslots = [Slot("h%d" % k) for k in range(KC)]
    base = ar.mark()
    aT = [ar.alloc(ncols, BF16, "aT") for _ in range(KC)]
    aslots = [Slot("a%d" % k) for k in range(KC)]
    C.wring = WCache(S, [ar.alloc(KC * 512, BF16, "w") for _ in range(2)], "w")
    wgring = Ring([ar.alloc(8 * 128, BF16, "wg") for _ in range(2)], "wg")
    C.xring = Ring([ar.alloc(ncols, F32, "xc") for _ in range(3)], "xc")
    C.yring = Ring([ar.alloc(ncols, F32, "yc") for _ in range(3)], "yc")
    C.sqring = Ring([ar.alloc(ncols, F32, "sq") for _ in range(2)], "sq")
    uring = Ring([ar.alloc(ncols, F32, "u") for _ in range(2)], "u")
    sA = ar.alloc(ncols, F32, "sA")
    sB = ar.alloc(ncols, F32, "sB")
    sAs, sBs = Slot("sA"), Slot("sB")
    pooled = [ar.alloc(ncols, BF16, "pooled") for _ in range(8)]
    pslots_ = [Slot("pl%d" % k) for k in range(8)]
    sgring = Ring([ar.alloc(ncols, BF16, "sg") for _ in range(2)], "sg")
    tmp16 = ar.alloc(16, F32, "tmp16")
    tmp16s = Slot("tmp16")
    for b, s in ((sA, sAs), (sB, sBs)):
        S.op("dve", lambda h, b=b: h.memset(b, 0.0), writes=[s])
    for b, s in zip(uring.bufs, uring.slots):
        S.op("dve", lambda h, b=b: h.memset(b, 0.0), writes=[s])
    for b, s in zip(pooled, pslots_):
        S.op("dve", lambda h, b=b: h.memset(b, 0.0), writes=[s])

    xin_slot, x1_slot, x2_slot, y_slot = Slot("xin"), Slot("x1"), Slot("x2"), Slot("yscr")
    kt_slot, v_slot = Slot("kt"), Slot("v")

    for p in range(2):
        cb = p * PA_COLS

        def src_of(t, cb=cb):
            return lambda k: t[k * 128:(k + 1) * 128, cb:cb + ncols]
        for l in range(2):
            xsrc, sslot = (dr["xT"], xin_slot) if l == 0 else (x1, x1_slot)
            xdst, dslot = (x1, x1_slot) if l == 0 else (x2, x2_slot)
            rms_stats(S, C, src_of(xsrc), ncols, cts, l * KC, hT, hslots, sslot)
            w_in = dr["a_w_in"][l]
            for og in range(4):
                win = WINS[og]
                for j in range(8):
                    m = og * 8 + j
                    pb, psl = C.psring.next()
                    proj_chunk(S, C, (w_in, "a_w_in%d" % l, m * 128), KC, hT, hslots, cts, C.wring, pb, psl)
                    ub, us = uring.next()
                    for i, (c0, c1) in enumerate(cts):
                        S.op("act", lambda h, i=i, c0=c0, c1=c1, pb=pb, ub=ub: h.activation(
                            out=ub[:, c0:c1], in_=pb[i][:, 0:c1 - c0], func=AF.Copy), reads=[psl[i]], writes=[us])
                    cur, curs = ub, us
                    sh = 1
                    tgl = 0
                    while sh < win:
                        nb, nbs = (sA, sAs) if tgl == 0 else (sB, sBs)
                        S.op("dve", lambda h, cur=cur, nb=nb, sh=sh: h.tensor_tensor(
                            out=nb[:, sh:ncols], in0=cur[:, sh:ncols], in1=cur[:, 0:ncols - sh], op=ALU.add),
                            reads=[curs], writes=[nbs])
                        cur, curs = nb, nbs
                        sh *= 2
                        tgl ^= 1
                    S.op("dve", lambda h, cur=cur, ub=ub, j=j, win=win: h.scalar_tensor_tensor(
                        out=pooled[j][:, 16:ncols], in0=cur[:, 16:ncols], scalar=1.0 / win, in1=ub[:, 16:ncols],
                        op0=ALU.mult, op1=ALU.subtract), reads=[curs, us], writes=[pslots_[j]])
                    if p == 0:
                        S.op("dve", lambda h, cur=cur, og=og: h.tensor_tensor(
                            out=tmp16, in0=cur[:, HALO:HALO + 16], in1=icnt[:, og * 16:(og + 1) * 16], op=ALU.mult),
                            reads=[curs, C.const_slot], writes=[tmp16s], sync_same=True)
                        S.op("dve", lambda h, ub=ub, j=j: h.tensor_tensor(
                            out=pooled[j][:, HALO:HALO + 16], in0=tmp16, in1=ub[:, HALO:HALO + 16], op=ALU.subtract),
                            reads=[tmp16s, us], writes=[pslots_[j]], sync_same=True)
                for jo in range(8):
                    m = og * 8 + jo
                    pb, psl = C.psring.next()
                    proj_chunk(S, C, (w_in, "a_w_in%d" % l, D + m * 128), KC, hT, hslots, cts, C.wring, pb, psl)
                    gb, gs = sgring.next()
                    for i, (c0, c1) in enumerate(cts):
                        S.op("act", lambda h, i=i, c0=c0, c1=c1, pb=pb, gb=gb: h.activation(
                            out=gb[:, c0:c1], in_=pb[i][:, 0:c1 - c0], func=AF.Silu), reads=[psl[i]], writes=[gs])
                    pb2, psl2 = C.psring.next()
                    proj_chunk(S, C, dr["a_w_group"][l, og][:, jo * 128:(jo + 1) * 128], 8, pooled, pslots_, cts,
                               wgring, pb2, psl2)
                    for i, (c0, c1) in enumerate(cts):
                        S.op("dve", lambda h, i=i, c0=c0, c1=c1, pb2=pb2, gb=gb, m=m, l=l: h.scalar_tensor_tensor(
                            out=aT[m][:, c0:c1], in0=pb2[i][:, 0:c1 - c0],
                            scalar=C.vecs[:, (2 + l) * KC + m:(2 + l) * KC + m + 1], in1=gb[:, c0:c1],
                            op0=ALU.mult, op1=ALU.mult), reads=[psl2[i], gs, C.const_slot], writes=[aslots[m]])
            wout_and_residual(S, C, dr["a_w_out"][l], "a_w_out%d" % l, aT, aslots, ncols, cts, (4 + l) * KC,
                              src_of(xsrc), sslot, src_of(xdst), dslot,
                              lambda m: yscr[m * 128:(m + 1) * 128, :], y_slot)
        rms_stats(S, C, src_of(x2), ncols, cts, 6 * KC, hT, hslots, x2_slot)
        ktring = Ring([aT[0][:, 0:512], aT[1][:, 0:512]], "ktb")
        for a_ in (0, 1):
            ktring.slots[a_] = aslots[a_]
        for hd in range(NH):
            pb, psl = C.psring.next()
            proj_chunk(S, C, (dr["w_kv"], "w_kv", hd * 128), KC, hT, hslots, cts, C.wring, pb, psl)
            kb_, ks_ = ktring.next()
            for s4 in range(4):
                i = s4 // 2
                o = (s4 % 2) * SW + HALO
                S.op("act", lambda h, i=i, o=o, s4=s4, pb=pb, kb_=kb_: h.activation(
                    out=kb_[:, s4 * 128:(s4 + 1) * 128], in_=pb[i][:, o:o + 128], func=AF.Copy),
                    reads=[psl[i]], writes=[ks_])
            S.op("act", lambda h, hd=hd, kb_=kb_, p=p: h.dma_start(
                out=kto[hd, :, p * 512:(p + 1) * 512], in_=kb_), reads=[ks_], writes=[kt_slot], dma=kt_slot)
        m0 = ar.mark()
        ar.reset(base + 2 * ncols * 2)
        wv_bufs = [ar.alloc(16 * 512, BF16, "wv") for _ in range(2)]
        vb_bufs = [ar.alloc(512, BF16, "vb") for _ in range(2)]
        assert ar.mark() <= base + KC * ncols * 2
        ar.reset(m0)
        wvring = Ring(wv_bufs, "wv")
        vbring = Ring(vb_bufs, "vb")
        for s_ in wvring.slots + vbring.slots:
            s_.w = None
        region_slots = aslots[2:]
        for ft in range(8):
            for half in range(2):
                wb, ws = wvring.next()
                wv = wb.rearrange("p (a b) -> p a b", b=512)
                src = dr["w_kv"][half * 2048:(half + 1) * 2048, D + ft * 512:D + (ft + 1) * 512]
                S.op("pool", lambda h, wv=wv, src=src: h.dma_start(
                    out=wv, in_=src.rearrange("(a p) n -> p a n", p=128)),
                    reads=[], writes=[ws] + (region_slots if (ft == 0) else []), dma=ws)
                for s4 in range(4):
                    o = s4 * SW + HALO
                    for kk in range(16):
                        k = half * 16 + kk
                        S.op("pe", lambda h, s4=s4, o=o, k=k, kk=kk, wv=wv: h.matmul(
                            C.banks[s4][:, :], lhsT=hT[k][:, o:o + 128], rhs=wv[:, kk, :],
                            start=(k == 0), stop=(k == KC - 1)),
                            reads=[ws, hslots[k]], writes=[C.bank_slots[s4]])
            for s4 in range(4):
                vb, vs = vbring.next()
                S.op("act" if s4 % 2 == 0 else "dve",
                     (lambda h, s4=s4, vb=vb: h.activation(out=vb, in_=C.banks[s4][:, :], func=AF.Copy))
                     if s4 % 2 == 0 else
                     (lambda h, s4=s4, vb=vb: h.tensor_copy(out=vb, in_=C.banks[s4][:, :])),
                     reads=[C.bank_slots[s4]], writes=[vs] + (region_slots if (ft == 0 and s4 < 2) else []))
                r0 = p * 512 + s4 * 128
                S.op("act", lambda h, vb=vb, r0=r0, ft=ft: h.dma_start(
                    out=vo[r0:r0 + 128, ft * 512:(ft + 1) * 512], in_=vb), reads=[vs], writes=[v_slot], dma=v_slot)
        for s_ in region_slots:
            for r_ in wvring.slots + vbring.slots:
                if r_.w is not None:
                    s_.r.append(r_.w)
                s_.r.extend(r_.r)
    S.emit(nc, [x2_slot, kt_slot, v_slot])
    return nc


def build_phase_b():
    nc = bass.Bass("TRN2", target_bir_lowering=False)
    NT = NSTR * 128
    dr = {}
    dr["x2T"] = nc.dram_tensor("x2T", [D, NT], F32, kind="ExternalInput").ap()
    dr["KT"] = nc.dram_tensor("KT", [NH, 128, SEQ], BF16, kind="ExternalInput").ap()
    dr["Vh"] = nc.dram_tensor("Vh", [NH, 128, SEQ], BF16, kind="ExternalInput").ap()
    dr["b_w_in"] = nc.dram_tensor("b_w_in", [2, D, 2 * D], F32, kind="ExternalInput").ap()
    dr["b_w_out"] = nc.dram_tensor("b_w_out", [2, D, D], F32, kind="ExternalInput").ap()
    dr["vecs"] = nc.dram_tensor("vecs", [128, 4 * KC], F32, kind="ExternalInput").ap()
    dr["masks"] = nc.dram_tensor("masks", [128, 8 * 128], BF16, kind="ExternalInput").ap()
    dr["negU"] = nc.dram_tensor("negU", [128, 128], BF16, kind="ExternalInput").ap()
    dr["negO"] = nc.dram_tensor("negO", [128, 128], BF16, kind="ExternalInput").ap()
    dr["zer"] = nc.dram_tensor("zer", [128, 128], BF16, kind="ExternalInput").ap()
    dr["ones32"] = nc.dram_tensor("ones32", [128, 128], F32, kind="ExternalInput").ap()
    dr["epsv"] = nc.dram_tensor("epsv", [128, 1], F32, kind="ExternalInput").ap()
    x3 = nc.dram_tensor("x3s", [D, NT], F32).ap()
    outT = nc.dram_tensor("outT", [D, NT], F32, kind="ExternalOutput").ap()
    yscr = nc.dram_tensor("yscr", [D, PB_COLS], F32).ap()

    S = Sched()
    C = Ctx()
    ar = Arena(nc, 206 * 1024)
    alloc_psum(nc, C)
    consts, C.const_slot = load_consts(S, ar, dr, [
        ("vecs", 4 * KC, F32), ("masks", 8 * 128, BF16), ("negU", 128, BF16), ("negO", 128, BF16),
        ("zer", 128, BF16), ("ones32", 128, F32), ("epsv", 1, F32)])
    C.vecs = consts["vecs"]
    C.ones32 = consts["ones32"]
    C.eps_ap = consts["epsv"]
    masks, negU, negO, zer = consts["masks"], consts["negU"], consts["negO"], consts["zer"]
    ncols = PB_COLS
    cts = [(0, ncols)]
    C.rstd = ar.alloc(ncols, F32, "rstd")
    C.rstd_slot = Slot("rstd")
    C.psS = [C.banks[7]]
    C.psS_slot = [C.bank_slots[7]]

    class PsRing:
        def __init__(self):
            self.i = 0

        def next(self):
            if self.mode == 1:
                b = 6 + (self.i % 2)
            else:
                b = self.i % 4
            self.i += 1
            return [C.banks[b]], [C.bank_slots[b]]
    C.psring = PsRing()
    C.psring.mode = 0
    hT = [ar.alloc(ncols, BF16, "hT") for _ in range(KC)]
    hslots = [Slot("h%d" % k) for k in range(KC)]
    aT = [ar.alloc(ncols, BF16, "aT") for _ in range(KC)]
    aslots = [Slot("a%d" % k) for k in range(KC)]
    C.wring = WCache(S, [ar.alloc(KC * 512, BF16, "w") for _ in range(2)], "w")
    C.xring = Ring([ar.alloc(ncols, F32, "xc") for _ in range(3)], "xc")
    C.yring = Ring([ar.alloc(ncols, F32, "yc") for _ in range(3)], "yc")
    C.sqring = Ring([ar.alloc(ncols, F32, "sq") for _ in range(2)], "sq")
    qring = Ring([ar.alloc(ncols, BF16, "qT") for _ in range(2)], "qT")
    sgring = Ring([ar.alloc(ncols, BF16, "sg") for _ in range(2)], "sg")
    NQ = 4
    kq_bufs = [ar.alloc(2048, BF16, "kq") for _ in range(NQ)]
    vq_bufs = [ar.alloc(2048, BF16, "vq") for _ in range(NQ)]
    kq_slots = [Slot("kq%d" % i) for i in range(NQ)]
    vq_slots = [Slot("vq%d" % i) for i in range(NQ)]
    e_buf = ar.alloc(ncols, F32, "e")
    e_slot = Slot("e")
    spring = Ring([ar.alloc(ncols, BF16, "sp") for _ in range(3)], "sp")
    pring = Ring([ar.alloc(ncols, BF16, "P") for _ in range(4)], "P")
    ssum = [ar.alloc(ncols, BF16, "ssum") for _ in range(2)]
    ssum_slots = [Slot("ssum0"), Slot("ssum1")]

    x2_slot, x3_slot, out_slot, y_slot = Slot("x2"), Slot("x3"), Slot("out"), Slot("yscr")
    qcount = [0]

    for G in range(2):
        g0 = G * ncols

        def src_of(t, g0=g0):
            return lambda k: t[k * 128:(k + 1) * 128, g0:g0 + ncols]
        for l in range(2):
            xsrc, sslot = (dr["x2T"], x2_slot) if l == 0 else (x3, x3_slot)
            xdst, dslot = (x3, x3_slot) if l == 0 else (outT, out_slot)
            rms_stats(S, C, src_of(xsrc), ncols, cts, l * KC, hT, hslots, sslot)
            w_in = dr["b_w_in"][l]

            def head_proj_ops(hd):
                thunks = []
                qb, qs = qring.next()
                gb, gs = sgring.next()
                for which in range(2):
                    col = hd * 128 + which * D
                    pb, psl = C.psring.next()
                    wv, ws = C.wring.get(w_in, "b_w_in%d" % l, col, KC)
                    for k in range(KC):
                        def t_mm(k=k, wv=wv, ws=ws, pb=pb, psl=psl):
                            S.op("pe", lambda h: h.matmul(pb[0][:, :], lhsT=wv[:, k, :], rhs=hT[k],
                                                          start=(k == 0), stop=(k == KC - 1)),
                                 reads=[ws, hslots[k]], writes=[psl[0]])
                        thunks.append(t_mm)
                    if which == 0:
                        def t_ev(pb=pb, psl=psl, qb=qb, qs=qs):
                            S.op("dve", lambda h: h.tensor_scalar(out=qb, in0=pb[0][:, :], scalar1=float(128 ** -0.5),
                                                                  scalar2=None, op0=ALU.mult),
                                 reads=[psl[0]], writes=[qs])
                    else:
                        def t_ev(pb=pb, psl=psl, gb=gb, gs=gs):
                            S.op("act", lambda h: h.activation(out=gb, in_=pb[0][:, :], func=AF.Silu),
                                 reads=[psl[0]], writes=[gs])
                    thunks.append(t_ev)
                return thunks, qb, qs, gb, gs

            if G == 0:
                kb_list = list(range(31, -1, -1))
            else:
                kb_list = list(range(63, -1, -1))
            nsteps = len(kb_list)
            heads = []
            C.psring.mode = 1
            th0, qb0, qs0, gb0, gs0 = head_proj_ops(0)
            for t_ in th0:
                t_()
            heads.append((qb0, qs0, gb0, gs0))
            steps = []
            for hd in range(NH):
                for si, kb in enumerate(kb_list):
                    steps.append({"hd": hd, "si": si, "kb": kb})
            pending = []
            state = {}
            qlist = [(hd_, kb_ // 16) for hd_ in range(NH) for kb_ in kb_list if kb_ % 16 == 15]
            qslot_of = {}
            T = len(steps)

            def emit_qk(st):
                hd, si, kb = st["hd"], st["si"], st["kb"]
                if si == 0:
                    if hd + 1 < NH:
                        th, qb, qs, gb, gs = head_proj_ops(hd + 1)
                        pending.extend(th)
                        heads.append((qb, qs, gb, gs))
                if kb % 16 == 15:
                    qidx = state.get("qidx", -1) + 1
                    state["qidx"] = qidx
                    while state.get("emitted", 0) < min(qidx + 3, len(qlist)):
                        e_i = state.get("emitted", 0)
                        hd_, qq_ = qlist[e_i]
                        qi_ = qcount[0] % NQ
                        qcount[0] += 1
                        qslot_of[e_i] = qi_
                        S.op("sp", lambda h, hd_=hd_, qq_=qq_, qi_=qi_: h.dma_start(
                            out=kq_bufs[qi_], in_=dr["KT"][hd_, :, qq_ * 2048:(qq_ + 1) * 2048]),
                            writes=[kq_slots[qi_]], dma=kq_slots[qi_])
                        S.op("sp", lambda h, hd_=hd_, qq_=qq_, qi_=qi_: h.dma_start(
                            out=vq_bufs[qi_], in_=dr["Vh"][hd_, :, qq_ * 2048:(qq_ + 1) * 2048]),
                            writes=[vq_slots[qi_]], dma=vq_slots[qi_])
                        state["emitted"] = e_i + 1
                    state["qi"] = qslot_of[qidx]
                st["qi"] = state["qi"]
                qi = st["qi"]
                if G == 0 or kb >= 32:
                    r = (kb % 32) // 8
                    st["a0"] = 128 * r
                    st["mask"] = kb % 8
                else:
                    st["a0"] = 0
                    st["mask"] = None
                a0 = st["a0"]
                zb = st["gi"] % 4
                st["zb"] = zb
                qb, qs, gb, gs = heads[hd]
                ko = (kb % 16) * 128
                S.op("pe", lambda h: h.matmul(C.banks[zb][:, a0:ncols], lhsT=kq_bufs[qi][:, ko:ko + 128],
                                              rhs=qb[:, a0:ncols], start=True, stop=False, skip_group_check=True),
                     reads=[kq_slots[qi], qs], writes=[C.bank_slots[zb]])

            def emit_exp1(st):
                a0, zb = st["a0"], st["zb"]
                S.op("act", lambda h: h.activation(out=e_buf[:, a0:ncols], in_=C.banks[zb][:, a0:ncols], func=AF.Exp),
                     reads=[C.bank_slots[zb]], writes=[e_slot])

            def emit_ln(st):
                a0 = st["a0"]
                sb, ss = spring.next()
                st["sb"], st["ss"] = sb, ss
                S.op("act", lambda h: h.activation(out=sb[:, a0:ncols], in_=e_buf[:, a0:ncols], func=AF.Ln,
                                                   bias=1.0, scale=1.0), reads=[e_slot], writes=[ss])
                if st["mask"] is not None:
                    mk = st["mask"]
                    S.op("dve", lambda h: h.tensor_tensor(out=sb[:, a0:a0 + 128], in0=sb[:, a0:a0 + 128],
                                                          in1=masks[:, mk * 128:(mk + 1) * 128], op=ALU.mult),
                         reads=[ss, C.const_slot], writes=[ss])

            def emit_s(st):
                si = st["si"]
                a0, zb, sb, ss = st["a0"], st["zb"], st["sb"], st["ss"]
                first = (si == 0)
                S.op("pe", lambda h: h.matmul(C.banks[zb][:, a0:ncols], lhsT=negU, rhs=sb[:, a0:ncols],
                                              start=False, stop=first, skip_group_check=True),
                     reads=[ss, C.const_slot], writes=[C.bank_slots[zb]])
                cur = si % 2
                nxt = (si + 1) % 2
                if not first:
                    S.op("pe", lambda h: h.matmul(C.banks[zb][:, a0:ncols], lhsT=negO, rhs=ssum[cur][:, a0:ncols],
                                                  start=False, stop=True, skip_group_check=True),
                         reads=[ssum_slots[cur], C.const_slot], writes=[C.bank_slots[zb]])
                    if si + 1 < nsteps:
                        S.op("dve", lambda h: h.tensor_tensor(out=ssum[nxt][:, a0:ncols], in0=ssum[cur][:, a0:ncols],
                                                              in1=sb[:, a0:ncols], op=ALU.add),
                             reads=[ssum_slots[cur], ss], writes=[ssum_slots[nxt]], sync_same=True)
                else:
                    if si + 1 < nsteps:
                        S.op("dve", lambda h: h.tensor_copy(out=ssum[nxt][:, a0:ncols], in_=sb[:, a0:ncols]),
                             reads=[ss], writes=[ssum_slots[nxt]], sync_same=True)
                        if a0 > 0:
                            S.op("dve", lambda h: h.memset(ssum[nxt][:, 0:a0], 0.0), writes=[ssum_slots[nxt]],
                                 sync_same=True)
                            S.op("dve", lambda h: h.memset(ssum[cur][:, 0:a0], 0.0), writes=[ssum_slots[cur]],
                                 sync_same=True)

            def emit_expp(st):
                a0, zb = st["a0"], st["zb"]
                pb_, ps_ = pring.next()
                st["pb"], st["ps"] = pb_, ps_
                S.op("act", lambda h: h.activation(out=pb_[:, a0:ncols], in_=C.banks[zb][:, a0:ncols], func=AF.Exp),
                     reads=[C.bank_slots[zb]], writes=[ps_])
                if st["mask"] is not None:
                    mk = st["mask"]
                    S.op("dve", lambda h: h.tensor_tensor(out=pb_[:, a0:a0 + 128], in0=pb_[:, a0:a0 + 128],
                                                          in1=masks[:, mk * 128:(mk + 1) * 128], op=ALU.mult),
                         reads=[ps_, C.const_slot], writes=[ps_])

            def emit_av(st):
                hd, si, kb = st["hd"], st["si"], st["kb"]
                a0, qi = st["a0"], st["qi"]
                ob = 4 + (hd % 2)
                qb, qs, gb, gs = heads[hd]
                if si == 0:
                    S.op("pe", lambda h: h.matmul(C.banks[ob][:, :], lhsT=zer, rhs=qb, start=True, stop=False,
                                                  skip_group_check=True),
                         reads=[C.const_slot, qs], writes=[C.bank_slots[ob]])
                vo_ = (kb % 16) * 128
                pb_, ps_ = st["pb"], st["ps"]
                S.op("pe", lambda h: h.matmul(C.banks[ob][:, a0:ncols], lhsT=vq_bufs[qi][:, vo_:vo_ + 128],
                                              rhs=pb_[:, a0:ncols], start=False, stop=(si == nsteps - 1),
                                              skip_group_check=True),
                     reads=[vq_slots[qi], ps_], writes=[C.bank_slots[ob]])
                if si == nsteps - 1:
                    S.op("dve", lambda h: h.tensor_tensor(out=aT[hd], in0=C.banks[ob][:, :], in1=gb, op=ALU.mult),
                         reads=[C.bank_slots[ob], gs], writes=[aslots[hd]])

            for gi, st in enumerate(steps):
                st["gi"] = gi
            npend_per = 3 if G == 0 else 2
            C.psring.mode = 1
            emit_qk(steps[0])
            for t in range(T + 3):
                if t + 1 < T:
                    emit_qk(steps[t + 1])
                if t < T:
                    emit_exp1(steps[t])
                if 1 <= t <= T:
                    emit_s(steps[t - 1])
                if 2 <= t <= T + 1:
                    emit_expp(steps[t - 2])
                if t < T:
                    emit_ln(steps[t])
                if t >= 3:
                    emit_av(steps[t - 3])
                for _ in range(npend_per):
                    if pending:
                        pending.pop(0)()
            while pending:
                pending.pop(0)()
            C.psring.mode = 0
            wout_and_residual(S, C, dr["b_w_out"][l], "b_w_out%d" % l, aT, aslots, ncols, cts, (2 + l) * KC,
                              src_of(xsrc), sslot, src_of(xdst), dslot,
                              lambda m: yscr[m * 128:(m + 1) * 128, :], y_slot)
    S.emit(nc, [out_slot])
    return nc


def _vec_cols(v):
    return np.ascontiguousarray(np.asarray(v, np.float32).reshape(KC, 128).T)


_CACHE = {}


def kernel(x, a_pre_norm, a_w_in, a_w_group, a_scale, a_w_out, a_post_norm,
           kv_norm, w_kv, b_pre_norm, b_w_in, b_w_out, b_post_norm):
    bf = ml_dtypes.bfloat16
    x = np.asarray(x, np.float32)[0]
    xT_full = np.ascontiguousarray(x.T)
    a_w_in = np.asarray(a_w_in, np.float32)
    a_w_group = np.asarray(a_w_group, np.float32)
    a_w_out = np.asarray(a_w_out, np.float32)
    w_kv = np.asarray(w_kv, np.float32)
    b_w_in = np.asarray(b_w_in, np.float32)
    b_w_out = np.asarray(b_w_out, np.float32)
    ones32 = np.ones((128, 128), np.float32)
    epsv = np.full((128, 1), EPS, np.float32)
    vecsA = np.concatenate([_vec_cols(a_pre_norm[0]), _vec_cols(a_pre_norm[1]), _vec_cols(a_scale[0]),
                            _vec_cols(a_scale[1]), _vec_cols(a_post_norm[0]), _vec_cols(a_post_norm[1]),
                            _vec_cols(kv_norm)], axis=1)
    in_maps = []
    for c in range(NCORE):
        xT = np.zeros((D, NSTR * SW), np.float32)
        for k in range(NSTR):
            b = 8 * k + c
            lo = 128 * b - HALO
            if lo >= 0:
                xT[:, k * SW:(k + 1) * SW] = xT_full[:, lo:lo + SW]
            else:
                xT[:, k * SW + HALO:(k + 1) * SW] = xT_full[:, 0:128]
        icnt = np.zeros((128, 64), np.float32)
        for g, w in enumerate(WINS):
            pos = 128 * c + np.arange(16)
            icnt[:, g * 16:(g + 1) * 16] = (1.0 / np.minimum(pos + 1, w)).astype(np.float32)[None, :]
        in_maps.append({"xT": xT, "a_w_in": a_w_in, "a_w_group": a_w_group, "a_w_out": a_w_out, "w_kv": w_kv,
                        "vecs": vecsA, "icnt": icnt, "ones32": ones32, "epsv": epsv})
    if "A" not in _CACHE:
        _CACHE["A"] = build_phase_a()
    resA = run_bass_kernel_spmd(_CACHE["A"], in_maps, core_ids=list(range(NCORE))).results
    KT = np.zeros((NH, 128, SEQ), bf)
    Vh = np.zeros((NH, 128, SEQ), bf)
    x2_list = []
    for c in range(NCORE):
        kt = np.asarray(resA[c]["KT"])
        v = np.asarray(resA[c]["V"])
        x2 = np.asarray(resA[c]["x2T"])
        own = np.concatenate([np.arange(k * SW + HALO, (k + 1) * SW) for k in range(NSTR)])
        x2_list.append(np.ascontiguousarray(x2[:, own]))
        for k in range(NSTR):
            b = 8 * k + c
            KT[:, :, 128 * b:128 * b + 128] = kt[:, :, 128 * k:128 * k + 128]
            vb = v[128 * k:128 * k + 128, :].reshape(128, NH, 128)
            Vh[:, :, 128 * b:128 * b + 128] = vb.transpose(1, 0, 2)
    vecsB = np.concatenate([_vec_cols(b_pre_norm[0]), _vec_cols(b_pre_norm[1]),
                            _vec_cols(b_post_norm[0]), _vec_cols(b_post_norm[1])], axis=1)
    jj = np.arange(128)
    tri = (jj[:, None] < jj[None, :]).astype(np.float32)
    negU = (-(jj[:, None] >= jj[None, :]).astype(np.float32)).astype(bf)
    negO = (-np.ones((128, 128), np.float32)).astype(bf)
    zer = np.zeros((128, 128), bf)
    in_maps = []
    for c in range(NCORE):
        m = np.zeros((128, 8, 128), np.float32)
        for r in range(8):
            if r < c:
                m[:, r, :] = 1.0
            elif r == c:
                m[:, r, :] = tri
        in_maps.append({"x2T": x2_list[c], "KT": KT, "Vh": Vh, "b_w_in": b_w_in, "b_w_out": b_w_out,
                        "vecs": vecsB, "masks": m.reshape(128, 1024).astype(bf), "negU": negU, "negO": negO,
                        "zer": zer, "ones32": ones32, "epsv": epsv})
    if "B" not in _CACHE:
        _CACHE["B"] = build_phase_b()
    resB = run_bass_kernel_spmd(_CACHE["B"], in_maps, core_ids=list(range(NCORE))).results
    out = np.zeros((SEQ, D), np.float32)
    for c in range(NCORE):
        oT = np.asarray(resB[c]["outT"])
        for k in range(NSTR):
            b = 8 * k + c
            out[128 * b:128 * b + 128, :] = oT[:, 128 * k:128 * k + 128].T
    return out[None]
```

```python
import contextlib
import numpy as np
import ml_dtypes
import concourse.bass as bass
import concourse.mybir as mybir
from concourse.bass_utils import run_bass_kernel_spmd

F32 = mybir.dt.float32
BF16 = mybir.dt.bfloat16
AF = mybir.ActivationFunctionType
ALU = mybir.AluOpType

D = 4096
KC = D // 128
SEQ = 8192
NCORE = 8
NH = 32
EPS = 1e-6
HALO = 32
SW = 128 + HALO
NSTR = 8
PA_COLS = 4 * SW
PA_CT = PA_COLS // 2
PB_COLS = 512
WINS = (2, 4, 8, 16)
ENGS = ("pe", "act", "dve", "pool", "sp")


class Slot:
    __slots__ = ("name", "w", "r", "sem", "ndma")

    def __init__(self, name):
        self.name = name
        self.w = None
        self.r = []
        self.sem = None
        self.ndma = 0


class Sched:
    def __init__(self):
        self.ops = {e: [] for e in ENGS}
        self.seen_c = {e: {} for e in ENGS}
        self.seen_d = {e: {} for e in ENGS}
        self.dma_slots = []

    def _need(self, eng, tok, is_dma):
        if tok is None:
            return False
        if tok[0] == "c":
            _, se, idx = tok
            if se == eng and not is_dma and not self.sync_same:
                return False
            if self.seen_c[eng].get(se, -1) >= idx:
                return False
            self.seen_c[eng][se] = idx
            return True
        _, sl, cnt = tok
        if self.seen_d[eng].get(sl, 0) >= cnt:
            return False
        self.seen_d[eng][sl] = cnt
        return True

    def op(self, eng, fn, reads=(), writes=(), dma=None, sync_same=False):
        is_dma = dma is not None
        self.sync_same = sync_same
        waits = []
        for s in reads:
            if self._need(eng, s.w, is_dma):
                waits.append(s.w)
        for s in writes:
            if self._need(eng, s.w, is_dma):
                waits.append(s.w)
            for t in s.r:
                if self._need(eng, t, is_dma):
                    waits.append(t)
        idx = len(self.ops[eng])
        if is_dma:
            if dma.sem is None:
                dma.sem = True
                self.dma_slots.append(dma)
            dma.ndma += 1
            tok = ("d", dma, dma.ndma)
        else:
            tok = ("c", eng, idx)
        self.ops[eng].append({"fn": fn, "waits": waits, "dma": dma, "inc": False})
        for s in reads:
            s.r.append(tok)
        for s in writes:
            s.w = tok
            s.r = []
        return tok

    def emit(self, nc, final_slots):
        for e in ENGS:
            for o in self.ops[e]:
                for t in o["waits"]:
                    if t[0] == "c":
                        self.ops[t[1]][t[2]]["inc"] = True
        cum = {}
        for e in ENGS:
            c = 0
            arr = []
            for o in self.ops[e]:
                if o["inc"]:
                    c += 1
                arr.append(c)
            cum[e] = arr
        with contextlib.ExitStack() as st:
            esem = {e: st.enter_context(nc.semaphore("e_" + e)) for e in ENGS}
            for i, sl in enumerate(self.dma_slots):
                sl.sem = st.enter_context(nc.semaphore("d%d" % i))
            block = st.enter_context(nc.Block())

            def run(e, h):
                for o in self.ops[e]:
                    for t in o["waits"]:
                        if t[0] == "c":
                            h.wait_ge(esem[t[1]], cum[t[1]][t[2]])
                        else:
                            h.wait_ge(t[1].sem, 16 * t[2])
                    ins = o["fn"](h)
                    if o["dma"] is not None:
                        ins.then_inc(o["dma"].sem, 16)
                    elif o["inc"]:
                        ins.then_inc(esem[e], 1)
                if e == "sp":
                    for sl in final_slots:
                        if sl.ndma:
                            h.wait_ge(sl.sem, 16 * sl.ndma)

            @block.tensor
            def _(h):
                run("pe", h)

            @block.scalar
            def _(h):
                run("act", h)

            @block.vector
            def _(h):
                run("dve", h)

            @block.gpsimd
            def _(h):
                run("pool", h)

            @block.sync
            def _(h):
                run("sp", h)


class Arena:
    def __init__(self, nc, nbytes):
        self.t = nc.alloc_sbuf_tensor("arena", [128, nbytes // 4], F32)
        self.off = 0
        self.cap = nbytes

    def mark(self):
        return self.off

    def reset(self, m):
        self.off = m

    def alloc(self, n, dt, name=None):
        nb = n * (2 if dt == BF16 else 4)
        nb = (nb + 63) // 64 * 64
        assert self.off + nb <= self.cap, ("SBUF overflow", name, self.off, nb)
        a = self.t[:, self.off // 4:(self.off + nb) // 4]
        self.off += nb
        if dt == BF16:
            a = a.bitcast(BF16)
        return a[:, 0:n]


class Ring:
    def __init__(self, bufs, name):
        self.bufs = bufs
        self.slots = [Slot("%s%d" % (name, i)) for i in range(len(bufs))]
        self.i = 0

    def next(self):
        k = self.i % len(self.bufs)
        self.i += 1
        return self.bufs[k], self.slots[k]


class WCache:
    def __init__(self, S, bufs, name):
        self.S = S
        self.ring = Ring(bufs, name)
        self.keys = {}

    def get(self, w2d, wname, col, nk):
        g = col // 512
        key = (wname, g)
        if key not in self.keys:
            wb, ws = self.ring.next()
            for k_ in [k_ for k_, v_ in self.keys.items() if v_[1] is ws]:
                del self.keys[k_]
            wv = wb.rearrange("p (a b) -> p a b", b=512)
            src = w2d[:, g * 512:(g + 1) * 512].rearrange("(a p) n -> p a n", p=128)
            self.S.op("pool", lambda h: h.dma_start(out=wv[:, 0:nk, :], in_=src), writes=[ws], dma=ws)
            self.keys[key] = (wv, ws)
        wv, ws = self.keys[key]
        o = col - g * 512
        return wv[:, :, o:o + 128], ws


def load_consts(S, ar, dram, names):
    out = {}
    sl = Slot("consts")
    for key, n, dt in names:
        buf = ar.alloc(n, dt, key)
        out[key] = buf
        S.op("sp", lambda h, b=buf, k=key: h.dma_start(out=b, in_=dram[k]), writes=[sl], dma=sl)
    return out, sl


class Ctx:
    pass


def finalize_rstd(S, C, cts, ncols, dst, dst_slot):
    for i, (c0, c1) in enumerate(cts):
        S.op("act", lambda h, i=i, c0=c0, c1=c1: h.activation(
            out=dst[:, c0:c1], in_=C.psS[i][:, 0:c1 - c0], func=AF.Sqrt, bias=C.eps_ap, scale=1.0 / D),
            reads=[C.psS_slot[i], C.const_slot], writes=[dst_slot])
    S.op("dve", lambda h: h.reciprocal(out=dst[:, 0:ncols], in_=dst[:, 0:ncols]),
         reads=[dst_slot], writes=[dst_slot])


def norm_feed(S, C, xb, xs, kc, ncols, cts, gcol, hT, hslots):
    qb, qs = C.sqring.next()
    S.op("act", lambda h: h.activation(out=qb[:, 0:ncols], in_=xb[:, 0:ncols], func=AF.Square),
         reads=[xs], writes=[qs])
    for i, (c0, c1) in enumerate(cts):
        S.op("pe", lambda h, i=i, c0=c0, c1=c1: h.matmul(
            C.psS[i][:, 0:c1 - c0], lhsT=C.ones32, rhs=qb[:, c0:c1], start=(kc == 0), stop=(kc == KC - 1)),
            reads=[qs, C.const_slot], writes=[C.psS_slot[i]])
    S.op("dve", lambda h: h.tensor_scalar(out=hT[kc][:, 0:ncols], in0=xb[:, 0:ncols],
                                          scalar1=C.vecs[:, gcol + kc:gcol + kc + 1], scalar2=None, op0=ALU.mult),
         reads=[xs, C.const_slot], writes=[hslots[kc]])


def first_norm(S, C, xsrc_fn, ncols, cts, gcol, hT, hslots, src_slot):
    for kc in range(KC):
        xb, xs = C.xring.next()
        S.op("sp", lambda h, b=xb, k=kc: h.dma_start(out=b[:, 0:ncols], in_=xsrc_fn(k)),
             reads=[src_slot], writes=[xs], dma=xs)
        norm_feed(S, C, xb, xs, kc, ncols, cts, gcol, hT, hslots)
    finalize_rstd(S, C, cts, ncols, C.rstdh, C.rstdh_slot)


def proj_chunk(S, C, w_ap, nk, rhs_list, rhs_slots, cts, wring, ps_bufs, ps_slots):
    if isinstance(wring, WCache):
        w2d, wname, col = w_ap
        wv, ws = wring.get(w2d, wname, col, nk)
    else:
        wb, ws = wring.next()
        wv = wb.rearrange("p (a b) -> p a b", b=128)
        S.op("pool", lambda h: h.dma_start(out=wv[:, 0:nk, :], in_=w_ap.rearrange("(a p) n -> p a n", p=128)),
             writes=[ws], dma=ws)
    for k in range(nk):
        for i, (c0, c1) in enumerate(cts):
            S.op("pe", lambda h, k=k, i=i, c0=c0, c1=c1: h.matmul(
                ps_bufs[i][:, 0:c1 - c0], lhsT=wv[:, k, :], rhs=rhs_list[k][:, c0:c1],
                start=(k == 0), stop=(k == nk - 1)),
                reads=[ws, rhs_slots[k]], writes=[ps_slots[i]])


def wout_and_residual(S, C, w_out_ap, wname, aT, aslots, ncols, cts, gcol, xsrc_fn, src_slot, xdst_fn, dst_slot, yscr_fn, yslot,
                      next_gcol=None, hT=None, hslots=None):
    pend = None
    for m in range(KC):
        pb, pslots = C.psring.next()
        proj_chunk(S, C, (w_out_ap, wname, m * 128), KC, aT, aslots, cts, C.wring, pb, pslots)
        yb, ys = C.yring.next()
        qb, qs = C.sqring.next()
        for i, (c0, c1) in enumerate(cts):
            S.op("act", lambda h, i=i, c0=c0, c1=c1, pb=pb, yb=yb: h.activation(
                out=yb[:, c0:c1], in_=pb[i][:, 0:c1 - c0], func=AF.Copy), reads=[pslots[i]], writes=[ys])
            S.op("act", lambda h, i=i, c0=c0, c1=c1, pb=pb, qb=qb: h.activation(
                out=qb[:, c0:c1], in_=pb[i][:, 0:c1 - c0], func=AF.Square), reads=[pslots[i]], writes=[qs])
        S.op("act", lambda h, m=m, yb=yb: h.dma_start(out=yscr_fn(m), in_=yb[:, 0:ncols]),
             reads=[ys], writes=[yslot], dma=yslot)
        if pend is not None:
            pend()

        def mk(m=m, qb=qb, qs=qs):
            def f():
                for i, (c0, c1) in enumerate(cts):
                    S.op("pe", lambda h, i=i, c0=c0, c1=c1: h.matmul(
                        C.psS[i][:, 0:c1 - c0], lhsT=C.ones32, rhs=qb[:, c0:c1], start=(m == 0), stop=(m == KC - 1)),
                        reads=[qs, C.const_slot], writes=[C.psS_slot[i]])
            return f
        pend = mk()
    pend()
    finalize_rstd(S, C, cts, ncols, C.rstd, C.rstd_slot)
    for m in range(KC):
        yb, ys = C.yring.next()
        S.op("sp", lambda h, m=m, yb=yb: h.dma_start(out=yb[:, 0:ncols], in_=yscr_fn(m)),
             reads=[yslot], writes=[ys], dma=ys)
        xb, xs = C.xring.next()
        S.op("sp", lambda h, m=m, xb=xb: h.dma_start(out=xb[:, 0:ncols], in_=xsrc_fn(m)),
             reads=[src_slot], writes=[xs], dma=xs)
        S.op("dve", lambda h, m=m, yb=yb: h.scalar_tensor_tensor(
            out=yb[:, 0:ncols], in0=yb[:, 0:ncols], scalar=C.vecs[:, gcol + m:gcol + m + 1],
            in1=C.rstd[:, 0:ncols], op0=ALU.mult, op1=ALU.mult),
            reads=[ys, C.rstd_slot, C.const_slot], writes=[ys])
        S.op("dve", lambda h, yb=yb, xb=xb: h.tensor_tensor(
            out=xb[:, 0:ncols], in0=yb[:, 0:ncols], in1=xb[:, 0:ncols], op=ALU.add),
            reads=[ys, xs], writes=[xs])
        S.op("act", lambda h, m=m, xb=xb: h.dma_start(out=xdst_fn(m), in_=xb[:, 0:ncols]),
             reads=[xs], writes=[dst_slot], dma=dst_slot)
        if next_gcol is not None:
            norm_feed(S, C, xb, xs, m, ncols, cts, next_gcol, hT, hslots)
    if next_gcol is not None:
        finalize_rstd(S, C, cts, ncols, C.rstdh, C.rstdh_slot)


def alloc_psum(nc, C):
    C.banks = [nc.alloc_psum_tensor("bank%d" % i, [128, 512], F32) for i in range(8)]
    C.bank_slots = [Slot("bank%d" % i) for i in range(8)]


def build_phase_a():
    nc = bass.Bass("TRN2", target_bir_lowering=False)
    NT = NSTR * SW
    dr = {}
    dr["xT"] = nc.dram_tensor("xT", [D, NT], F32, kind="ExternalInput").ap()
    dr["a_w_in"] = nc.dram_tensor("a_w_in", [2, D, 2 * D], F32, kind="ExternalInput").ap()
    dr["a_w_group"] = nc.dram_tensor("a_w_group", [2, 4, 1024, 1024], F32, kind="ExternalInput").ap()
    dr["a_w_out"] = nc.dram_tensor("a_w_out", [2, D, D], F32, kind="ExternalInput").ap()
    dr["w_kv"] = nc.dram_tensor("w_kv", [D, 2 * D], F32, kind="ExternalInput").ap()
    dr["vecs"] = nc.dram_tensor("vecs", [128, 7 * KC], F32, kind="ExternalInput").ap()
    dr["icnt"] = nc.dram_tensor("icnt", [128, 64], F32, kind="ExternalInput").ap()
    dr["ones32"] = nc.dram_tensor("ones32", [128, 128], F32, kind="ExternalInput").ap()
    dr["epsv"] = nc.dram_tensor("epsv", [128, 1], F32, kind="ExternalInput").ap()
    x1 = nc.dram_tensor("x1s", [D, NT], F32).ap()
    x2 = nc.dram_tensor("x2T", [D, NT], F32, kind="ExternalOutput").ap()
    yscr = nc.dram_tensor("yscr", [D, PA_COLS], F32).ap()
    kto = nc.dram_tensor("KT", [NH, 128, NSTR * 128], BF16, kind="ExternalOutput").ap()
    vo = nc.dram_tensor("V", [NSTR * 128, D], BF16, kind="ExternalOutput").ap()

    S = Sched()
    C = Ctx()
    ar = Arena(nc, 206 * 1024)
    alloc_psum(nc, C)
    consts, C.const_slot = load_consts(S, ar, dr, [("vecs", 7 * KC, F32), ("icnt", 64, F32),
                                                    ("ones32", 128, F32), ("epsv", 1, F32)])
    C.vecs = consts["vecs"]
    C.ones32 = consts["ones32"]
    C.eps_ap = consts["epsv"]
    icnt = consts["icnt"]
    ncols = PA_COLS
    cts = [(0, PA_CT), (PA_CT, PA_COLS)]
    C.rstd = ar.alloc(ncols, F32, "rstd")
    C.rstd_slot = Slot("rstd")
    C.rstdh = ar.alloc(ncols, F32, "rstdh")
    C.rstdh_slot = Slot("rstdh")
    C.psS = [C.banks[6], C.banks[7]]
    C.psS_slot = [C.bank_slots[6], C.bank_slots[7]]

    class PsRing:
        def __init__(self):
            self.i = 0

        def next(self):
            b = self.i % 3
            self.i += 1
            return [C.banks[2 * b], C.banks[2 * b + 1]], [C.bank_slots[2 * b], C.bank_slots[2 * b + 1]]
    C.psring = PsRing()
    hT = [ar.alloc(ncols, BF16, "hT") for _ in range(KC)]
    hslots = [Slot("h%d" % k) for k in range(KC)]
    base = ar.mark()
    aT = [ar.alloc(ncols, BF16, "aT") for _ in range(KC)]
    aslots = [Slot("a%d" % k) for k in range(KC)]
    C.wring = WCache(S, [ar.alloc(KC * 512, BF16, "w") for _ in range(2)], "w")
    wgring = Ring([ar.alloc(8 * 128, BF16, "wg") for _ in range(2)], "wg")
    C.xring = Ring([ar.alloc(ncols, F32, "xc") for _ in range(3)], "xc")
    C.yring = Ring([ar.alloc(ncols, F32, "yc") for _ in range(3)], "yc")
    C.sqring = Ring([ar.alloc(ncols, F32, "sq") for _ in range(2)], "sq")
    uring = Ring([ar.alloc(ncols, F32, "u") for _ in range(2)], "u")
    sA = ar.alloc(ncols, F32, "sA")
    sB = ar.alloc(ncols, F32, "sB")
    sAs, sBs = Slot("sA"), Slot("sB")
    pooled = [ar.alloc(ncols, BF16, "pooled") for _ in range(8)]
    pslots_ = [Slot("pl%d" % k) for k in range(8)]
    sgring = Ring([ar.alloc(ncols, BF16, "sg") for _ in range(2)], "sg")
    gtring = Ring([ar.alloc(ncols, F32, "gt") for _ in range(2)], "gt")
    rcol = ar.alloc(4, F32, "rcol")
    rcol_slot = Slot("rcol")
    tmp16 = ar.alloc(16, F32, "tmp16")
    tmp16s = Slot("tmp16")
    for b, s in ((sA, sAs), (sB, sBs)):
        S.op("dve", lambda h, b=b: h.memset(b, 0.0), writes=[s])
    for b, s in zip(uring.bufs, uring.slots):
        S.op("dve", lambda h, b=b: h.memset(b, 0.0), writes=[s])
    for b, s in zip(pooled, pslots_):
        S.op("dve", lambda h, b=b: h.memset(b, 0.0), writes=[s])

    xin_slot, x1_slot, x2_slot, y_slot = Slot("xin"), Slot("x1"), Slot("x2"), Slot("yscr")
    kt_slot, v_slot = Slot("kt"), Slot("v")

    for p in range(2):
        cb = p * PA_COLS

        def src_of(t, cb=cb):
            return lambda k: t[k * 128:(k + 1) * 128, cb:cb + ncols]
        for l in range(2):
            xsrc, sslot = (dr["xT"], xin_slot) if l == 0 else (x1, x1_slot)
            xdst, dslot = (x1, x1_slot) if l == 0 else (x2, x2_slot)
            if l == 0:
                first_norm(S, C, src_of(xsrc), ncols, cts, 0, hT, hslots, sslot)
            w_in = dr["a_w_in"][l]
            for og in range(4):
                win = WINS[og]
                for j in range(8):
                    m = og * 8 + j
                    pb, psl = C.psring.next()
                    proj_chunk(S, C, (w_in, "a_w_in%d" % l, m * 128), KC, hT, hslots, cts, C.wring, pb, psl)
                    ub, us = uring.next()
                    for i, (c0, c1) in enumerate(cts):
                        S.op("dve", lambda h, i=i, c0=c0, c1=c1, pb=pb, ub=ub: h.tensor_tensor(
                            out=ub[:, c0:c1], in0=pb[i][:, 0:c1 - c0], in1=C.rstdh[:, c0:c1], op=ALU.mult),
                            reads=[psl[i], C.rstdh_slot], writes=[us])
                    cur, curs = ub, us
                    sh = 1
                    tgl = 0
                    while sh < win:
                        nb, nbs = (sA, sAs) if tgl == 0 else (sB, sBs)
                        S.op("dve", lambda h, cur=cur, nb=nb, sh=sh: h.tensor_tensor(
                            out=nb[:, sh:ncols], in0=cur[:, sh:ncols], in1=cur[:, 0:ncols - sh], op=ALU.add),
                            reads=[curs], writes=[nbs])
                        cur, curs = nb, nbs
                        sh *= 2
                        tgl ^= 1
                    S.op("dve", lambda h, cur=cur, ub=ub, j=j, win=win: h.scalar_tensor_tensor(
                        out=pooled[j][:, 16:ncols], in0=cur[:, 16:ncols], scalar=1.0 / win, in1=ub[:, 16:ncols],
                        op0=ALU.mult, op1=ALU.subtract), reads=[curs, us], writes=[pslots_[j]])
                    if p == 0:
                        S.op("dve", lambda h, cur=cur, og=og: h.tensor_tensor(
                            out=tmp16, in0=cur[:, HALO:HALO + 16], in1=icnt[:, og * 16:(og + 1) * 16], op=ALU.mult),
                            reads=[curs, C.const_slot], writes=[tmp16s], sync_same=True)
                        S.op("dve", lambda h, ub=ub, j=j: h.tensor_tensor(
                            out=pooled[j][:, HALO:HALO + 16], in0=tmp16, in1=ub[:, HALO:HALO + 16], op=ALU.subtract),
                            reads=[tmp16s, us], writes=[pslots_[j]], sync_same=True)
                for jo in range(8):
                    m = og * 8 + jo
                    pb, psl = C.psring.next()
                    proj_chunk(S, C, (w_in, "a_w_in%d" % l, D + m * 128), KC, hT, hslots, cts, C.wring, pb, psl)
                    gb, gs = sgring.next()
                    gt, gts = gtring.next()
                    for i, (c0, c1) in enumerate(cts):
                        S.op("dve", lambda h, i=i, c0=c0, c1=c1, pb=pb, gt=gt: h.tensor_tensor(
                            out=gt[:, c0:c1], in0=pb[i][:, 0:c1 - c0], in1=C.rstdh[:, c0:c1], op=ALU.mult),
                            reads=[psl[i], C.rstdh_slot], writes=[gts])
                    S.op("act", lambda h, gb=gb, gt=gt: h.activation(out=gb, in_=gt, func=AF.Silu),
                         reads=[gts], writes=[gs])
                    for i, (c0, c1) in []:
                        pass
                    pb2, psl2 = C.psring.next()
                    proj_chunk(S, C, dr["a_w_group"][l, og][:, jo * 128:(jo + 1) * 128], 8, pooled, pslots_, cts,
                               wgring, pb2, psl2)
                    for i, (c0, c1) in enumerate(cts):
                        S.op("dve", lambda h, i=i, c0=c0, c1=c1, pb2=pb2, gb=gb, m=m, l=l: h.scalar_tensor_tensor(
                            out=aT[m][:, c0:c1], in0=pb2[i][:, 0:c1 - c0],
                            scalar=C.vecs[:, (2 + l) * KC + m:(2 + l) * KC + m + 1], in1=gb[:, c0:c1],
                            op0=ALU.mult, op1=ALU.mult), reads=[psl2[i], gs, C.const_slot], writes=[aslots[m]])
            wout_and_residual(S, C, dr["a_w_out"][l], "a_w_out%d" % l, aT, aslots, ncols, cts, (4 + l) * KC,
                              src_of(xsrc), sslot, src_of(xdst), dslot,
                              lambda m: yscr[m * 128:(m + 1) * 128, :], y_slot,
                              next_gcol=(KC if l == 0 else 6 * KC), hT=hT, hslots=hslots)
        for s4 in range(4):
            o = s4 * SW + HALO
            S.op("pe", lambda h, o=o: h.matmul(C.psS[0][:, 0:1], lhsT=C.rstdh[:, o:o + 128], rhs=C.ones32[:, 0:1],
                                              start=True, stop=True),
                 reads=[C.rstdh_slot, C.const_slot], writes=[C.psS_slot[0]])
            S.op("act", lambda h, s4=s4: h.activation(out=rcol[:, s4:s4 + 1], in_=C.psS[0][:, 0:1], func=AF.Copy,
                                                      scale=1.0 / 128), reads=[C.psS_slot[0]], writes=[rcol_slot])
        ktring = Ring([aT[0][:, 0:512], aT[1][:, 0:512]], "ktb")
        for a_ in (0, 1):
            ktring.slots[a_] = aslots[a_]
        for hd in range(NH):
            pb, psl = C.psring.next()
            proj_chunk(S, C, (dr["w_kv"], "w_kv", hd * 128), KC, hT, hslots, cts, C.wring, pb, psl)
            kb_, ks_ = ktring.next()
            for s4 in range(4):
                i = s4 // 2
                o = (s4 % 2) * SW + HALO
                S.op("dve", lambda h, i=i, o=o, s4=s4, pb=pb, kb_=kb_: h.tensor_tensor(
                    out=kb_[:, s4 * 128:(s4 + 1) * 128], in0=pb[i][:, o:o + 128],
                    in1=C.rstdh[:, i * PA_CT + o:i * PA_CT + o + 128], op=ALU.mult),
                    reads=[psl[i], C.rstdh_slot], writes=[ks_])
            S.op("act", lambda h, hd=hd, kb_=kb_, p=p: h.dma_start(
                out=kto[hd, :, p * 512:(p + 1) * 512], in_=kb_), reads=[ks_], writes=[kt_slot], dma=kt_slot)
        m0 = ar.mark()
        ar.reset(base + 2 * ncols * 2)
        wv_bufs = [ar.alloc(16 * 512, BF16, "wv") for _ in range(2)]
        vb_bufs = [ar.alloc(512, BF16, "vb") for _ in range(2)]
        assert ar.mark() <= base + KC * ncols * 2
        ar.reset(m0)
        wvring = Ring(wv_bufs, "wv")
        vbring = Ring(vb_bufs, "vb")
        for s_ in wvring.slots + vbring.slots:
            s_.w = None
        region_slots = aslots[2:]
        for ft in range(8):
            for half in range(2):
                wb, ws = wvring.next()
                wv = wb.rearrange("p (a b) -> p a b", b=512)
                src = dr["w_kv"][half * 2048:(half + 1) * 2048, D + ft * 512:D + (ft + 1) * 512]
                S.op("pool", lambda h, wv=wv, src=src: h.dma_start(
                    out=wv, in_=src.rearrange("(a p) n -> p a n", p=128)),
                    reads=[], writes=[ws] + (region_slots if (ft == 0) else []), dma=ws)
                for s4 in range(4):
                    o = s4 * SW + HALO
                    for kk in range(16):
                        k = half * 16 + kk
                        S.op("pe", lambda h, s4=s4, o=o, k=k, kk=kk, wv=wv: h.matmul(
                            C.banks[s4][:, :], lhsT=hT[k][:, o:o + 128], rhs=wv[:, kk, :],
                            start=(k == 0), stop=(k == KC - 1)),
                            reads=[ws, hslots[k]], writes=[C.bank_slots[s4]])
            for s4 in range(4):
                vb, vs = vbring.next()
                S.op("act" if s4 % 2 == 0 else "dve",
                     (lambda h, s4=s4, vb=vb: h.activation(out=vb, in_=C.banks[s4][:, :], func=AF.Copy,
                                                           scale=rcol[:, s4:s4 + 1]))
                     if s4 % 2 == 0 else
                     (lambda h, s4=s4, vb=vb: h.tensor_scalar(out=vb, in0=C.banks[s4][:, :],
                                                              scalar1=rcol[:, s4:s4 + 1], scalar2=None, op0=ALU.mult)),
                     reads=[C.bank_slots[s4], rcol_slot],
                     writes=[vs] + (region_slots if (ft == 0 and s4 < 2) else []))
                r0 = p * 512 + s4 * 128
                S.op("act", lambda h, vb=vb, r0=r0, ft=ft: h.dma_start(
                    out=vo[r0:r0 + 128, ft * 512:(ft + 1) * 512], in_=vb), reads=[vs], writes=[v_slot], dma=v_slot)
        for s_ in region_slots:
            for r_ in wvring.slots + vbring.slots:
                if r_.w is not None:
                    s_.r.append(r_.w)
                s_.r.extend(r_.r)
    S.emit(nc, [x2_slot, kt_slot, v_slot])
    return nc


def build_phase_b():
    nc = bass.Bass("TRN2", target_bir_lowering=False)
    NT = NSTR * 128
    dr = {}
    dr["x2T"] = nc.dram_tensor("x2T", [D, NT], F32, kind="ExternalInput").ap()
    dr["KT"] = nc.dram_tensor("KT", [NH, 128, SEQ], BF16, kind="ExternalInput").ap()
    dr["Vh"] = nc.dram_tensor("Vh", [NH, 128, SEQ], BF16, kind="ExternalInput").ap()
    dr["b_w_in"] = nc.dram_tensor("b_w_in", [2, D, 2 * D], F32, kind="ExternalInput").ap()
    dr["b_w_out"] = nc.dram_tensor("b_w_out", [2, D, D], F32, kind="ExternalInput").ap()
    dr["vecs"] = nc.dram_tensor("vecs", [128, 4 * KC], F32, kind="ExternalInput").ap()
    dr["masks"] = nc.dram_tensor("masks", [128, 8 * 128], BF16, kind="ExternalInput").ap()
    dr["negU"] = nc.dram_tensor("negU", [128, 128], BF16, kind="ExternalInput").ap()
    dr["negO"] = nc.dram_tensor("negO", [128, 128], BF16, kind="ExternalInput").ap()
    dr["zer"] = nc.dram_tensor("zer", [128, 128], BF16, kind="ExternalInput").ap()
    dr["ones32"] = nc.dram_tensor("ones32", [128, 128], F32, kind="ExternalInput").ap()
    dr["epsv"] = nc.dram_tensor("epsv", [128, 1], F32, kind="ExternalInput").ap()
    x3 = nc.dram_tensor("x3s", [D, NT], F32).ap()
    outT = nc.dram_tensor("outT", [D, NT], F32, kind="ExternalOutput").ap()
    yscr = nc.dram_tensor("yscr", [D, PB_COLS], F32).ap()

    S = Sched()
    C = Ctx()
    ar = Arena(nc, 206 * 1024)
    alloc_psum(nc, C)
    consts, C.const_slot = load_consts(S, ar, dr, [
        ("vecs", 4 * KC, F32), ("masks", 8 * 128, BF16), ("negU", 128, BF16), ("negO", 128, BF16),
        ("zer", 128, BF16), ("ones32", 128, F32), ("epsv", 1, F32)])
    C.vecs = consts["vecs"]
    C.ones32 = consts["ones32"]
    C.eps_ap = consts["epsv"]
    masks, negU, negO, zer = consts["masks"], consts["negU"], consts["negO"], consts["zer"]
    ncols = PB_COLS
    cts = [(0, ncols)]
    C.rstd = ar.alloc(ncols, F32, "rstd")
    C.rstd_slot = Slot("rstd")
    C.rstdh = ar.alloc(ncols, F32, "rstdh")
    C.rstdh_slot = Slot("rstdh")
    C.psS = [C.banks[7]]
    C.psS_slot = [C.bank_slots[7]]

    class PsRing:
        def __init__(self):
            self.i = 0

        def next(self):
            if self.mode == 1:
                b = 6 + (self.i % 2)
            else:
                b = self.i % 4
            self.i += 1
            return [C.banks[b]], [C.bank_slots[b]]
    C.psring = PsRing()
    C.psring.mode = 0
    hT = [ar.alloc(ncols, BF16, "hT") for _ in range(KC)]
    hslots = [Slot("h%d" % k) for k in range(KC)]
    aT = [ar.alloc(ncols, BF16, "aT") for _ in range(KC)]
    aslots = [Slot("a%d" % k) for k in range(KC)]
    C.wring = WCache(S, [ar.alloc(KC * 512, BF16, "w") for _ in range(2)], "w")
    C.xring = Ring([ar.alloc(ncols, F32, "xc") for _ in range(3)], "xc")
    C.yring = Ring([ar.alloc(ncols, F32, "yc") for _ in range(3)], "yc")
    C.sqring = Ring([ar.alloc(ncols, F32, "sq") for _ in range(2)], "sq")
    qring = Ring([ar.alloc(ncols, BF16, "qT") for _ in range(2)], "qT")
    sgring = Ring([ar.alloc(ncols, BF16, "sg") for _ in range(2)], "sg")
    gtring = Ring([ar.alloc(ncols, F32, "gt") for _ in range(2)], "gt")
    NQ = 4
    kq_bufs = [ar.alloc(2048, BF16, "kq") for _ in range(NQ)]
    vq_bufs = [ar.alloc(2048, BF16, "vq") for _ in range(NQ)]
    kq_slots = [Slot("kq%d" % i) for i in range(NQ)]
    vq_slots = [Slot("vq%d" % i) for i in range(NQ)]
    e_buf = ar.alloc(ncols, F32, "e")
    e_slot = Slot("e")
    spring = Ring([ar.alloc(ncols, BF16, "sp") for _ in range(3)], "sp")
    pring = Ring([ar.alloc(ncols, BF16, "P") for _ in range(4)], "P")
    ssum = [ar.alloc(ncols, BF16, "ssum") for _ in range(2)]
    ssum_slots = [Slot("ssum0"), Slot("ssum1")]

    x2_slot, x3_slot, out_slot, y_slot = Slot("x2"), Slot("x3"), Slot("out"), Slot("yscr")
    qcount = [0]

    for G in range(2):
        g0 = G * ncols

        def src_of(t, g0=g0):
            return lambda k: t[k * 128:(k + 1) * 128, g0:g0 + ncols]
        for l in range(2):
            xsrc, sslot = (dr["x2T"], x2_slot) if l == 0 else (x3, x3_slot)
            xdst, dslot = (x3, x3_slot) if l == 0 else (outT, out_slot)
            if l == 0:
                first_norm(S, C, src_of(xsrc), ncols, cts, 0, hT, hslots, sslot)
            w_in = dr["b_w_in"][l]

            def head_proj_ops(hd):
                thunks = []
                qb, qs = qring.next()
                gb, gs = sgring.next()
                for which in range(2):
                    col = hd * 128 + which * D
                    pb, psl = C.psring.next()
                    wv, ws = C.wring.get(w_in, "b_w_in%d" % l, col, KC)
                    for k in range(KC):
                        def t_mm(k=k, wv=wv, ws=ws, pb=pb, psl=psl):
                            S.op("pe", lambda h: h.matmul(pb[0][:, :], lhsT=wv[:, k, :], rhs=hT[k],
                                                          start=(k == 0), stop=(k == KC - 1)),
                                 reads=[ws, hslots[k]], writes=[psl[0]])
                        thunks.append(t_mm)
                    if which == 0:
                        def t_ev(pb=pb, psl=psl, qb=qb, qs=qs):
                            S.op("dve", lambda h: h.scalar_tensor_tensor(out=qb, in0=pb[0][:, :],
                                                                         scalar=float(128 ** -0.5), in1=C.rstdh,
                                                                         op0=ALU.mult, op1=ALU.mult),
                                 reads=[psl[0], C.rstdh_slot], writes=[qs])
                    else:
                        def t_ev(pb=pb, psl=psl, gb=gb, gs=gs):
                            gt, gts = gtring.next()
                            S.op("dve", lambda h: h.tensor_tensor(out=gt, in0=pb[0][:, :], in1=C.rstdh, op=ALU.mult),
                                 reads=[psl[0], C.rstdh_slot], writes=[gts])
                            S.op("act", lambda h: h.activation(out=gb, in_=gt, func=AF.Silu),
                                 reads=[gts], writes=[gs])
                    thunks.append(t_ev)
                return thunks, qb, qs, gb, gs

            if G == 0:
                kb_list = list(range(31, -1, -1))
            else:
                kb_list = list(range(63, -1, -1))
            nsteps = len(kb_list)
            heads = []
            C.psring.mode = 1
            th0, qb0, qs0, gb0, gs0 = head_proj_ops(0)
            for t_ in th0:
                t_()
            heads.append((qb0, qs0, gb0, gs0))
            steps = []
            for hd in range(NH):
                for si, kb in enumerate(kb_list):
                    steps.append({"hd": hd, "si": si, "kb": kb})
            pending = []
            state = {}
            qlist = [(hd_, kb_ // 16) for hd_ in range(NH) for kb_ in kb_list if kb_ % 16 == 15]
            qslot_of = {}
            T = len(steps)

            def emit_qk(st):
                hd, si, kb = st["hd"], st["si"], st["kb"]
                if si == 0:
                    if hd + 1 < NH:
                        th, qb, qs, gb, gs = head_proj_ops(hd + 1)
                        pending.extend(th)
                        heads.append((qb, qs, gb, gs))
                if kb % 16 == 15:
                    qidx = state.get("qidx", -1) + 1
                    state["qidx"] = qidx
                    while state.get("emitted", 0) < min(qidx + 3, len(qlist)):
                        e_i = state.get("emitted", 0)
                        hd_, qq_ = qlist[e_i]
                        qi_ = qcount[0] % NQ
                        qcount[0] += 1
                        qslot_of[e_i] = qi_
                        S.op("sp", lambda h, hd_=hd_, qq_=qq_, qi_=qi_: h.dma_start(
                            out=kq_bufs[qi_], in_=dr["KT"][hd_, :, qq_ * 2048:(qq_ + 1) * 2048]),
                            writes=[kq_slots[qi_]], dma=kq_slots[qi_])
                        S.op("sp", lambda h, hd_=hd_, qq_=qq_, qi_=qi_: h.dma_start(
                            out=vq_bufs[qi_], in_=dr["Vh"][hd_, :, qq_ * 2048:(qq_ + 1) * 2048]),
                            writes=[vq_slots[qi_]], dma=vq_slots[qi_])
                        state["emitted"] = e_i + 1
                    state["qi"] = qslot_of[qidx]
                st["qi"] = state["qi"]
                qi = st["qi"]
                if G == 0 or kb >= 32:
                    r = (kb % 32) // 8
                    st["a0"] = 128 * r
                    st["mask"] = kb % 8
                else:
                    st["a0"] = 0
                    st["mask"] = None
                a0 = st["a0"]
                zb = st["gi"] % 4
                st["zb"] = zb
                qb, qs, gb, gs = heads[hd]
                ko = (kb % 16) * 128
                S.op("pe", lambda h: h.matmul(C.banks[zb][:, a0:ncols], lhsT=kq_bufs[qi][:, ko:ko + 128],
                                              rhs=qb[:, a0:ncols], start=True, stop=False, skip_group_check=True),
                     reads=[kq_slots[qi], qs], writes=[C.bank_slots[zb]])

            def emit_exp1(st):
                a0, zb = st["a0"], st["zb"]
                S.op("act", lambda h: h.activation(out=e_buf[:, a0:ncols], in_=C.banks[zb][:, a0:ncols], func=AF.Exp),
                     reads=[C.bank_slots[zb]], writes=[e_slot])

            def emit_ln(st):
                a0 = st["a0"]
                sb, ss = spring.next()
                st["sb"], st["ss"] = sb, ss
                S.op("act", lambda h: h.activation(out=sb[:, a0:ncols], in_=e_buf[:, a0:ncols], func=AF.Ln,
                                                   bias=1.0, scale=1.0), reads=[e_slot], writes=[ss])
                if st["mask"] is not None:
                    mk = st["mask"]
                    S.op("dve", lambda h: h.tensor_tensor(out=sb[:, a0:a0 + 128], in0=sb[:, a0:a0 + 128],
                                                          in1=masks[:, mk * 128:(mk + 1) * 128], op=ALU.mult),
                         reads=[ss, C.const_slot], writes=[ss])

            def emit_s(st):
                si = st["si"]
                a0, zb, sb, ss = st["a0"], st["zb"], st["sb"], st["ss"]
                first = (si == 0)
                S.op("pe", lambda h: h.matmul(C.banks[zb][:, a0:ncols], lhsT=negU, rhs=sb[:, a0:ncols],
                                              start=False, stop=first, skip_group_check=True),
                     reads=[ss, C.const_slot], writes=[C.bank_slots[zb]])
                cur = si % 2
                nxt = (si + 1) % 2
                if not first:
                    S.op("pe", lambda h: h.matmul(C.banks[zb][:, a0:ncols], lhsT=negO, rhs=ssum[cur][:, a0:ncols],
                                                  start=False, stop=True, skip_group_check=True),
                         reads=[ssum_slots[cur], C.const_slot], writes=[C.bank_slots[zb]])
                    if si + 1 < nsteps:
                        S.op("dve", lambda h: h.tensor_tensor(out=ssum[nxt][:, a0:ncols], in0=ssum[cur][:, a0:ncols],
                                                              in1=sb[:, a0:ncols], op=ALU.add),
                             reads=[ssum_slots[cur], ss], writes=[ssum_slots[nxt]], sync_same=True)
                else:
                    if si + 1 < nsteps:
                        S.op("dve", lambda h: h.tensor_copy(out=ssum[nxt][:, a0:ncols], in_=sb[:, a0:ncols]),
                             reads=[ss], writes=[ssum_slots[nxt]], sync_same=True)
                        if a0 > 0:
                            S.op("dve", lambda h: h.memset(ssum[nxt][:, 0:a0], 0.0), writes=[ssum_slots[nxt]],
                                 sync_same=True)
                            S.op("dve", lambda h: h.memset(ssum[cur][:, 0:a0], 0.0), writes=[ssum_slots[cur]],
                                 sync_same=True)

            def emit_expp(st):
                a0, zb = st["a0"], st["zb"]
                pb_, ps_ = pring.next()
                st["pb"], st["ps"] = pb_, ps_
                S.op("act", lambda h: h.activation(out=pb_[:, a0:ncols], in_=C.banks[zb][:, a0:ncols], func=AF.Exp),
                     reads=[C.bank_slots[zb]], writes=[ps_])
                if st["mask"] is not None:
                    mk = st["mask"]
                    S.op("dve", lambda h: h.tensor_tensor(out=pb_[:, a0:a0 + 128], in0=pb_[:, a0:a0 + 128],
                                                          in1=masks[:, mk * 128:(mk + 1) * 128], op=ALU.mult),
                         reads=[ps_, C.const_slot], writes=[ps_])

            def emit_av(st):
                hd, si, kb = st["hd"], st["si"], st["kb"]
                a0, qi = st["a0"], st["qi"]
                ob = 4 + (hd % 2)
                qb, qs, gb, gs = heads[hd]
                if si == 0:
                    S.op("pe", lambda h: h.matmul(C.banks[ob][:, :], lhsT=zer, rhs=qb, start=True, stop=False,
                                                  skip_group_check=True),
                         reads=[C.const_slot, qs], writes=[C.bank_slots[ob]])
                vo_ = (kb % 16) * 128
                pb_, ps_ = st["pb"], st["ps"]
                S.op("pe", lambda h: h.matmul(C.banks[ob][:, a0:ncols], lhsT=vq_bufs[qi][:, vo_:vo_ + 128],
                                              rhs=pb_[:, a0:ncols], start=False, stop=(si == nsteps - 1),
                                              skip_group_check=True),
                     reads=[vq_slots[qi], ps_], writes=[C.bank_slots[ob]])
                if si == nsteps - 1:
                    S.op("dve", lambda h: h.tensor_tensor(out=aT[hd], in0=C.banks[ob][:, :], in1=gb, op=ALU.mult),
                         reads=[C.bank_slots[ob], gs], writes=[aslots[hd]])

            for gi, st in enumerate(steps):
                st["gi"] = gi
            npend_per = 3 if G == 0 else 2
            C.psring.mode = 1
            emit_qk(steps[0])
            for t in range(T + 3):
                if t + 1 < T:
                    emit_qk(steps[t + 1])
                if t < T:
                    emit_exp1(steps[t])
                if 1 <= t <= T:
                    emit_s(steps[t - 1])
                if 2 <= t <= T + 1:
                    emit_expp(steps[t - 2])
                if t < T:
                    emit_ln(steps[t])
                if t >= 3:
                    emit_av(steps[t - 3])
                for _ in range(npend_per):
                    if pending:
                        pending.pop(0)()
            while pending:
                pending.pop(0)()
            C.psring.mode = 0
            wout_and_residual(S, C, dr["b_w_out"][l], "b_w_out%d" % l, aT, aslots, ncols, cts, (2 + l) * KC,
                              src_of(xsrc), sslot, src_of(xdst), dslot,
                              lambda m: yscr[m * 128:(m + 1) * 128, :], y_slot,
                              next_gcol=(KC if l == 0 else None), hT=hT, hslots=hslots)
    S.emit(nc, [out_slot])
    return nc


def _vec_cols(v):
    return np.ascontiguousarray(np.asarray(v, np.float32).reshape(KC, 128).T)


_CACHE = {}


def kernel(x, a_pre_norm, a_w_in, a_w_group, a_scale, a_w_out, a_post_norm,
           kv_norm, w_kv, b_pre_norm, b_w_in, b_w_out, b_post_norm):
    bf = ml_dtypes.bfloat16
    x = np.asarray(x, np.float32)[0]
    xT_full = np.ascontiguousarray(x.T)
    a_w_in = np.asarray(a_w_in, np.float32)
    a_w_group = np.asarray(a_w_group, np.float32)
    a_w_out = np.asarray(a_w_out, np.float32)
    w_kv = np.asarray(w_kv, np.float32)
    b_w_in = np.asarray(b_w_in, np.float32)
    b_w_out = np.asarray(b_w_out, np.float32)
    ones32 = np.ones((128, 128), np.float32)
    epsv = np.full((128, 1), EPS, np.float32)
    vecsA = np.concatenate([_vec_cols(a_pre_norm[0]), _vec_cols(a_pre_norm[1]), _vec_cols(a_scale[0]),
                            _vec_cols(a_scale[1]), _vec_cols(a_post_norm[0]), _vec_cols(a_post_norm[1]),
                            _vec_cols(kv_norm)], axis=1)
    in_maps = []
    for c in range(NCORE):
        xT = np.zeros((D, NSTR * SW), np.float32)
        for k in range(NSTR):
            b = 8 * k + c
            lo = 128 * b - HALO
            if lo >= 0:
                xT[:, k * SW:(k + 1) * SW] = xT_full[:, lo:lo + SW]
            else:
                xT[:, k * SW + HALO:(k + 1) * SW] = xT_full[:, 0:128]
        icnt = np.zeros((128, 64), np.float32)
        for g, w in enumerate(WINS):
            pos = 128 * c + np.arange(16)
            icnt[:, g * 16:(g + 1) * 16] = (1.0 / np.minimum(pos + 1, w)).astype(np.float32)[None, :]
        in_maps.append({"xT": xT, "a_w_in": a_w_in, "a_w_group": a_w_group, "a_w_out": a_w_out, "w_kv": w_kv,
                        "vecs": vecsA, "icnt": icnt, "ones32": ones32, "epsv": epsv})
    if "A" not in _CACHE:
        _CACHE["A"] = build_phase_a()
    resA = run_bass_kernel_spmd(_CACHE["A"], in_maps, core_ids=list(range(NCORE))).results
    KT = np.zeros((NH, 128, SEQ), bf)
    Vh = np.zeros((NH, 128, SEQ), bf)
    x2_list = []
    for c in range(NCORE):
        kt = np.asarray(resA[c]["KT"])
        v = np.asarray(resA[c]["V"])
        x2 = np.asarray(resA[c]["x2T"])
        own = np.concatenate([np.arange(k * SW + HALO, (k + 1) * SW) for k in range(NSTR)])
        x2_list.append(np.ascontiguousarray(x2[:, own]))
        for k in range(NSTR):
            b = 8 * k + c
            KT[:, :, 128 * b:128 * b + 128] = kt[:, :, 128 * k:128 * k + 128]
            vb = v[128 * k:128 * k + 128, :].reshape(128, NH, 128)
            Vh[:, :, 128 * b:128 * b + 128] = vb.transpose(1, 0, 2)
    vecsB = np.concatenate([_vec_cols(b_pre_norm[0]), _vec_cols(b_pre_norm[1]),
                            _vec_cols(b_post_norm[0]), _vec_cols(b_post_norm[1])], axis=1)
    jj = np.arange(128)
    tri = (jj[:, None] < jj[None, :]).astype(np.float32)
    negU = (-(jj[:, None] >= jj[None, :]).astype(np.float32)).astype(bf)
    negO = (-np.ones((128, 128), np.float32)).astype(bf)
    zer = np.zeros((128, 128), bf)
    in_maps = []
    for c in range(NCORE):
        m = np.zeros((128, 8, 128), np.float32)
        for r in range(8):
            if r < c:
                m[:, r, :] = 1.0
            elif r == c:
                m[:, r, :] = tri
        in_maps.append({"x2T": x2_list[c], "KT": KT, "Vh": Vh, "b_w_in": b_w_in, "b_w_out": b_w_out,
                        "vecs": vecsB, "masks": m.reshape(128, 1024).astype(bf), "negU": negU, "negO": negO,
                        "zer": zer, "ones32": ones32, "epsv": epsv})
    if "B" not in _CACHE:
        _CACHE["B"] = build_phase_b()
    resB = run_bass_kernel_spmd(_CACHE["B"], in_maps, core_ids=list(range(NCORE))).results
    out = np.zeros((SEQ, D), np.float32)
    for c in range(NCORE):
        oT = np.asarray(resB[c]["outT"])
        for k in range(NSTR):
            b = 8 * k + c
            out[128 * b:128 * b + 128, :] = oT[:, 128 * k:128 * k + 128].T
    return out[None]
```

```python
import contextlib
import numpy as np
import ml_dtypes
import concourse.bass as bass
import concourse.mybir as mybir
from concourse.bass_utils import run_bass_kernel_spmd

F32 = mybir.dt.float32
BF16 = mybir.dt.bfloat16
AF = mybir.ActivationFunctionType
ALU = mybir.AluOpType

D = 4096
KC = D // 128
SEQ = 8192
NCORE = 8
NH = 32
EPS = 1e-6
HALO = 32
SW = 128 + HALO
NSTR = 8
PA_COLS = 4 * SW
PA_CT = PA_COLS // 2
PB_COLS = 512
WINS = (2, 4, 8, 16)
ENGS = ("pe", "act", "dve", "pool", "sp")


class Slot:
    __slots__ = ("name", "w", "r", "sem", "ndma")

    def __init__(self, name):
        self.name = name
        self.w = None
        self.r = []
        self.sem = None
        self.ndma = 0


class Sched:
    def __init__(self):
        self.ops = {e: [] for e in ENGS}
        self.seen_c = {e: {} for e in ENGS}
        self.seen_d = {e: {} for e in ENGS}
        self.dma_slots = []

    def _need(self, eng, tok, is_dma):
        if tok is None:
            return False
        if tok[0] == "c":
            _, se, idx = tok
            if se == eng and not is_dma and not self.sync_same:
                return False
            if self.seen_c[eng].get(se, -1) >= idx:
                return False
            self.seen_c[eng][se] = idx
            return True
        _, sl, cnt = tok
        if self.seen_d[eng].get(sl, 0) >= cnt:
            return False
        self.seen_d[eng][sl] = cnt
        return True

    def op(self, eng, fn, reads=(), writes=(), dma=None, sync_same=False):
        is_dma = dma is not None
        self.sync_same = sync_same
        waits = []
        for s in reads:
            if self._need(eng, s.w, is_dma):
                waits.append(s.w)
        for s in writes:
            if self._need(eng, s.w, is_dma):
                waits.append(s.w)
            for t in s.r:
                if self._need(eng, t, is_dma):
                    waits.append(t)
        idx = len(self.ops[eng])
        if is_dma:
            if dma.sem is None:
                dma.sem = True
                self.dma_slots.append(dma)
            dma.ndma += 1
            tok = ("d", dma, dma.ndma)
        else:
            tok = ("c", eng, idx)
        self.ops[eng].append({"fn": fn, "waits": waits, "dma": dma, "inc": False})
        for s in reads:
            s.r.append(tok)
        for s in writes:
            s.w = tok
            s.r = []
        return tok

    def emit(self, nc, final_slots):
        for e in ENGS:
            for o in self.ops[e]:
                for t in o["waits"]:
                    if t[0] == "c":
                        self.ops[t[1]][t[2]]["inc"] = True
        cum = {}
        for e in ENGS:
            c = 0
            arr = []
            for o in self.ops[e]:
                if o["inc"]:
                    c += 1
                arr.append(c)
            cum[e] = arr
        with contextlib.ExitStack() as st:
            esem = {e: st.enter_context(nc.semaphore("e_" + e)) for e in ENGS}
            for i, sl in enumerate(self.dma_slots):
                sl.sem = st.enter_context(nc.semaphore("d%d" % i))
            block = st.enter_context(nc.Block())

            def run(e, h):
                for o in self.ops[e]:
                    for t in o["waits"]:
                        if t[0] == "c":
                            h.wait_ge(esem[t[1]], cum[t[1]][t[2]])
                        else:
                            h.wait_ge(t[1].sem, 16 * t[2])
                    ins = o["fn"](h)
                    if o["dma"] is not None:
                        ins.then_inc(o["dma"].sem, 16)
                    elif o["inc"]:
                        ins.then_inc(esem[e], 1)
                if e == "sp":
                    for sl in final_slots:
                        if sl.ndma:
                            h.wait_ge(sl.sem, 16 * sl.ndma)

            @block.tensor
            def _(h):
                run("pe", h)

            @block.scalar
            def _(h):
                run("act", h)

            @block.vector
            def _(h):
                run("dve", h)

            @block.gpsimd
            def _(h):
                run("pool", h)

            @block.sync
            def _(h):
                run("sp", h)


class Arena:
    def __init__(self, nc, nbytes):
        self.t = nc.alloc_sbuf_tensor("arena", [128, nbytes // 4], F32)
        self.off = 0
        self.cap = nbytes

    def mark(self):
        return self.off

    def reset(self, m):
        self.off = m

    def alloc(self, n, dt, name=None):
        nb = n * (2 if dt == BF16 else 4)
        nb = (nb + 63) // 64 * 64
        assert self.off + nb <= self.cap, ("SBUF overflow", name, self.off, nb)
        a = self.t[:, self.off // 4:(self.off + nb) // 4]
        self.off += nb
        if dt == BF16:
            a = a.bitcast(BF16)
        return a[:, 0:n]


class Ring:
    def __init__(self, bufs, name):
        self.bufs = bufs
        self.slots = [Slot("%s%d" % (name, i)) for i in range(len(bufs))]
        self.i = 0

    def next(self):
        k = self.i % len(self.bufs)
        self.i += 1
        return self.bufs[k], self.slots[k]


class WCache:
    def __init__(self, S, bufs, name):
        self.S = S
        self.ring = Ring(bufs, name)
        self.keys = {}

    def get(self, w2d, wname, col, nk):
        g = col // 512
        key = (wname, g)
        if key not in self.keys:
            wb, ws = self.ring.next()
            for k_ in [k_ for k_, v_ in self.keys.items() if v_[1] is ws]:
                del self.keys[k_]
            wv = wb.rearrange("p (a b) -> p a b", b=512)
            src = w2d[:, g * 512:(g + 1) * 512].rearrange("(a p) n -> p a n", p=128)
            self.S.op("pool", lambda h: h.dma_start(out=wv[:, 0:nk, :], in_=src), writes=[ws], dma=ws)
            self.keys[key] = (wv, ws)
        wv, ws = self.keys[key]
        o = col - g * 512
        return wv[:, :, o:o + 128], ws


def load_consts(S, ar, dram, names):
    out = {}
    sl = Slot("consts")
    for key, n, dt in names:
        buf = ar.alloc(n, dt, key)
        out[key] = buf
        S.op("sp", lambda h, b=buf, k=key: h.dma_start(out=b, in_=dram[k]), writes=[sl], dma=sl)
    return out, sl


class Ctx:
    pass


def finalize_rstd(S, C, cts, ncols, dst, dst_slot):
    for i, (c0, c1) in enumerate(cts):
        S.op("act", lambda h, i=i, c0=c0, c1=c1: h.activation(
            out=dst[:, c0:c1], in_=C.psS[i][:, 0:c1 - c0], func=AF.Sqrt, bias=C.eps_ap, scale=1.0 / D),
            reads=[C.psS_slot[i], C.const_slot], writes=[dst_slot])
    S.op("dve", lambda h: h.reciprocal(out=dst[:, 0:ncols], in_=dst[:, 0:ncols]),
         reads=[dst_slot], writes=[dst_slot])


def norm_feed(S, C, xb, xs, kc, ncols, cts, gcol, hT, hslots):
    qb, qs = C.sqring.next()
    S.op("act", lambda h: h.activation(out=qb[:, 0:ncols], in_=xb[:, 0:ncols], func=AF.Square),
         reads=[xs], writes=[qs])
    for i, (c0, c1) in enumerate(cts):
        S.op("pe", lambda h, i=i, c0=c0, c1=c1: h.matmul(
            C.psS[i][:, 0:c1 - c0], lhsT=C.ones32, rhs=qb[:, c0:c1], start=(kc == 0), stop=(kc == KC - 1)),
            reads=[qs, C.const_slot], writes=[C.psS_slot[i]])
    S.op("dve", lambda h: h.tensor_scalar(out=hT[kc][:, 0:ncols], in0=xb[:, 0:ncols],
                                          scalar1=C.vecs[:, gcol + kc:gcol + kc + 1], scalar2=None, op0=ALU.mult),
         reads=[xs, C.const_slot], writes=[hslots[kc]])


def first_norm(S, C, xsrc_fn, ncols, cts, gcol, hT, hslots, src_slot):
    for kc in range(KC):
        xb, xs = C.xring.next()
        S.op("sp", lambda h, b=xb, k=kc: h.dma_start(out=b[:, 0:ncols], in_=xsrc_fn(k)),
             reads=[src_slot], writes=[xs], dma=xs)
        norm_feed(S, C, xb, xs, kc, ncols, cts, gcol, hT, hslots)
    finalize_rstd(S, C, cts, ncols, C.rstdh, C.rstdh_slot)


def proj_chunk(S, C, w_ap, nk, rhs_list, rhs_slots, cts, wring, ps_bufs, ps_slots):
    if isinstance(wring, WCache):
        w2d, wname, col = w_ap
        wv, ws = wring.get(w2d, wname, col, nk)
    else:
        wb, ws = wring.next()
        wv = wb.rearrange("p (a b) -> p a b", b=128)
        S.op("pool", lambda h: h.dma_start(out=wv[:, 0:nk, :], in_=w_ap.rearrange("(a p) n -> p a n", p=128)),
             writes=[ws], dma=ws)
    for k in range(nk):
        for i, (c0, c1) in enumerate(cts):
            S.op("pe", lambda h, k=k, i=i, c0=c0, c1=c1: h.matmul(
                ps_bufs[i][:, 0:c1 - c0], lhsT=wv[:, k, :], rhs=rhs_list[k][:, c0:c1],
                start=(k == 0), stop=(k == nk - 1)),
                reads=[ws, rhs_slots[k]], writes=[ps_slots[i]])


def wout_and_residual(S, C, w_out_ap, wname, aT, aslots, ncols, cts, gcol, xsrc_fn, src_slot, xdst_fn, dst_slot, yscr_fn, yslot,
                      next_gcol=None, hT=None, hslots=None):
    pend = None
    for m in range(KC):
        pb, pslots = C.psring.next()
        proj_chunk(S, C, (w_out_ap, wname, m * 128), KC, aT, aslots, cts, C.wring, pb, pslots)
        yb, ys = C.yring.next()
        qb, qs = C.sqring.next()
        for i, (c0, c1) in enumerate(cts):
            S.op("act", lambda h, i=i, c0=c0, c1=c1, pb=pb, yb=yb: h.activation(
                out=yb[:, c0:c1], in_=pb[i][:, 0:c1 - c0], func=AF.Copy), reads=[pslots[i]], writes=[ys])
            S.op("act", lambda h, i=i, c0=c0, c1=c1, pb=pb, qb=qb: h.activation(
                out=qb[:, c0:c1], in_=pb[i][:, 0:c1 - c0], func=AF.Square), reads=[pslots[i]], writes=[qs])
        S.op("act", lambda h, m=m, yb=yb: h.dma_start(out=yscr_fn(m), in_=yb[:, 0:ncols]),
             reads=[ys], writes=[yslot], dma=yslot)
        if pend is not None:
            pend()

        def mk(m=m, qb=qb, qs=qs):
            def f():
                for i, (c0, c1) in enumerate(cts):
                    S.op("pe", lambda h, i=i, c0=c0, c1=c1: h.matmul(
                        C.psS[i][:, 0:c1 - c0], lhsT=C.ones32, rhs=qb[:, c0:c1], start=(m == 0), stop=(m == KC - 1)),
                        reads=[qs, C.const_slot], writes=[C.psS_slot[i]])
            return f
        pend = mk()
    pend()
    finalize_rstd(S, C, cts, ncols, C.rstd, C.rstd_slot)
    for m in range(KC):
        yb, ys = C.yring.next()
        S.op("sp", lambda h, m=m, yb=yb: h.dma_start(out=yb[:, 0:ncols], in_=yscr_fn(m)),
             reads=[yslot], writes=[ys], dma=ys)
        xb, xs = C.xring.next()
        S.op("sp", lambda h, m=m, xb=xb: h.dma_start(out=xb[:, 0:ncols], in_=xsrc_fn(m)),
             reads=[src_slot], writes=[xs], dma=xs)
        S.op("dve", lambda h, m=m, yb=yb: h.scalar_tensor_tensor(
            out=yb[:, 0:ncols], in0=yb[:, 0:ncols], scalar=C.vecs[:, gcol + m:gcol + m + 1],
            in1=C.rstd[:, 0:ncols], op0=ALU.mult, op1=ALU.mult),
            reads=[ys, C.rstd_slot, C.const_slot], writes=[ys])
        S.op("dve", lambda h, yb=yb, xb=xb: h.tensor_tensor(
            out=xb[:, 0:ncols], in0=yb[:, 0:ncols], in1=xb[:, 0:ncols], op=ALU.add),
            reads=[ys, xs], writes=[xs])
        S.op("act", lambda h, m=m, xb=xb: h.dma_start(out=xdst_fn(m), in_=xb[:, 0:ncols]),
             reads=[xs], writes=[dst_slot], dma=dst_slot)
        if next_gcol is not None:
            norm_feed(S, C, xb, xs, m, ncols, cts, next_gcol, hT, hslots)
    if next_gcol is not None:
        finalize_rstd(S, C, cts, ncols, C.rstdh, C.rstdh_slot)


def alloc_psum(nc, C):
    C.banks = [nc.alloc_psum_tensor("bank%d" % i, [128, 512], F32) for i in range(8)]
    C.bank_slots = [Slot("bank%d" % i) for i in range(8)]


def build_phase_a():
    nc = bass.Bass("TRN2", target_bir_lowering=False)
    NT = NSTR * SW
    dr = {}
    dr["xT"] = nc.dram_tensor("xT", [D, NT], F32, kind="ExternalInput").ap()
    dr["a_w_in"] = nc.dram_tensor("a_w_in", [2, D, 2 * D], F32, kind="ExternalInput").ap()
    dr["a_w_group"] = nc.dram_tensor("a_w_group", [2, 4, 1024, 1024], F32, kind="ExternalInput").ap()
    dr["a_w_out"] = nc.dram_tensor("a_w_out", [2, D, D], F32, kind="ExternalInput").ap()
    dr["w_kv"] = nc.dram_tensor("w_kv", [D, 2 * D], F32, kind="ExternalInput").ap()
    dr["vecs"] = nc.dram_tensor("vecs", [128, 7 * KC], F32, kind="ExternalInput").ap()
    dr["icnt"] = nc.dram_tensor("icnt", [128, 64], F32, kind="ExternalInput").ap()
    dr["ones32"] = nc.dram_tensor("ones32", [128, 128], F32, kind="ExternalInput").ap()
    dr["epsv"] = nc.dram_tensor("epsv", [128, 1], F32, kind="ExternalInput").ap()
    x1 = nc.dram_tensor("x1s", [D, NT], F32).ap()
    x2 = nc.dram_tensor("x2T", [D, NT], F32, kind="ExternalOutput").ap()
    yscr = nc.dram_tensor("yscr", [D, PA_COLS], F32).ap()
    kto = nc.dram_tensor("KT", [NH, 128, NSTR * 128], BF16, kind="ExternalOutput").ap()
    vo = nc.dram_tensor("V", [NSTR * 128, D], BF16, kind="ExternalOutput").ap()

    S = Sched()
    C = Ctx()
    ar = Arena(nc, 206 * 1024)
    alloc_psum(nc, C)
    consts, C.const_slot = load_consts(S, ar, dr, [("vecs", 7 * KC, F32), ("icnt", 64, F32),
                                                    ("ones32", 128, F32), ("epsv", 1, F32)])
    C.vecs = consts["vecs"]
    C.ones32 = consts["ones32"]
    C.eps_ap = consts["epsv"]
    icnt = consts["icnt"]
    ncols = PA_COLS
    cts = [(0, PA_CT), (PA_CT, PA_COLS)]
    C.rstd = ar.alloc(ncols, F32, "rstd")
    C.rstd_slot = Slot("rstd")
    C.rstdh = ar.alloc(ncols, F32, "rstdh")
    C.rstdh_slot = Slot("rstdh")
    C.psS = [C.banks[6], C.banks[7]]
    C.psS_slot = [C.bank_slots[6], C.bank_slots[7]]

    class PsRing:
        def __init__(self):
            self.i = 0

        def next(self):
            b = self.i % 3
            self.i += 1
            return [C.banks[2 * b], C.banks[2 * b + 1]], [C.bank_slots[2 * b], C.bank_slots[2 * b + 1]]
    C.psring = PsRing()
    hT = [ar.alloc(ncols, BF16, "hT") for _ in range(KC)]
    hslots = [Slot("h%d" % k) for k in range(KC)]
    base = ar.mark()
    aT = [ar.alloc(ncols, BF16, "aT") for _ in range(KC)]
    aslots = [Slot("a%d" % k) for k in range(KC)]
    C.wring = WCache(S, [ar.alloc(KC * 512, BF16, "w") for _ in range(2)], "w")
    wgring = Ring([ar.alloc(8 * 128, BF16, "wg") for _ in range(2)], "wg")
    C.xring = Ring([ar.alloc(ncols, F32, "xc") for _ in range(3)], "xc")
    C.yring = Ring([ar.alloc(ncols, F32, "yc") for _ in range(3)], "yc")
    C.sqring = Ring([ar.alloc(ncols, F32, "sq") for _ in range(2)], "sq")
    uring = Ring([ar.alloc(ncols, F32, "u") for _ in range(2)], "u")
    sA = ar.alloc(ncols, F32, "sA")
    sB = ar.alloc(ncols, F32, "sB")
    sAs, sBs = Slot("sA"), Slot("sB")
    pooled = [ar.alloc(ncols, BF16, "pooled") for _ in range(8)]
    pslots_ = [Slot("pl%d" % k) for k in range(8)]
    sgring = Ring([ar.alloc(ncols, BF16, "sg") for _ in range(2)], "sg")
    gtring = Ring([ar.alloc(ncols, F32, "gt") for _ in range(2)], "gt")
    rcol = ar.alloc(4, F32, "rcol")
    rcol_slot = Slot("rcol")
    tmp16 = ar.alloc(16, F32, "tmp16")
    tmp16s = Slot("tmp16")
    for b, s in ((sA, sAs), (sB, sBs)):
        S.op("dve", lambda h, b=b: h.memset(b, 0.0), writes=[s])
    for b, s in zip(uring.bufs, uring.slots):
        S.op("dve", lambda h, b=b: h.memset(b, 0.0), writes=[s])
    for b, s in zip(pooled, pslots_):
        S.op("dve", lambda h, b=b: h.memset(b, 0.0), writes=[s])

    xin_slot, x1_slot, x2_slot, y_slot = Slot("xin"), Slot("x1"), Slot("x2"), Slot("yscr")
    kt_slot, v_slot = Slot("kt"), Slot("v")

    for p in range(2):
        cb = p * PA_COLS

        def src_of(t, cb=cb):
            return lambda k: t[k * 128:(k + 1) * 128, cb:cb + ncols]
        for l in range(2):
            xsrc, sslot = (dr["xT"], xin_slot) if l == 0 else (x1, x1_slot)
            xdst, dslot = (x1, x1_slot) if l == 0 else (x2, x2_slot)
            if l == 0:
                first_norm(S, C, src_of(xsrc), ncols, cts, 0, hT, hslots, sslot)
            w_in = dr["a_w_in"][l]
            for og in range(4):
                win = WINS[og]
                for j in range(8):
                    m = og * 8 + j
                    pb, psl = C.psring.next()
                    proj_chunk(S, C, (w_in, "a_w_in%d" % l, m * 128), KC, hT, hslots, cts, C.wring, pb, psl)
                    ub, us = uring.next()
                    for i, (c0, c1) in enumerate(cts):
                        S.op("dve", lambda h, i=i, c0=c0, c1=c1, pb=pb, ub=ub: h.tensor_tensor(
                            out=ub[:, c0:c1], in0=pb[i][:, 0:c1 - c0], in1=C.rstdh[:, c0:c1], op=ALU.mult),
                            reads=[psl[i], C.rstdh_slot], writes=[us])
                    cur, curs = ub, us
                    sh = 1
                    tgl = 0
                    while sh < win:
                        nb, nbs = (sA, sAs) if tgl == 0 else (sB, sBs)
                        S.op("dve", lambda h, cur=cur, nb=nb, sh=sh: h.tensor_tensor(
                            out=nb[:, sh:ncols], in0=cur[:, sh:ncols], in1=cur[:, 0:ncols - sh], op=ALU.add),
                            reads=[curs], writes=[nbs])
                        cur, curs = nb, nbs
                        sh *= 2
                        tgl ^= 1
                    S.op("dve", lambda h, cur=cur, ub=ub, j=j, win=win: h.scalar_tensor_tensor(
                        out=pooled[j][:, 16:ncols], in0=cur[:, 16:ncols], scalar=1.0 / win, in1=ub[:, 16:ncols],
                        op0=ALU.mult, op1=ALU.subtract), reads=[curs, us], writes=[pslots_[j]])
                    if p == 0:
                        S.op("dve", lambda h, cur=cur, og=og: h.tensor_tensor(
                            out=tmp16, in0=cur[:, HALO:HALO + 16], in1=icnt[:, og * 16:(og + 1) * 16], op=ALU.mult),
                            reads=[curs, C.const_slot], writes=[tmp16s], sync_same=True)
                        S.op("dve", lambda h, ub=ub, j=j: h.tensor_tensor(
                            out=pooled[j][:, HALO:HALO + 16], in0=tmp16, in1=ub[:, HALO:HALO + 16], op=ALU.subtract),
                            reads=[tmp16s, us], writes=[pslots_[j]], sync_same=True)
                for jo in range(8):
                    m = og * 8 + jo
                    pb, psl = C.psring.next()
                    proj_chunk(S, C, (w_in, "a_w_in%d" % l, D + m * 128), KC, hT, hslots, cts, C.wring, pb, psl)
                    gb, gs = sgring.next()
                    gt, gts = gtring.next()
                    for i, (c0, c1) in enumerate(cts):
                        S.op("dve", lambda h, i=i, c0=c0, c1=c1, pb=pb, gt=gt: h.tensor_tensor(
                            out=gt[:, c0:c1], in0=pb[i][:, 0:c1 - c0], in1=C.rstdh[:, c0:c1], op=ALU.mult),
                            reads=[psl[i], C.rstdh_slot], writes=[gts])
                    S.op("act", lambda h, gb=gb, gt=gt: h.activation(out=gb, in_=gt, func=AF.Silu),
                         reads=[gts], writes=[gs])
                    for i, (c0, c1) in []:
                        pass
                    pb2, psl2 = C.psring.next()
                    proj_chunk(S, C, dr["a_w_group"][l, og][:, jo * 128:(jo + 1) * 128], 8, pooled, pslots_, cts,
                               wgring, pb2, psl2)
                    for i, (c0, c1) in enumerate(cts):
                        S.op("dve", lambda h, i=i, c0=c0, c1=c1, pb2=pb2, gb=gb, m=m, l=l: h.scalar_tensor_tensor(
                            out=aT[m][:, c0:c1], in0=pb2[i][:, 0:c1 - c0],
                            scalar=C.vecs[:, (2 + l) * KC + m:(2 + l) * KC + m + 1], in1=gb[:, c0:c1],
                            op0=ALU.mult, op1=ALU.mult), reads=[psl2[i], gs, C.const_slot], writes=[aslots[m]])
            wout_and_residual(S, C, dr["a_w_out"][l], "a_w_out%d" % l, aT, aslots, ncols, cts, (4 + l) * KC,
                              src_of(xsrc), sslot, src_of(xdst), dslot,
                              lambda m: yscr[m * 128:(m + 1) * 128, :], y_slot,
                              next_gcol=(KC if l == 0 else 6 * KC), hT=hT, hslots=hslots)
        for s4 in range(4):
            o = s4 * SW + HALO
            S.op("pe", lambda h, o=o: h.matmul(C.psS[0][:, 0:1], lhsT=C.rstdh[:, o:o + 128], rhs=C.ones32[:, 0:1],
                                              start=True, stop=True),
                 reads=[C.rstdh_slot, C.const_slot], writes=[C.psS_slot[0]])
            S.op("act", lambda h, s4=s4: h.activation(out=rcol[:, s4:s4 + 1], in_=C.psS[0][:, 0:1], func=AF.Copy,
                                                      scale=1.0 / 128), reads=[C.psS_slot[0]], writes=[rcol_slot])
        ktring = Ring([aT[0][:, 0:512], aT[1][:, 0:512]], "ktb")
        for a_ in (0, 1):
            ktring.slots[a_] = aslots[a_]
        for hd in range(NH):
            pb, psl = C.psring.next()
            proj_chunk(S, C, (dr["w_kv"], "w_kv", hd * 128), KC, hT, hslots, cts, C.wring, pb, psl)
            kb_, ks_ = ktring.next()
            for s4 in range(4):
                i = s4 // 2
                o = (s4 % 2) * SW + HALO
                S.op("dve", lambda h, i=i, o=o, s4=s4, pb=pb, kb_=kb_: h.tensor_tensor(
                    out=kb_[:, s4 * 128:(s4 + 1) * 128], in0=pb[i][:, o:o + 128],
                    in1=C.rstdh[:, i * PA_CT + o:i * PA_CT + o + 128], op=ALU.mult),
                    reads=[psl[i], C.rstdh_slot], writes=[ks_])
            S.op("act", lambda h, hd=hd, kb_=kb_, p=p: h.dma_start(
                out=kto[hd, :, p * 512:(p + 1) * 512], in_=kb_), reads=[ks_], writes=[kt_slot], dma=kt_slot)
        m0 = ar.mark()
        ar.reset(base + 2 * ncols * 2)
        wv_bufs = [ar.alloc(16 * 512, BF16, "wv") for _ in range(2)]
        vb_bufs = [ar.alloc(512, BF16, "vb") for _ in range(2)]
        assert ar.mark() <= base + KC * ncols * 2
        ar.reset(m0)
        wvring = Ring(wv_bufs, "wv")
        vbring = Ring(vb_bufs, "vb")
        for s_ in wvring.slots + vbring.slots:
            s_.w = None
        region_slots = aslots[2:]
        for ft in range(8):
            for half in range(2):
                wb, ws = wvring.next()
                wv = wb.rearrange("p (a b) -> p a b", b=512)
                src = dr["w_kv"][half * 2048:(half + 1) * 2048, D + ft * 512:D + (ft + 1) * 512]
                S.op("pool", lambda h, wv=wv, src=src: h.dma_start(
                    out=wv, in_=src.rearrange("(a p) n -> p a n", p=128)),
                    reads=[], writes=[ws] + (region_slots if (ft == 0) else []), dma=ws)
                for s4 in range(4):
                    o = s4 * SW + HALO
                    for kk in range(16):
                        k = half * 16 + kk
                        S.op("pe", lambda h, s4=s4, o=o, k=k, kk=kk, wv=wv: h.matmul(
                            C.banks[s4][:, :], lhsT=hT[k][:, o:o + 128], rhs=wv[:, kk, :],
                            start=(k == 0), stop=(k == KC - 1)),
                            reads=[ws, hslots[k]], writes=[C.bank_slots[s4]])
            for s4 in range(4):
                vb, vs = vbring.next()
                S.op("act" if s4 % 2 == 0 else "dve",
                     (lambda h, s4=s4, vb=vb: h.activation(out=vb, in_=C.banks[s4][:, :], func=AF.Copy,
                                                           scale=rcol[:, s4:s4 + 1]))
                     if s4 % 2 == 0 else
                     (lambda h, s4=s4, vb=vb: h.tensor_scalar(out=vb, in0=C.banks[s4][:, :],
                                                              scalar1=rcol[:, s4:s4 + 1], scalar2=None, op0=ALU.mult)),
                     reads=[C.bank_slots[s4], rcol_slot],
                     writes=[vs] + (region_slots if (ft == 0 and s4 < 2) else []))
                r0 = p * 512 + s4 * 128
                S.op("act", lambda h, vb=vb, r0=r0, ft=ft: h.dma_start(
                    out=vo[r0:r0 + 128, ft * 512:(ft + 1) * 512], in_=vb), reads=[vs], writes=[v_slot], dma=v_slot)
        for s_ in region_slots:
            for r_ in wvring.slots + vbring.slots:
                if r_.w is not None:
                    s_.r.append(r_.w)
                s_.r.extend(r_.r)
    S.emit(nc, [x2_slot, kt_slot, v_slot])
    return nc


def build_phase_b():
    nc = bass.Bass("TRN2", target_bir_lowering=False)
    NT = NSTR * 128
    dr = {}
    dr["x2T"] = nc.dram_tensor("x2T", [D, NT], F32, kind="ExternalInput").ap()
    dr["KT"] = nc.dram_tensor("KT", [NH, 128, SEQ], BF16, kind="ExternalInput").ap()
    dr["Vh"] = nc.dram_tensor("Vh", [NH, 128, SEQ], BF16, kind="ExternalInput").ap()
    dr["b_w_in"] = nc.dram_tensor("b_w_in", [2, D, 2 * D], F32, kind="ExternalInput").ap()
    dr["b_w_out"] = nc.dram_tensor("b_w_out", [2, D, D], F32, kind="ExternalInput").ap()
    dr["vecs"] = nc.dram_tensor("vecs", [128, 4 * KC], F32, kind="ExternalInput").ap()
    dr["masks"] = nc.dram_tensor("masks", [128, 8 * 128], BF16, kind="ExternalInput").ap()
    dr["negU"] = nc.dram_tensor("negU", [128, 128], BF16, kind="ExternalInput").ap()
    dr["negO"] = nc.dram_tensor("negO", [128, 128], BF16, kind="ExternalInput").ap()
    dr["zer"] = nc.dram_tensor("zer", [128, 128], BF16, kind="ExternalInput").ap()
    dr["ones32"] = nc.dram_tensor("ones32", [128, 128], F32, kind="ExternalInput").ap()
    dr["epsv"] = nc.dram_tensor("epsv", [128, 1], F32, kind="ExternalInput").ap()
    x3 = nc.dram_tensor("x3s", [D, NT], F32).ap()
    outT = nc.dram_tensor("outT", [D, NT], F32, kind="ExternalOutput").ap()
    yscr = nc.dram_tensor("yscr", [D, PB_COLS], F32).ap()

    S = Sched()
    C = Ctx()
    ar = Arena(nc, 206 * 1024)
    alloc_psum(nc, C)
    consts, C.const_slot = load_consts(S, ar, dr, [
        ("vecs", 4 * KC, F32), ("masks", 8 * 128, BF16), ("negU", 128, BF16), ("negO", 128, BF16),
        ("zer", 128, BF16), ("ones32", 128, F32), ("epsv", 1, F32)])
    C.vecs = consts["vecs"]
    C.ones32 = consts["ones32"]
    C.eps_ap = consts["epsv"]
    masks, negU, negO, zer = consts["masks"], consts["negU"], consts["negO"], consts["zer"]
    ncols = PB_COLS
    cts = [(0, ncols)]
    C.rstd = ar.alloc(ncols, F32, "rstd")
    C.rstd_slot = Slot("rstd")
    C.rstdh = ar.alloc(ncols, F32, "rstdh")
    C.rstdh_slot = Slot("rstdh")
    C.psS = [C.banks[7]]
    C.psS_slot = [C.bank_slots[7]]

    class PsRing:
        def __init__(self):
            self.i = 0

        def next(self):
            if self.mode == 1:
                b = 6 + (self.i % 2)
            else:
                b = self.i % 4
            self.i += 1
            return [C.banks[b]], [C.bank_slots[b]]
    C.psring = PsRing()
    C.psring.mode = 0
    hT = [ar.alloc(ncols, BF16, "hT") for _ in range(KC)]
    hslots = [Slot("h%d" % k) for k in range(KC)]
    aT = [ar.alloc(ncols, BF16, "aT") for _ in range(KC)]
    aslots = [Slot("a%d" % k) for k in range(KC)]
    C.wring = WCache(S, [ar.alloc(KC * 512, BF16, "w") for _ in range(2)], "w")
    C.xring = Ring([ar.alloc(ncols, F32, "xc") for _ in range(3)], "xc")
    C.yring = Ring([ar.alloc(ncols, F32, "yc") for _ in range(3)], "yc")
    C.sqring = Ring([ar.alloc(ncols, F32, "sq") for _ in range(2)], "sq")
    qring = Ring([ar.alloc(ncols, BF16, "qT") for _ in range(2)], "qT")
    sgring = Ring([ar.alloc(ncols, BF16, "sg") for _ in range(2)], "sg")
    gtring = Ring([ar.alloc(ncols, F32, "gt") for _ in range(1)], "gt")
    eering = Ring([ar.alloc(ncols, F32, "ee") for _ in range(1)], "ee")
    NQ = 4
    kq_bufs = [ar.alloc(2048, BF16, "kq") for _ in range(NQ)]
    vq_bufs = [ar.alloc(2048, BF16, "vq") for _ in range(NQ)]
    kq_slots = [Slot("kq%d" % i) for i in range(NQ)]
    vq_slots = [Slot("vq%d" % i) for i in range(NQ)]
    e_buf = ar.alloc(ncols, F32, "e")
    e_slot = Slot("e")
    spring = Ring([ar.alloc(ncols, BF16, "sp") for _ in range(3)], "sp")
    pring = Ring([ar.alloc(ncols, BF16, "P") for _ in range(4)], "P")
    ssum = [ar.alloc(ncols, BF16, "ssum") for _ in range(2)]
    ssum_slots = [Slot("ssum0"), Slot("ssum1")]

    x2_slot, x3_slot, out_slot, y_slot = Slot("x2"), Slot("x3"), Slot("out"), Slot("yscr")
    qcount = [0]

    for G in range(2):
        g0 = G * ncols

        def src_of(t, g0=g0):
            return lambda k: t[k * 128:(k + 1) * 128, g0:g0 + ncols]
        for l in range(2):
            xsrc, sslot = (dr["x2T"], x2_slot) if l == 0 else (x3, x3_slot)
            xdst, dslot = (x3, x3_slot) if l == 0 else (outT, out_slot)
            if l == 0:
                first_norm(S, C, src_of(xsrc), ncols, cts, 0, hT, hslots, sslot)
            w_in = dr["b_w_in"][l]

            def head_proj_ops(hd):
                thunks = []
                qb, qs = qring.next()
                gb, gs = sgring.next()
                for which in range(2):
                    col = hd * 128 + which * D
                    pb, psl = C.psring.next()
                    wv, ws = C.wring.get(w_in, "b_w_in%d" % l, col, KC)
                    for k in range(KC):
                        def t_mm(k=k, wv=wv, ws=ws, pb=pb, psl=psl):
                            S.op("pe", lambda h: h.matmul(pb[0][:, :], lhsT=wv[:, k, :], rhs=hT[k],
                                                          start=(k == 0), stop=(k == KC - 1)),
                                 reads=[ws, hslots[k]], writes=[psl[0]])
                        thunks.append(t_mm)
                    if which == 0:
                        def t_ev(pb=pb, psl=psl, qb=qb, qs=qs):
                            S.op("dve", lambda h: h.scalar_tensor_tensor(out=qb, in0=pb[0][:, :],
                                                                         scalar=float(128 ** -0.5), in1=C.rstdh,
                                                                         op0=ALU.mult, op1=ALU.mult),
                                 reads=[psl[0], C.rstdh_slot], writes=[qs])
                    else:
                        def t_ev(pb=pb, psl=psl, gb=gb, gs=gs):
                            gt, gts = gtring.next()
                            ee, ees = eering.next()
                            S.op("dve", lambda h: h.tensor_tensor(out=gt, in0=pb[0][:, :], in1=C.rstdh, op=ALU.mult),
                                 reads=[psl[0], C.rstdh_slot], writes=[gts])
                            S.op("act", lambda h: h.activation(out=ee, in_=gt, func=AF.Exp, scale=-1.0),
                                 reads=[gts], writes=[ees])
                            S.op("dve", lambda h: h.tensor_scalar(out=ee, in0=ee, scalar1=1.0, scalar2=None, op0=ALU.add),
                                 reads=[ees], writes=[ees])
                            S.op("dve", lambda h: h.reciprocal(out=ee, in_=ee), reads=[ees], writes=[ees])
                            S.op("dve", lambda h: h.tensor_tensor(out=gb, in0=gt, in1=ee, op=ALU.mult),
                                 reads=[gts, ees], writes=[gs])
                    thunks.append(t_ev)
                return thunks, qb, qs, gb, gs

            if G == 0:
                kb_list = list(range(31, -1, -1))
            else:
                kb_list = list(range(63, -1, -1))
            nsteps = len(kb_list)
            heads = []
            C.psring.mode = 1
            th0, qb0, qs0, gb0, gs0 = head_proj_ops(0)
            for t_ in th0:
                t_()
            heads.append((qb0, qs0, gb0, gs0))
            steps = []
            for hd in range(NH):
                for si, kb in enumerate(kb_list):
                    steps.append({"hd": hd, "si": si, "kb": kb})
            pending = []
            state = {}
            qlist = [(hd_, kb_ // 16) for hd_ in range(NH) for kb_ in kb_list if kb_ % 16 == 15]
            qslot_of = {}
            T = len(steps)

            def emit_qk(st):
                hd, si, kb = st["hd"], st["si"], st["kb"]
                if si == 0:
                    if hd + 1 < NH:
                        th, qb, qs, gb, gs = head_proj_ops(hd + 1)
                        pending.extend(th)
                        heads.append((qb, qs, gb, gs))
                if kb % 16 == 15:
                    qidx = state.get("qidx", -1) + 1
                    state["qidx"] = qidx
                    while state.get("emitted", 0) < min(qidx + 3, len(qlist)):
                        e_i = state.get("emitted", 0)
                        hd_, qq_ = qlist[e_i]
                        qi_ = qcount[0] % NQ
                        qcount[0] += 1
                        qslot_of[e_i] = qi_
                        S.op("sp", lambda h, hd_=hd_, qq_=qq_, qi_=qi_: h.dma_start(
                            out=kq_bufs[qi_], in_=dr["KT"][hd_, :, qq_ * 2048:(qq_ + 1) * 2048]),
                            writes=[kq_slots[qi_]], dma=kq_slots[qi_])
                        S.op("sp", lambda h, hd_=hd_, qq_=qq_, qi_=qi_: h.dma_start(
                            out=vq_bufs[qi_], in_=dr["Vh"][hd_, :, qq_ * 2048:(qq_ + 1) * 2048]),
                            writes=[vq_slots[qi_]], dma=vq_slots[qi_])
                        state["emitted"] = e_i + 1
                    state["qi"] = qslot_of[qidx]
                st["qi"] = state["qi"]
                qi = st["qi"]
                if G == 0 or kb >= 32:
                    r = (kb % 32) // 8
                    st["a0"] = 128 * r
                    st["mask"] = kb % 8
                else:
                    st["a0"] = 0
                    st["mask"] = None
                a0 = st["a0"]
                zb = st["gi"] % 4
                st["zb"] = zb
                qb, qs, gb, gs = heads[hd]
                ko = (kb % 16) * 128
                S.op("pe", lambda h: h.matmul(C.banks[zb][:, a0:ncols], lhsT=kq_bufs[qi][:, ko:ko + 128],
                                              rhs=qb[:, a0:ncols], start=True, stop=False, skip_group_check=True),
                     reads=[kq_slots[qi], qs], writes=[C.bank_slots[zb]])

            def emit_exp1(st):
                a0, zb = st["a0"], st["zb"]
                S.op("act", lambda h: h.activation(out=e_buf[:, a0:ncols], in_=C.banks[zb][:, a0:ncols], func=AF.Exp),
                     reads=[C.bank_slots[zb]], writes=[e_slot])

            def emit_ln(st):
                a0 = st["a0"]
                sb, ss = spring.next()
                st["sb"], st["ss"] = sb, ss
                S.op("act", lambda h: h.activation(out=sb[:, a0:ncols], in_=e_buf[:, a0:ncols], func=AF.Ln,
                                                   bias=1.0, scale=1.0), reads=[e_slot], writes=[ss])
                if st["mask"] is not None:
                    mk = st["mask"]
                    S.op("dve", lambda h: h.tensor_tensor(out=sb[:, a0:a0 + 128], in0=sb[:, a0:a0 + 128],
                                                          in1=masks[:, mk * 128:(mk + 1) * 128], op=ALU.mult),
                         reads=[ss, C.const_slot], writes=[ss])

            def emit_s(st):
                si = st["si"]
                a0, zb, sb, ss = st["a0"], st["zb"], st["sb"], st["ss"]
                first = (si == 0)
                S.op("pe", lambda h: h.matmul(C.banks[zb][:, a0:ncols], lhsT=negU, rhs=sb[:, a0:ncols],
                                              start=False, stop=first, skip_group_check=True),
                     reads=[ss, C.const_slot], writes=[C.bank_slots[zb]])
                cur = si % 2
                nxt = (si + 1) % 2
                if not first:
                    S.op("pe", lambda h: h.matmul(C.banks[zb][:, a0:ncols], lhsT=negO, rhs=ssum[cur][:, a0:ncols],
                                                  start=False, stop=True, skip_group_check=True),
                         reads=[ssum_slots[cur], C.const_slot], writes=[C.bank_slots[zb]])
                    if si + 1 < nsteps:
                        S.op("dve", lambda h: h.tensor_tensor(out=ssum[nxt][:, a0:ncols], in0=ssum[cur][:, a0:ncols],
                                                              in1=sb[:, a0:ncols], op=ALU.add),
                             reads=[ssum_slots[cur], ss], writes=[ssum_slots[nxt]], sync_same=True)
                else:
                    if si + 1 < nsteps:
                        S.op("dve", lambda h: h.tensor_copy(out=ssum[nxt][:, a0:ncols], in_=sb[:, a0:ncols]),
                             reads=[ss], writes=[ssum_slots[nxt]], sync_same=True)
                        if a0 > 0:
                            S.op("dve", lambda h: h.memset(ssum[nxt][:, 0:a0], 0.0), writes=[ssum_slots[nxt]],
                                 sync_same=True)
                            S.op("dve", lambda h: h.memset(ssum[cur][:, 0:a0], 0.0), writes=[ssum_slots[cur]],
                                 sync_same=True)

            def emit_expp(st):
                a0, zb = st["a0"], st["zb"]
                pb_, ps_ = pring.next()
                st["pb"], st["ps"] = pb_, ps_
                S.op("act", lambda h: h.activation(out=pb_[:, a0:ncols], in_=C.banks[zb][:, a0:ncols], func=AF.Exp),
                     reads=[C.bank_slots[zb]], writes=[ps_])
                if st["mask"] is not None:
                    mk = st["mask"]
                    S.op("dve", lambda h: h.tensor_tensor(out=pb_[:, a0:a0 + 128], in0=pb_[:, a0:a0 + 128],
                                                          in1=masks[:, mk * 128:(mk + 1) * 128], op=ALU.mult),
                         reads=[ps_, C.const_slot], writes=[ps_])

            def emit_av(st):
                hd, si, kb = st["hd"], st["si"], st["kb"]
                a0, qi = st["a0"], st["qi"]
                ob = 4 + (hd % 2)
                qb, qs, gb, gs = heads[hd]
                if si == 0:
                    S.op("pe", lambda h: h.matmul(C.banks[ob][:, :], lhsT=zer, rhs=qb, start=True, stop=False,
                                                  skip_group_check=True),
                         reads=[C.const_slot, qs], writes=[C.bank_slots[ob]])
                vo_ = (kb % 16) * 128
                pb_, ps_ = st["pb"], st["ps"]
                S.op("pe", lambda h: h.matmul(C.banks[ob][:, a0:ncols], lhsT=vq_bufs[qi][:, vo_:vo_ + 128],
                                              rhs=pb_[:, a0:ncols], start=False, stop=(si == nsteps - 1),
                                              skip_group_check=True),
                     reads=[vq_slots[qi], ps_], writes=[C.bank_slots[ob]])
                if si == nsteps - 1:
                    S.op("dve", lambda h: h.tensor_tensor(out=aT[hd], in0=C.banks[ob][:, :], in1=gb, op=ALU.mult),
                         reads=[C.bank_slots[ob], gs], writes=[aslots[hd]])

            for gi, st in enumerate(steps):
                st["gi"] = gi
            npend_per = 3 if G == 0 else 2
            C.psring.mode = 1
            emit_qk(steps[0])
            for t in range(T + 3):
                if t + 1 < T:
                    emit_qk(steps[t + 1])
                if t < T:
                    emit_exp1(steps[t])
                if 1 <= t <= T:
                    emit_s(steps[t - 1])
                if 2 <= t <= T + 1:
                    emit_expp(steps[t - 2])
                if t < T:
                    emit_ln(steps[t])
                if t >= 3:
                    emit_av(steps[t - 3])
                for _ in range(npend_per):
                    if pending:
                        pending.pop(0)()
            while pending:
                pending.pop(0)()
            C.psring.mode = 0
            wout_and_residual(S, C, dr["b_w_out"][l], "b_w_out%d" % l, aT, aslots, ncols, cts, (2 + l) * KC,
                              src_of(xsrc), sslot, src_of(xdst), dslot,
                              lambda m: yscr[m * 128:(m + 1) * 128, :], y_slot,
                              next_gcol=(KC if l == 0 else None), hT=hT, hslots=hslots)
    S.emit(nc, [out_slot])
    return nc


def _vec_cols(v):
    return np.ascontiguousarray(np.asarray(v, np.float32).reshape(KC, 128).T)


_CACHE = {}


def kernel(x, a_pre_norm, a_w_in, a_w_group, a_scale, a_w_out, a_post_norm,
           kv_norm, w_kv, b_pre_norm, b_w_in, b_w_out, b_post_norm):
    bf = ml_dtypes.bfloat16
    x = np.asarray(x, np.float32)[0]
    xT_full = np.ascontiguousarray(x.T)
    a_w_in = np.asarray(a_w_in, np.float32)
    a_w_group = np.asarray(a_w_group, np.float32)
    a_w_out = np.asarray(a_w_out, np.float32)
    w_kv = np.asarray(w_kv, np.float32)
    b_w_in = np.asarray(b_w_in, np.float32)
    b_w_out = np.asarray(b_w_out, np.float32)
    ones32 = np.ones((128, 128), np.float32)
    epsv = np.full((128, 1), EPS, np.float32)
    vecsA = np.concatenate([_vec_cols(a_pre_norm[0]), _vec_cols(a_pre_norm[1]), _vec_cols(a_scale[0]),
                            _vec_cols(a_scale[1]), _vec_cols(a_post_norm[0]), _vec_cols(a_post_norm[1]),
                            _vec_cols(kv_norm)], axis=1)
    in_maps = []
    for c in range(NCORE):
        xT = np.zeros((D, NSTR * SW), np.float32)
        for k in range(NSTR):
            b = 8 * k + c
            lo = 128 * b - HALO
            if lo >= 0:
                xT[:, k * SW:(k + 1) * SW] = xT_full[:, lo:lo + SW]
            else:
                xT[:, k * SW + HALO:(k + 1) * SW] = xT_full[:, 0:128]
        icnt = np.zeros((128, 64), np.float32)
        for g, w in enumerate(WINS):
            pos = 128 * c + np.arange(16)
            icnt[:, g * 16:(g + 1) * 16] = (1.0 / np.minimum(pos + 1, w)).astype(np.float32)[None, :]
        in_maps.append({"xT": xT, "a_w_in": a_w_in, "a_w_group": a_w_group, "a_w_out": a_w_out, "w_kv": w_kv,
                        "vecs": vecsA, "icnt": icnt, "ones32": ones32, "epsv": epsv})
    if "A" not in _CACHE:
        _CACHE["A"] = build_phase_a()
    resA = run_bass_kernel_spmd(_CACHE["A"], in_maps, core_ids=list(range(NCORE))).results
    KT = np.zeros((NH, 128, SEQ), bf)
    Vh = np.zeros((NH, 128, SEQ), bf)
    x2_list = []
    for c in range(NCORE):
        kt = np.asarray(resA[c]["KT"])
        v = np.asarray(resA[c]["V"])
        x2 = np.asarray(resA[c]["x2T"])
        own = np.concatenate([np.arange(k * SW + HALO, (k + 1) * SW) for k in range(NSTR)])
        x2_list.append(np.ascontiguousarray(x2[:, own]))
        for k in range(NSTR):
            b = 8 * k + c
            KT[:, :, 128 * b:128 * b + 128] = kt[:, :, 128 * k:128 * k + 128]
            vb = v[128 * k:128 * k + 128, :].reshape(128, NH, 128)
            Vh[:, :, 128 * b:128 * b + 128] = vb.transpose(1, 0, 2)
    vecsB = np.concatenate([_vec_cols(b_pre_norm[0]), _vec_cols(b_pre_norm[1]),
                            _vec_cols(b_post_norm[0]), _vec_cols(b_post_norm[1])], axis=1)
    jj = np.arange(128)
    tri = (jj[:, None] < jj[None, :]).astype(np.float32)
    negU = (-(jj[:, None] >= jj[None, :]).astype(np.float32)).astype(bf)
    negO = (-np.ones((128, 128), np.float32)).astype(bf)
    zer = np.zeros((128, 128), bf)
    in_maps = []
    for c in range(NCORE):
        m = np.zeros((128, 8, 128), np.float32)
        for r in range(8):
            if r < c:
                m[:, r, :] = 1.0
            elif r == c:
                m[:, r, :] = tri
        in_maps.append({"x2T": x2_list[c], "KT": KT, "Vh": Vh, "b_w_in": b_w_in, "b_w_out": b_w_out,
                        "vecs": vecsB, "masks": m.reshape(128, 1024).astype(bf), "negU": negU, "negO": negO,
                        "zer": zer, "ones32": ones32, "epsv": epsv})
    if "B" not in _CACHE:
        _CACHE["B"] = build_phase_b()
    resB = run_bass_kernel_spmd(_CACHE["B"], in_maps, core_ids=list(range(NCORE))).results
    out = np.zeros((SEQ, D), np.float32)
    for c in range(NCORE):
        oT = np.asarray(resB[c]["outT"])
        for k in range(NSTR):
            b = 8 * k + c
            out[128 * b:128 * b + 128, :] = oT[:, 128 * k:128 * k + 128].T
    return out[None]
```

```python
import contextlib
import numpy as np
import ml_dtypes
import concourse.bass as bass
import concourse.mybir as mybir
from concourse.bass_utils import run_bass_kernel_spmd

F32 = mybir.dt.float32
BF16 = mybir.dt.bfloat16
AF = mybir.ActivationFunctionType
ALU = mybir.AluOpType

D = 4096
KC = D // 128
SEQ = 8192
NCORE = 8
NH = 32
EPS = 1e-6
HALO = 32
SW = 128 + HALO
NSTR = 8
PA_COLS = 4 * SW
PA_CT = PA_COLS // 2
PB_COLS = 512
WINS = (2, 4, 8, 16)
ENGS = ("pe", "act", "dve", "pool", "sp")


class Slot:
    __slots__ = ("name", "w", "r", "sem", "ndma")

    def __init__(self, name):
        self.name = name
        self.w = None
        self.r = []
        self.sem = None
        self.ndma = 0


class Sched:
    def __init__(self):
        self.ops = {e: [] for e in ENGS}
        self.seen_c = {e: {} for e in ENGS}
        self.seen_d = {e: {} for e in ENGS}
        self.dma_slots = []

    def _need(self, eng, tok, is_dma):
        if tok is None:
            return False
        if tok[0] == "c":
            _, se, idx = tok
            if se == eng and not is_dma and not self.sync_same:
                return False
            if self.seen_c[eng].get(se, -1) >= idx:
                return False
            self.seen_c[eng][se] = idx
            return True
        _, sl, cnt = tok
        if self.seen_d[eng].get(sl, 0) >= cnt:
            return False
        self.seen_d[eng][sl] = cnt
        return True

    def op(self, eng, fn, reads=(), writes=(), dma=None, sync_same=False):
        is_dma = dma is not None
        self.sync_same = sync_same
        cands = []
        for s_ in reads:
            if s_.w is not None:
                cands.append(s_.w)
        for s_ in writes:
            if s_.w is not None:
                cands.append(s_.w)
            cands.extend(s_.r)
        best_c, best_d = {}, {}
        for t in cands:
            if t[0] == "c":
                if t[2] > best_c.get(t[1], -1):
                    best_c[t[1]] = t[2]
            else:
                if t[2] > best_d.get(t[1], 0):
                    best_d[t[1]] = t[2]
        waits = []
        for se, ix in best_c.items():
            t = ("c", se, ix)
            if self._need(eng, t, is_dma):
                waits.append(t)
        for sl, cnt in best_d.items():
            t = ("d", sl, cnt)
            if self._need(eng, t, is_dma):
                waits.append(t)
        idx = len(self.ops[eng])
        if is_dma:
            if dma.sem is None:
                dma.sem = True
                self.dma_slots.append(dma)
            dma.ndma += 1
            tok = ("d", dma, dma.ndma)
        else:
            tok = ("c", eng, idx)
        self.ops[eng].append({"fn": fn, "waits": waits, "dma": dma, "inc": False})
        for s in reads:
            s.r.append(tok)
        for s in writes:
            s.w = tok
            s.r = []
        return tok

    def emit(self, nc, final_slots):
        for e in ENGS:
            for o in self.ops[e]:
                for t in o["waits"]:
                    if t[0] == "c":
                        self.ops[t[1]][t[2]]["inc"] = True
        cum = {}
        for e in ENGS:
            c = 0
            arr = []
            for o in self.ops[e]:
                if o["inc"]:
                    c += 1
                arr.append(c)
            cum[e] = arr
        with contextlib.ExitStack() as st:
            esem = {e: st.enter_context(nc.semaphore("e_" + e)) for e in ENGS}
            for i, sl in enumerate(self.dma_slots):
                sl.sem = st.enter_context(nc.semaphore("d%d" % i))
            block = st.enter_context(nc.Block())

            def run(e, h):
                for o in self.ops[e]:
                    for t in o["waits"]:
                        if t[0] == "c":
                            h.wait_ge(esem[t[1]], cum[t[1]][t[2]])
                        else:
                            h.wait_ge(t[1].sem, 16 * t[2])
                    ins = o["fn"](h)
                    if o["dma"] is not None:
                        ins.then_inc(o["dma"].sem, 16)
                    elif o["inc"]:
                        ins.then_inc(esem[e], 1)
                if e == "sp":
                    for sl in final_slots:
                        if sl.ndma:
                            h.wait_ge(sl.sem, 16 * sl.ndma)

            @block.tensor
            def _(h):
                run("pe", h)

            @block.scalar
            def _(h):
                run("act", h)

            @block.vector
            def _(h):
                run("dve", h)

            @block.gpsimd
            def _(h):
                run("pool", h)

            @block.sync
            def _(h):
                run("sp", h)


class Arena:
    def __init__(self, nc, nbytes):
        self.t = nc.alloc_sbuf_tensor("arena", [128, nbytes // 4], F32)
        self.off = 0
        self.cap = nbytes

    def mark(self):
        return self.off

    def reset(self, m):
        self.off = m

    def alloc(self, n, dt, name=None):
        nb = n * (2 if dt == BF16 else 4)
        nb = (nb + 63) // 64 * 64
        assert self.off + nb <= self.cap, ("SBUF overflow", name, self.off, nb)
        a = self.t[:, self.off // 4:(self.off + nb) // 4]
        self.off += nb
        if dt == BF16:
            a = a.bitcast(BF16)
        return a[:, 0:n]


class Ring:
    def __init__(self, bufs, name):
        self.bufs = bufs
        self.slots = [Slot("%s%d" % (name, i)) for i in range(len(bufs))]
        self.i = 0

    def next(self):
        k = self.i % len(self.bufs)
        self.i += 1
        return self.bufs[k], self.slots[k]


class WCache:
    def __init__(self, S, bufs, name):
        self.S = S
        self.ring = Ring(bufs, name)
        self.keys = {}

    def get(self, w2d, wname, col, nk):
        g = col // 512
        key = (wname, g)
        if key not in self.keys:
            wb, ws = self.ring.next()
            for k_ in [k_ for k_, v_ in self.keys.items() if v_[1] is ws]:
                del self.keys[k_]
            wv = wb.rearrange("p (a b) -> p a b", b=512)
            src = w2d[:, g * 512:(g + 1) * 512].rearrange("(a p) n -> p a n", p=128)
            self.S.op("pool", lambda h: h.dma_start(out=wv[:, 0:nk, :], in_=src), writes=[ws], dma=ws)
            self.keys[key] = (wv, ws)
        wv, ws = self.keys[key]
        o = col - g * 512
        return wv[:, :, o:o + 128], ws


def load_consts(S, ar, dram, names):
    out = {}
    sl = Slot("consts")
    for key, n, dt in names:
        buf = ar.alloc(n, dt, key)
        out[key] = buf
        S.op("sp", lambda h, b=buf, k=key: h.dma_start(out=b, in_=dram[k]), writes=[sl], dma=sl)
    return out, sl


class Ctx:
    pass


def finalize_rstd(S, C, cts, ncols, dst, dst_slot):
    for i, (c0, c1) in enumerate(cts):
        S.op("act", lambda h, i=i, c0=c0, c1=c1: h.activation(
            out=dst[:, c0:c1], in_=C.psS[i][:, 0:c1 - c0], func=AF.Sqrt, bias=C.eps_ap, scale=1.0 / D),
            reads=[C.psS_slot[i], C.const_slot], writes=[dst_slot])
    S.op("dve", lambda h: h.reciprocal(out=dst[:, 0:ncols], in_=dst[:, 0:ncols]),
         reads=[dst_slot], writes=[dst_slot])


def norm_feed(S, C, xb, xs, kc, ncols, cts, gcol, hT, hslots):
    qb, qs = C.sqring.next()
    S.op("act", lambda h: h.activation(out=qb[:, 0:ncols], in_=xb[:, 0:ncols], func=AF.Square),
         reads=[xs], writes=[qs])
    for i, (c0, c1) in enumerate(cts):
        S.op("pe", lambda h, i=i, c0=c0, c1=c1: h.matmul(
            C.psS[i][:, 0:c1 - c0], lhsT=C.ones32, rhs=qb[:, c0:c1], start=(kc == 0), stop=(kc == KC - 1)),
            reads=[qs, C.const_slot], writes=[C.psS_slot[i]])
    S.op("dve", lambda h: h.tensor_scalar(out=hT[kc][:, 0:ncols], in0=xb[:, 0:ncols],
                                          scalar1=C.vecs[:, gcol + kc:gcol + kc + 1], scalar2=None, op0=ALU.mult),
         reads=[xs, C.const_slot], writes=[hslots[kc]])


def first_norm(S, C, xsrc_fn, ncols, cts, gcol, hT, hslots, src_slot):
    for kc in range(KC):
        xb, xs = C.xring.next()
        S.op("sp", lambda h, b=xb, k=kc: h.dma_start(out=b[:, 0:ncols], in_=xsrc_fn(k)),
             reads=[src_slot], writes=[xs], dma=xs)
        norm_feed(S, C, xb, xs, kc, ncols, cts, gcol, hT, hslots)
    finalize_rstd(S, C, cts, ncols, C.rstdh, C.rstdh_slot)


def proj_chunk(S, C, w_ap, nk, rhs_list, rhs_slots, cts, wring, ps_bufs, ps_slots):
    if isinstance(wring, WCache):
        w2d, wname, col = w_ap
        wv, ws = wring.get(w2d, wname, col, nk)
    else:
        wb, ws = wring.next()
        wv = wb.rearrange("p (a b) -> p a b", b=128)
        S.op("pool", lambda h: h.dma_start(out=wv[:, 0:nk, :], in_=w_ap.rearrange("(a p) n -> p a n", p=128)),
             writes=[ws], dma=ws)
    for k in range(nk):
        for i, (c0, c1) in enumerate(cts):
            S.op("pe", lambda h, k=k, i=i, c0=c0, c1=c1: h.matmul(
                ps_bufs[i][:, 0:c1 - c0], lhsT=wv[:, k, :], rhs=rhs_list[k][:, c0:c1],
                start=(k == 0), stop=(k == nk - 1)),
                reads=[ws, rhs_slots[k]], writes=[ps_slots[i]])


def wout_and_residual(S, C, w_out_ap, wname, aT, aslots, ncols, cts, gcol, xsrc_fn, src_slot, xdst_fn, dst_slot, yscr_fn, yslot,
                      next_gcol=None, hT=None, hslots=None):
    pend = None
    for m in range(KC):
        pb, pslots = C.psring.next()
        proj_chunk(S, C, (w_out_ap, wname, m * 128), KC, aT, aslots, cts, C.wring, pb, pslots)
        yb, ys = C.yring.next()
        qb, qs = C.sqring.next()
        for i, (c0, c1) in enumerate(cts):
            S.op("act", lambda h, i=i, c0=c0, c1=c1, pb=pb, yb=yb: h.activation(
                out=yb[:, c0:c1], in_=pb[i][:, 0:c1 - c0], func=AF.Copy), reads=[pslots[i]], writes=[ys])
            S.op("act", lambda h, i=i, c0=c0, c1=c1, pb=pb, qb=qb: h.activation(
                out=qb[:, c0:c1], in_=pb[i][:, 0:c1 - c0], func=AF.Square), reads=[pslots[i]], writes=[qs])
        S.op("act", lambda h, m=m, yb=yb: h.dma_start(out=yscr_fn(m), in_=yb[:, 0:ncols]),
             reads=[ys], writes=[yslot], dma=yslot)
        if pend is not None:
            pend()

        def mk(m=m, qb=qb, qs=qs):
            def f():
                for i, (c0, c1) in enumerate(cts):
                    S.op("pe", lambda h, i=i, c0=c0, c1=c1: h.matmul(
                        C.psS[i][:, 0:c1 - c0], lhsT=C.ones32, rhs=qb[:, c0:c1], start=(m == 0), stop=(m == KC - 1)),
                        reads=[qs, C.const_slot], writes=[C.psS_slot[i]])
            return f
        pend = mk()
    pend()
    finalize_rstd(S, C, cts, ncols, C.rstd, C.rstd_slot)
    for m in range(KC):
        yb, ys = C.yring.next()
        S.op("sp", lambda h, m=m, yb=yb: h.dma_start(out=yb[:, 0:ncols], in_=yscr_fn(m)),
             reads=[yslot], writes=[ys], dma=ys)
        xb, xs = C.xring.next()
        S.op("sp", lambda h, m=m, xb=xb: h.dma_start(out=xb[:, 0:ncols], in_=xsrc_fn(m)),
             reads=[src_slot], writes=[xs], dma=xs)
        S.op("dve", lambda h, m=m, yb=yb: h.scalar_tensor_tensor(
            out=yb[:, 0:ncols], in0=yb[:, 0:ncols], scalar=C.vecs[:, gcol + m:gcol + m + 1],
            in1=C.rstd[:, 0:ncols], op0=ALU.mult, op1=ALU.mult),
            reads=[ys, C.rstd_slot, C.const_slot], writes=[ys])
        S.op("dve", lambda h, yb=yb, xb=xb: h.tensor_tensor(
            out=xb[:, 0:ncols], in0=yb[:, 0:ncols], in1=xb[:, 0:ncols], op=ALU.add),
            reads=[ys, xs], writes=[xs])
        S.op("act", lambda h, m=m, xb=xb: h.dma_start(out=xdst_fn(m), in_=xb[:, 0:ncols]),
             reads=[xs], writes=[dst_slot], dma=dst_slot)
        if next_gcol is not None:
            norm_feed(S, C, xb, xs, m, ncols, cts, next_gcol, hT, hslots)
    if next_gcol is not None:
        finalize_rstd(S, C, cts, ncols, C.rstdh, C.rstdh_slot)


def alloc_psum(nc, C):
    C.banks = [nc.alloc_psum_tensor("bank%d" % i, [128, 512], F32) for i in range(8)]
    C.bank_slots = [Slot("bank%d" % i) for i in range(8)]


def build_phase_a():
    nc = bass.Bass("TRN2", target_bir_lowering=False)
    NT = NSTR * SW
    dr = {}
    dr["xT"] = nc.dram_tensor("xT", [D, NT], F32, kind="ExternalInput").ap()
    dr["a_w_in"] = nc.dram_tensor("a_w_in", [2, D, 2 * D], F32, kind="ExternalInput").ap()
    dr["a_w_group"] = nc.dram_tensor("a_w_group", [2, 4, 1024, 1024], F32, kind="ExternalInput").ap()
    dr["a_w_out"] = nc.dram_tensor("a_w_out", [2, D, D], F32, kind="ExternalInput").ap()
    dr["w_kv"] = nc.dram_tensor("w_kv", [D, 2 * D], F32, kind="ExternalInput").ap()
    dr["vecs"] = nc.dram_tensor("vecs", [128, 7 * KC], F32, kind="ExternalInput").ap()
    dr["icnt"] = nc.dram_tensor("icnt", [128, 64], F32, kind="ExternalInput").ap()
    dr["ones32"] = nc.dram_tensor("ones32", [128, 128], F32, kind="ExternalInput").ap()
    dr["epsv"] = nc.dram_tensor("epsv", [128, 1], F32, kind="ExternalInput").ap()
    x1 = nc.dram_tensor("x1s", [D, NT], F32).ap()
    x2 = nc.dram_tensor("x2T", [D, NT], F32, kind="ExternalOutput").ap()
    yscr = nc.dram_tensor("yscr", [D, PA_COLS], F32).ap()
    kto = nc.dram_tensor("KT", [NH, 128, NSTR * 128], BF16, kind="ExternalOutput").ap()
    vo = nc.dram_tensor("V", [NSTR * 128, D], BF16, kind="ExternalOutput").ap()

    S = Sched()
    C = Ctx()
    ar = Arena(nc, 206 * 1024)
    alloc_psum(nc, C)
    consts, C.const_slot = load_consts(S, ar, dr, [("vecs", 7 * KC, F32), ("icnt", 64, F32),
                                                    ("ones32", 128, F32), ("epsv", 1, F32)])
    C.vecs = consts["vecs"]
    C.ones32 = consts["ones32"]
    C.eps_ap = consts["epsv"]
    icnt = consts["icnt"]
    ncols = PA_COLS
    cts = [(0, PA_CT), (PA_CT, PA_COLS)]
    C.rstd = ar.alloc(ncols, F32, "rstd")
    C.rstd_slot = Slot("rstd")
    C.rstdh = ar.alloc(ncols, F32, "rstdh")
    C.rstdh_slot = Slot("rstdh")
    C.psS = [C.banks[6], C.banks[7]]
    C.psS_slot = [C.bank_slots[6], C.bank_slots[7]]

    class PsRing:
        def __init__(self):
            self.i = 0

        def next(self):
            b = self.i % 3
            self.i += 1
            return [C.banks[2 * b], C.banks[2 * b + 1]], [C.bank_slots[2 * b], C.bank_slots[2 * b + 1]]
    C.psring = PsRing()
    hT = [ar.alloc(ncols, BF16, "hT") for _ in range(KC)]
    hslots = [Slot("h%d" % k) for k in range(KC)]
    base = ar.mark()
    aT = [ar.alloc(ncols, BF16, "aT") for _ in range(KC)]
    aslots = [Slot("a%d" % k) for k in range(KC)]
    C.wring = WCache(S, [ar.alloc(KC * 512, BF16, "w") for _ in range(2)], "w")
    wgring = Ring([ar.alloc(8 * 128, BF16, "wg") for _ in range(2)], "wg")
    C.xring = Ring([ar.alloc(ncols, F32, "xc") for _ in range(3)], "xc")
    C.yring = Ring([ar.alloc(ncols, F32, "yc") for _ in range(3)], "yc")
    C.sqring = Ring([ar.alloc(ncols, F32, "sq") for _ in range(2)], "sq")
    uring = Ring([ar.alloc(ncols, F32, "u") for _ in range(2)], "u")
    sA = ar.alloc(ncols, F32, "sA")
    sB = ar.alloc(ncols, F32, "sB")
    sAs, sBs = Slot("sA"), Slot("sB")
    pooled = [ar.alloc(ncols, BF16, "pooled") for _ in range(8)]
    pslots_ = [Slot("pl%d" % k) for k in range(8)]
    sgring = Ring([ar.alloc(ncols, BF16, "sg") for _ in range(2)], "sg")
    gtring = Ring([ar.alloc(ncols, F32, "gt") for _ in range(2)], "gt")
    rcol = ar.alloc(4, F32, "rcol")
    rcol_slot = Slot("rcol")
    tmp16 = ar.alloc(16, F32, "tmp16")
    tmp16s = Slot("tmp16")
    for b, s in ((sA, sAs), (sB, sBs)):
        S.op("dve", lambda h, b=b: h.memset(b, 0.0), writes=[s])
    for b, s in zip(uring.bufs, uring.slots):
        S.op("dve", lambda h, b=b: h.memset(b, 0.0), writes=[s])
    for b, s in zip(pooled, pslots_):
        S.op("dve", lambda h, b=b: h.memset(b, 0.0), writes=[s])

    xin_slot, x1_slot, x2_slot, y_slot = Slot("xin"), Slot("x1"), Slot("x2"), Slot("yscr")
    kt_slot, v_slot = Slot("kt"), Slot("v")

    for p in range(2):
        cb = p * PA_COLS

        def src_of(t, cb=cb):
            return lambda k: t[k * 128:(k + 1) * 128, cb:cb + ncols]
        for l in range(2):
            xsrc, sslot = (dr["xT"], xin_slot) if l == 0 else (x1, x1_slot)
            xdst, dslot = (x1, x1_slot) if l == 0 else (x2, x2_slot)
            if l == 0:
                first_norm(S, C, src_of(xsrc), ncols, cts, 0, hT, hslots, sslot)
            w_in = dr["a_w_in"][l]
            for og in range(4):
                win = WINS[og]
                for j in range(8):
                    m = og * 8 + j
                    pb, psl = C.psring.next()
                    proj_chunk(S, C, (w_in, "a_w_in%d" % l, m * 128), KC, hT, hslots, cts, C.wring, pb, psl)
                    ub, us = uring.next()
                    for i, (c0, c1) in enumerate(cts):
                        S.op("dve", lambda h, i=i, c0=c0, c1=c1, pb=pb, ub=ub: h.tensor_tensor(
                            out=ub[:, c0:c1], in0=pb[i][:, 0:c1 - c0], in1=C.rstdh[:, c0:c1], op=ALU.mult),
                            reads=[psl[i], C.rstdh_slot], writes=[us])
                    cur, curs = ub, us
                    sh = 1
                    tgl = 0
                    while sh < win:
                        nb, nbs = (sA, sAs) if tgl == 0 else (sB, sBs)
                        S.op("dve", lambda h, cur=cur, nb=nb, sh=sh: h.tensor_tensor(
                            out=nb[:, sh:ncols], in0=cur[:, sh:ncols], in1=cur[:, 0:ncols - sh], op=ALU.add),
                            reads=[curs], writes=[nbs])
                        cur, curs = nb, nbs
                        sh *= 2
                        tgl ^= 1
                    S.op("dve", lambda h, cur=cur, ub=ub, j=j, win=win: h.scalar_tensor_tensor(
                        out=pooled[j][:, 16:ncols], in0=cur[:, 16:ncols], scalar=1.0 / win, in1=ub[:, 16:ncols],
                        op0=ALU.mult, op1=ALU.subtract), reads=[curs, us], writes=[pslots_[j]])
                    if p == 0:
                        S.op("dve", lambda h, cur=cur, og=og: h.tensor_tensor(
                            out=tmp16, in0=cur[:, HALO:HALO + 16], in1=icnt[:, og * 16:(og + 1) * 16], op=ALU.mult),
                            reads=[curs, C.const_slot], writes=[tmp16s], sync_same=True)
                        S.op("dve", lambda h, ub=ub, j=j: h.tensor_tensor(
                            out=pooled[j][:, HALO:HALO + 16], in0=tmp16, in1=ub[:, HALO:HALO + 16], op=ALU.subtract),
                            reads=[tmp16s, us], writes=[pslots_[j]], sync_same=True)
                for jo in range(8):
                    m = og * 8 + jo
                    pb, psl = C.psring.next()
                    proj_chunk(S, C, (w_in, "a_w_in%d" % l, D + m * 128), KC, hT, hslots, cts, C.wring, pb, psl)
                    gb, gs = sgring.next()
                    gt, gts = gtring.next()
                    for i, (c0, c1) in enumerate(cts):
                        S.op("dve", lambda h, i=i, c0=c0, c1=c1, pb=pb, gt=gt: h.tensor_tensor(
                            out=gt[:, c0:c1], in0=pb[i][:, 0:c1 - c0], in1=C.rstdh[:, c0:c1], op=ALU.mult),
                            reads=[psl[i], C.rstdh_slot], writes=[gts])
                    S.op("act", lambda h, gb=gb, gt=gt: h.activation(out=gb, in_=gt, func=AF.Silu),
                         reads=[gts], writes=[gs])
                    for i, (c0, c1) in []:
                        pass
                    pb2, psl2 = C.psring.next()
                    proj_chunk(S, C, dr["a_w_group"][l, og][:, jo * 128:(jo + 1) * 128], 8, pooled, pslots_, cts,
                               wgring, pb2, psl2)
                    for i, (c0, c1) in enumerate(cts):
                        S.op("dve", lambda h, i=i, c0=c0, c1=c1, pb2=pb2, gb=gb, m=m, l=l: h.scalar_tensor_tensor(
                            out=aT[m][:, c0:c1], in0=pb2[i][:, 0:c1 - c0],
                            scalar=C.vecs[:, (2 + l) * KC + m:(2 + l) * KC + m + 1], in1=gb[:, c0:c1],
                            op0=ALU.mult, op1=ALU.mult), reads=[psl2[i], gs, C.const_slot], writes=[aslots[m]])
            wout_and_residual(S, C, dr["a_w_out"][l], "a_w_out%d" % l, aT, aslots, ncols, cts, (4 + l) * KC,
                              src_of(xsrc), sslot, src_of(xdst), dslot,
                              lambda m: yscr[m * 128:(m + 1) * 128, :], y_slot,
                              next_gcol=(KC if l == 0 else 6 * KC), hT=hT, hslots=hslots)
        for s4 in range(4):
            o = s4 * SW + HALO
            S.op("pe", lambda h, o=o: h.matmul(C.psS[0][:, 0:1], lhsT=C.rstdh[:, o:o + 128], rhs=C.ones32[:, 0:1],
                                              start=True, stop=True),
                 reads=[C.rstdh_slot, C.const_slot], writes=[C.psS_slot[0]])
            S.op("act", lambda h, s4=s4: h.activation(out=rcol[:, s4:s4 + 1], in_=C.psS[0][:, 0:1], func=AF.Copy,
                                                      scale=1.0 / 128), reads=[C.psS_slot[0]], writes=[rcol_slot])
        ktring = Ring([aT[0][:, 0:512], aT[1][:, 0:512]], "ktb")
        for a_ in (0, 1):
            ktring.slots[a_] = aslots[a_]
        for hd in range(NH):
            pb, psl = C.psring.next()
            proj_chunk(S, C, (dr["w_kv"], "w_kv", hd * 128), KC, hT, hslots, cts, C.wring, pb, psl)
            kb_, ks_ = ktring.next()
            for s4 in range(4):
                i = s4 // 2
                o = (s4 % 2) * SW + HALO
                S.op("dve", lambda h, i=i, o=o, s4=s4, pb=pb, kb_=kb_: h.tensor_tensor(
                    out=kb_[:, s4 * 128:(s4 + 1) * 128], in0=pb[i][:, o:o + 128],
                    in1=C.rstdh[:, i * PA_CT + o:i * PA_CT + o + 128], op=ALU.mult),
                    reads=[psl[i], C.rstdh_slot], writes=[ks_])
            S.op("act", lambda h, hd=hd, kb_=kb_, p=p: h.dma_start(
                out=kto[hd, :, p * 512:(p + 1) * 512], in_=kb_), reads=[ks_], writes=[kt_slot], dma=kt_slot)
        m0 = ar.mark()
        ar.reset(base + 2 * ncols * 2)
        wv_bufs = [ar.alloc(16 * 512, BF16, "wv") for _ in range(2)]
        vb_bufs = [ar.alloc(512, BF16, "vb") for _ in range(2)]
        assert ar.mark() <= base + KC * ncols * 2
        ar.reset(m0)
        wvring = Ring(wv_bufs, "wv")
        vbring = Ring(vb_bufs, "vb")
        for s_ in wvring.slots + vbring.slots:
            s_.w = None
        region_slots = aslots[2:]
        for ft in range(8):
            for half in range(2):
                wb, ws = wvring.next()
                wv = wb.rearrange("p (a b) -> p a b", b=512)
                src = dr["w_kv"][half * 2048:(half + 1) * 2048, D + ft * 512:D + (ft + 1) * 512]
                S.op("pool", lambda h, wv=wv, src=src: h.dma_start(
                    out=wv, in_=src.rearrange("(a p) n -> p a n", p=128)),
                    reads=[], writes=[ws] + (region_slots if (ft == 0) else []), dma=ws)
                for s4 in range(4):
                    o = s4 * SW + HALO
                    for kk in range(16):
                        k = half * 16 + kk
                        S.op("pe", lambda h, s4=s4, o=o, k=k, kk=kk, wv=wv: h.matmul(
                            C.banks[s4][:, :], lhsT=hT[k][:, o:o + 128], rhs=wv[:, kk, :],
                            start=(k == 0), stop=(k == KC - 1)),
                            reads=[ws, hslots[k]], writes=[C.bank_slots[s4]])
            for s4 in range(4):
                vb, vs = vbring.next()
                S.op("act" if s4 % 2 == 0 else "dve",
                     (lambda h, s4=s4, vb=vb: h.activation(out=vb, in_=C.banks[s4][:, :], func=AF.Copy,
                                                           scale=rcol[:, s4:s4 + 1]))
                     if s4 % 2 == 0 else
                     (lambda h, s4=s4, vb=vb: h.tensor_scalar(out=vb, in0=C.banks[s4][:, :],
                                                              scalar1=rcol[:, s4:s4 + 1], scalar2=None, op0=ALU.mult)),
                     reads=[C.bank_slots[s4], rcol_slot],
                     writes=[vs] + (region_slots if (ft == 0 and s4 < 2) else []))
                r0 = p * 512 + s4 * 128
                S.op("act", lambda h, vb=vb, r0=r0, ft=ft: h.dma_start(
                    out=vo[r0:r0 + 128, ft * 512:(ft + 1) * 512], in_=vb), reads=[vs], writes=[v_slot], dma=v_slot)
        for s_ in region_slots:
            for r_ in wvring.slots + vbring.slots:
                if r_.w is not None:
                    s_.r.append(r_.w)
                s_.r.extend(r_.r)
    S.emit(nc, [x2_slot, kt_slot, v_slot])
    return nc


def build_phase_b():
    nc = bass.Bass("TRN2", target_bir_lowering=False)
    NT = NSTR * 128
    dr = {}
    dr["x2T"] = nc.dram_tensor("x2T", [D, NT], F32, kind="ExternalInput").ap()
    dr["KT"] = nc.dram_tensor("KT", [NH, 128, SEQ], BF16, kind="ExternalInput").ap()
    dr["Vh"] = nc.dram_tensor("Vh", [NH, 128, SEQ], BF16, kind="ExternalInput").ap()
    dr["b_w_in"] = nc.dram_tensor("b_w_in", [2, D, 2 * D], F32, kind="ExternalInput").ap()
    dr["b_w_out"] = nc.dram_tensor("b_w_out", [2, D, D], F32, kind="ExternalInput").ap()
    dr["vecs"] = nc.dram_tensor("vecs", [128, 4 * KC], F32, kind="ExternalInput").ap()
    dr["masks"] = nc.dram_tensor("masks", [128, 8 * 128], BF16, kind="ExternalInput").ap()
    dr["negU"] = nc.dram_tensor("negU", [128, 128], BF16, kind="ExternalInput").ap()
    dr["negO"] = nc.dram_tensor("negO", [128, 128], BF16, kind="ExternalInput").ap()
    dr["zer"] = nc.dram_tensor("zer", [128, 128], BF16, kind="ExternalInput").ap()
    dr["ones32"] = nc.dram_tensor("ones32", [128, 128], F32, kind="ExternalInput").ap()
    dr["epsv"] = nc.dram_tensor("epsv", [128, 1], F32, kind="ExternalInput").ap()
    x3 = nc.dram_tensor("x3s", [D, NT], F32).ap()
    outT = nc.dram_tensor("outT", [D, NT], F32, kind="ExternalOutput").ap()
    yscr = nc.dram_tensor("yscr", [D, PB_COLS], F32).ap()

    S = Sched()
    C = Ctx()
    ar = Arena(nc, 206 * 1024)
    alloc_psum(nc, C)
    consts, C.const_slot = load_consts(S, ar, dr, [
        ("vecs", 4 * KC, F32), ("masks", 8 * 128, BF16), ("negU", 128, BF16), ("negO", 128, BF16),
        ("zer", 128, BF16), ("ones32", 128, F32), ("epsv", 1, F32)])
    C.vecs = consts["vecs"]
    C.ones32 = consts["ones32"]
    C.eps_ap = consts["epsv"]
    masks, negU, negO, zer = consts["masks"], consts["negU"], consts["negO"], consts["zer"]
    ncols = PB_COLS
    cts = [(0, ncols)]
    C.rstd = ar.alloc(ncols, F32, "rstd")
    C.rstd_slot = Slot("rstd")
    C.rstdh = ar.alloc(ncols, F32, "rstdh")
    C.rstdh_slot = Slot("rstdh")
    C.psS = [C.banks[7]]
    C.psS_slot = [C.bank_slots[7]]

    class PsRing:
        def __init__(self):
            self.i = 0

        def next(self):
            if self.mode == 1:
                b = 6 + (self.i % 2)
            else:
                b = self.i % 4
            self.i += 1
            return [C.banks[b]], [C.bank_slots[b]]
    C.psring = PsRing()
    C.psring.mode = 0
    hT = [ar.alloc(ncols, BF16, "hT") for _ in range(KC)]
    hslots = [Slot("h%d" % k) for k in range(KC)]
    aT = [ar.alloc(ncols, BF16, "aT") for _ in range(KC)]
    aslots = [Slot("a%d" % k) for k in range(KC)]
    C.wring = WCache(S, [ar.alloc(KC * 512, BF16, "w") for _ in range(2)], "w")
    C.xring = Ring([ar.alloc(ncols, F32, "xc") for _ in range(3)], "xc")
    C.yring = Ring([ar.alloc(ncols, F32, "yc") for _ in range(3)], "yc")
    C.sqring = Ring([ar.alloc(ncols, F32, "sq") for _ in range(2)], "sq")
    qring = Ring([ar.alloc(ncols, BF16, "qT") for _ in range(2)], "qT")
    sgring = Ring([ar.alloc(ncols, BF16, "sg") for _ in range(2)], "sg")
    gtring = Ring([ar.alloc(ncols, F32, "gt") for _ in range(2)], "gt")
    NQ = 4
    kq_bufs = [ar.alloc(2048, BF16, "kq") for _ in range(NQ)]
    vq_bufs = [ar.alloc(2048, BF16, "vq") for _ in range(NQ)]
    kq_slots = [Slot("kq%d" % i) for i in range(NQ)]
    vq_slots = [Slot("vq%d" % i) for i in range(NQ)]
    e_buf = ar.alloc(ncols, F32, "e")
    e_slot = Slot("e")
    spring = Ring([ar.alloc(ncols, BF16, "sp") for _ in range(3)], "sp")
    pring = Ring([ar.alloc(ncols, BF16, "P") for _ in range(4)], "P")
    ssum = [ar.alloc(ncols, BF16, "ssum") for _ in range(2)]
    ssum_slots = [Slot("ssum0"), Slot("ssum1")]

    x2_slot, x3_slot, out_slot, y_slot = Slot("x2"), Slot("x3"), Slot("out"), Slot("yscr")
    qcount = [0]

    for G in range(2):
        g0 = G * ncols

        def src_of(t, g0=g0):
            return lambda k: t[k * 128:(k + 1) * 128, g0:g0 + ncols]
        for l in range(2):
            xsrc, sslot = (dr["x2T"], x2_slot) if l == 0 else (x3, x3_slot)
            xdst, dslot = (x3, x3_slot) if l == 0 else (outT, out_slot)
            if l == 0:
                first_norm(S, C, src_of(xsrc), ncols, cts, 0, hT, hslots, sslot)
            w_in = dr["b_w_in"][l]

            def head_proj_ops(hd):
                thunks = []
                qb, qs = qring.next()
                gb, gs = sgring.next()
                for which in range(2):
                    col = hd * 128 + which * D
                    pb, psl = C.psring.next()
                    wv, ws = C.wring.get(w_in, "b_w_in%d" % l, col, KC)
                    for k in range(KC):
                        def t_mm(k=k, wv=wv, ws=ws, pb=pb, psl=psl):
                            S.op("pe", lambda h: h.matmul(pb[0][:, :], lhsT=wv[:, k, :], rhs=hT[k],
                                                          start=(k == 0), stop=(k == KC - 1)),
                                 reads=[ws, hslots[k]], writes=[psl[0]])
                        thunks.append(t_mm)
                    if which == 0:
                        def t_ev(pb=pb, psl=psl, qb=qb, qs=qs):
                            S.op("dve", lambda h: h.scalar_tensor_tensor(out=qb, in0=pb[0][:, :],
                                                                         scalar=float(128 ** -0.5), in1=C.rstdh,
                                                                         op0=ALU.mult, op1=ALU.mult),
                                 reads=[psl[0], C.rstdh_slot], writes=[qs])
                    else:
                        def t_ev(pb=pb, psl=psl, gb=gb, gs=gs):
                            gt, gts = gtring.next()
                            S.op("dve", lambda h: h.tensor_tensor(out=gt, in0=pb[0][:, :], in1=C.rstdh, op=ALU.mult),
                                 reads=[psl[0], C.rstdh_slot], writes=[gts])
                            S.op("act", lambda h: h.activation(out=gb, in_=gt, func=AF.Silu),
                                 reads=[gts], writes=[gs])
                    thunks.append(t_ev)
                return thunks, qb, qs, gb, gs

            if G == 0:
                kb_list = list(range(31, -1, -1))
            else:
                kb_list = list(range(63, -1, -1))
            nsteps = len(kb_list)
            heads = []
            C.psring.mode = 1
            th0, qb0, qs0, gb0, gs0 = head_proj_ops(0)
            for t_ in th0:
                t_()
            heads.append((qb0, qs0, gb0, gs0))
            steps = []
            for hd in range(NH):
                for si, kb in enumerate(kb_list):
                    steps.append({"hd": hd, "si": si, "kb": kb})
            pending = []
            state = {}
            qlist = [(hd_, kb_ // 16) for hd_ in range(NH) for kb_ in kb_list if kb_ % 16 == 15]
            qslot_of = {}
            T = len(steps)

            def emit_qk(st):
                hd, si, kb = st["hd"], st["si"], st["kb"]
                if si == 0:
                    if hd + 1 < NH:
                        th, qb, qs, gb, gs = head_proj_ops(hd + 1)
                        pending.extend(th)
                        heads.append((qb, qs, gb, gs))
                if kb % 16 == 15:
                    qidx = state.get("qidx", -1) + 1
                    state["qidx"] = qidx
                    while state.get("emitted", 0) < min(qidx + 3, len(qlist)):
                        e_i = state.get("emitted", 0)
                        hd_, qq_ = qlist[e_i]
                        qi_ = qcount[0] % NQ
                        qcount[0] += 1
                        qslot_of[e_i] = qi_
                        S.op("sp", lambda h, hd_=hd_, qq_=qq_, qi_=qi_: h.dma_start(
                            out=kq_bufs[qi_], in_=dr["KT"][hd_, :, qq_ * 2048:(qq_ + 1) * 2048]),
                            writes=[kq_slots[qi_]], dma=kq_slots[qi_])
                        S.op("sp", lambda h, hd_=hd_, qq_=qq_, qi_=qi_: h.dma_start(
                            out=vq_bufs[qi_], in_=dr["Vh"][hd_, :, qq_ * 2048:(qq_ + 1) * 2048]),
                            writes=[vq_slots[qi_]], dma=vq_slots[qi_])
                        state["emitted"] = e_i + 1
                    state["qi"] = qslot_of[qidx]
                st["qi"] = state["qi"]
                qi = st["qi"]
                if G == 0 or kb >= 32:
                    r = (kb % 32) // 8
                    st["a0"] = 128 * r
                    st["mask"] = kb % 8
                else:
                    st["a0"] = 0
                    st["mask"] = None
                a0 = st["a0"]
                zb = st["gi"] % 4
                st["zb"] = zb
                qb, qs, gb, gs = heads[hd]
                ko = (kb % 16) * 128
                S.op("pe", lambda h: h.matmul(C.banks[zb][:, a0:ncols], lhsT=kq_bufs[qi][:, ko:ko + 128],
                                              rhs=qb[:, a0:ncols], start=True, stop=False, skip_group_check=True),
                     reads=[kq_slots[qi], qs], writes=[C.bank_slots[zb]])

            def emit_exp1(st):
                a0, zb = st["a0"], st["zb"]
                S.op("act", lambda h: h.activation(out=e_buf[:, a0:ncols], in_=C.banks[zb][:, a0:ncols], func=AF.Exp),
                     reads=[C.bank_slots[zb]], writes=[e_slot])

            def emit_ln(st):
                a0 = st["a0"]
                sb, ss = spring.next()
                st["sb"], st["ss"] = sb, ss
                S.op("act", lambda h: h.activation(out=sb[:, a0:ncols], in_=e_buf[:, a0:ncols], func=AF.Ln,
                                                   bias=1.0, scale=1.0), reads=[e_slot], writes=[ss])
                if st["mask"] is not None:
                    mk = st["mask"]
                    S.op("dve", lambda h: h.tensor_tensor(out=sb[:, a0:a0 + 128], in0=sb[:, a0:a0 + 128],
                                                          in1=masks[:, mk * 128:(mk + 1) * 128], op=ALU.mult),
                         reads=[ss, C.const_slot], writes=[ss])

            def emit_s(st):
                si = st["si"]
                a0, zb, sb, ss = st["a0"], st["zb"], st["sb"], st["ss"]
                first = (si == 0)
                S.op("pe", lambda h: h.matmul(C.banks[zb][:, a0:ncols], lhsT=negU, rhs=sb[:, a0:ncols],
                                              start=False, stop=first, skip_group_check=True),
                     reads=[ss, C.const_slot], writes=[C.bank_slots[zb]])
                cur = si % 2
                nxt = (si + 1) % 2
                if not first:
                    S.op("pe", lambda h: h.matmul(C.banks[zb][:, a0:ncols], lhsT=negO, rhs=ssum[cur][:, a0:ncols],
                                                  start=False, stop=True, skip_group_check=True),
                         reads=[ssum_slots[cur], C.const_slot], writes=[C.bank_slots[zb]])
                    if si + 1 < nsteps:
                        S.op("dve", lambda h: h.tensor_tensor(out=ssum[nxt][:, a0:ncols], in0=ssum[cur][:, a0:ncols],
                                                              in1=sb[:, a0:ncols], op=ALU.add),
                             reads=[ssum_slots[cur], ss], writes=[ssum_slots[nxt]], sync_same=True)
                else:
                    if si + 1 < nsteps:
                        S.op("dve", lambda h: h.tensor_copy(out=ssum[nxt][:, a0:ncols], in_=sb[:, a0:ncols]),
                             reads=[ss], writes=[ssum_slots[nxt]], sync_same=True)
                        if a0 > 0:
                            S.op("dve", lambda h: h.memset(ssum[nxt][:, 0:a0], 0.0), writes=[ssum_slots[nxt]],
                                 sync_same=True)
                            S.op("dve", lambda h: h.memset(ssum[cur][:, 0:a0], 0.0), writes=[ssum_slots[cur]],
                                 sync_same=True)

            def emit_expp(st):
                a0, zb = st["a0"], st["zb"]
                pb_, ps_ = pring.next()
                st["pb"], st["ps"] = pb_, ps_
                S.op("act", lambda h: h.activation(out=pb_[:, a0:ncols], in_=C.banks[zb][:, a0:ncols], func=AF.Exp),
                     reads=[C.bank_slots[zb]], writes=[ps_])
                if st["mask"] is not None:
                    mk = st["mask"]
                    S.op("dve", lambda h: h.tensor_tensor(out=pb_[:, a0:a0 + 128], in0=pb_[:, a0:a0 + 128],
                                                          in1=masks[:, mk * 128:(mk + 1) * 128], op=ALU.mult),
                         reads=[ps_, C.const_slot], writes=[ps_])

            def emit_av(st):
                hd, si, kb = st["hd"], st["si"], st["kb"]
                a0, qi = st["a0"], st["qi"]
                ob = 4 + (hd % 2)
                qb, qs, gb, gs = heads[hd]
                if si == 0:
                    S.op("pe", lambda h: h.matmul(C.banks[ob][:, :], lhsT=zer, rhs=qb, start=True, stop=False,
                                                  skip_group_check=True),
                         reads=[C.const_slot, qs], writes=[C.bank_slots[ob]])
                vo_ = (kb % 16) * 128
                pb_, ps_ = st["pb"], st["ps"]
                S.op("pe", lambda h: h.matmul(C.banks[ob][:, a0:ncols], lhsT=vq_bufs[qi][:, vo_:vo_ + 128],
                                              rhs=pb_[:, a0:ncols], start=False, stop=(si == nsteps - 1),
                                              skip_group_check=True),
                     reads=[vq_slots[qi], ps_], writes=[C.bank_slots[ob]])
                if si == nsteps - 1:
                    S.op("dve", lambda h: h.tensor_tensor(out=aT[hd], in0=C.banks[ob][:, :], in1=gb, op=ALU.mult),
                         reads=[C.bank_slots[ob], gs], writes=[aslots[hd]])

            for gi, st in enumerate(steps):
                st["gi"] = gi
            npend_per = 3 if G == 0 else 2
            C.psring.mode = 1
            emit_qk(steps[0])
            for t in range(T + 3):
                if t + 1 < T:
                    emit_qk(steps[t + 1])
                if t < T:
                    emit_exp1(steps[t])
                if 1 <= t <= T:
                    emit_s(steps[t - 1])
                if 2 <= t <= T + 1:
                    emit_expp(steps[t - 2])
                if t < T:
                    emit_ln(steps[t])
                if t >= 3:
                    emit_av(steps[t - 3])
                for _ in range(npend_per):
                    if pending:
                        pending.pop(0)()
            while pending:
                pending.pop(0)()
            C.psring.mode = 0
            wout_and_residual(S, C, dr["b_w_out"][l], "b_w_out%d" % l, aT, aslots, ncols, cts, (2 + l) * KC,
                              src_of(xsrc), sslot, src_of(xdst), dslot,
                              lambda m: yscr[m * 128:(m + 1) * 128, :], y_slot,
                              next_gcol=(KC if l == 0 else None), hT=hT, hslots=hslots)
    S.emit(nc, [out_slot])
    return nc


def _vec_cols(v):
    return np.ascontiguousarray(np.asarray(v, np.float32).reshape(KC, 128).T)


_CACHE = {}


def kernel(x, a_pre_norm, a_w_in, a_w_group, a_scale, a_w_out, a_post_norm,
           kv_norm, w_kv, b_pre_norm, b_w_in, b_w_out, b_post_norm):
    bf = ml_dtypes.bfloat16
    x = np.asarray(x, np.float32)[0]
    xT_full = np.ascontiguousarray(x.T)
    a_w_in = np.asarray(a_w_in, np.float32)
    a_w_group = np.asarray(a_w_group, np.float32)
    a_w_out = np.asarray(a_w_out, np.float32)
    w_kv = np.asarray(w_kv, np.float32)
    b_w_in = np.asarray(b_w_in, np.float32)
    b_w_out = np.asarray(b_w_out, np.float32)
    ones32 = np.ones((128, 128), np.float32)
    epsv = np.full((128, 1), EPS, np.float32)
    vecsA = np.concatenate([_vec_cols(a_pre_norm[0]), _vec_cols(a_pre_norm[1]), _vec_cols(a_scale[0]),
                            _vec_cols(a_scale[1]), _vec_cols(a_post_norm[0]), _vec_cols(a_post_norm[1]),
                            _vec_cols(kv_norm)], axis=1)
    in_maps = []
    for c in range(NCORE):
        xT = np.zeros((D, NSTR * SW), np.float32)
        for k in range(NSTR):
            b = 8 * k + c
            lo = 128 * b - HALO
            if lo >= 0:
                xT[:, k * SW:(k + 1) * SW] = xT_full[:, lo:lo + SW]
            else:
                xT[:, k * SW + HALO:(k + 1) * SW] = xT_full[:, 0:128]
        icnt = np.zeros((128, 64), np.float32)
        for g, w in enumerate(WINS):
            pos = 128 * c + np.arange(16)
            icnt[:, g * 16:(g + 1) * 16] = (1.0 / np.minimum(pos + 1, w)).astype(np.float32)[None, :]
        in_maps.append({"xT": xT, "a_w_in": a_w_in, "a_w_group": a_w_group, "a_w_out": a_w_out, "w_kv": w_kv,
                        "vecs": vecsA, "icnt": icnt, "ones32": ones32, "epsv": epsv})
    if "A" not in _CACHE:
        _CACHE["A"] = build_phase_a()
    resA = run_bass_kernel_spmd(_CACHE["A"], in_maps, core_ids=list(range(NCORE))).results
    KT = np.zeros((NH, 128, SEQ), bf)
    Vh = np.zeros((NH, 128, SEQ), bf)
    x2_list = []
    for c in range(NCORE):
        kt = np.asarray(resA[c]["KT"])
        v = np.asarray(resA[c]["V"])
        x2 = np.asarray(resA[c]["x2T"])
        own = np.concatenate([np.arange(k * SW + HALO, (k + 1) * SW) for k in range(NSTR)])
        x2_list.append(np.ascontiguousarray(x2[:, own]))
        for k in range(NSTR):
            b = 8 * k + c
            KT[:, :, 128 * b:128 * b + 128] = kt[:, :, 128 * k:128 * k + 128]
            vb = v[128 * k:128 * k + 128, :].reshape(128, NH, 128)
            Vh[:, :, 128 * b:128 * b + 128] = vb.transpose(1, 0, 2)
    vecsB = np.concatenate([_vec_cols(b_pre_norm[0]), _vec_cols(b_pre_norm[1]),
                            _vec_cols(b_post_norm[0]), _vec_cols(b_post_norm[1])], axis=1)
    jj = np.arange(128)
    tri = (jj[:, None] < jj[None, :]).astype(np.float32)
    negU = (-(jj[:, None] >= jj[None, :]).astype(np.float32)).astype(bf)
    negO = (-np.ones((128, 128), np.float32)).astype(bf)
    zer = np.zeros((128, 128), bf)
    in_maps = []
    for c in range(NCORE):
        m = np.zeros((128, 8, 128), np.float32)
        for r in range(8):
            if r < c:
                m[:, r, :] = 1.0
            elif r == c:
                m[:, r, :] = tri
        in_maps.append({"x2T": x2_list[c], "KT": KT, "Vh": Vh, "b_w_in": b_w_in, "b_w_out": b_w_out,
                        "vecs": vecsB, "masks": m.reshape(128, 1024).astype(bf), "negU": negU, "negO": negO,
                        "zer": zer, "ones32": ones32, "epsv": epsv})
    if "B" not in _CACHE:
        _CACHE["B"] = build_phase_b()
    resB = run_bass_kernel_spmd(_CACHE["B"], in_maps, core_ids=list(range(NCORE))).results
    out = np.zeros((SEQ, D), np.float32)
    for c in range(NCORE):
        oT = np.asarray(resB[c]["outT"])
        for k in range(NSTR):
            b = 8 * k + c
            out[128 * b:128 * b + 128, :] = oT[:, 128 * k:128 * k + 128].T
    return out[None]
```
